# Optimizing a Trainium2 kernel written in Bass

```python
import math
import jax, jax.numpy as jnp
from jax import lax
import numpy as np

D_MODEL = 2048
BATCH = 4
SEQ = 2048
DEPTH = 1
DEC_BATCH = 32
DEC_SEQ = 1
PAST_LEN = 8192
PAGE_SIZE = 128

D_MIX = D_MODEL
D_REC = D_MIX // 2
D_ATT = D_MIX - D_REC
N_REC_BLOCKS = 8
REC_BLOCK = D_REC // N_REC_BLOCKS
CONV_W = 4
LRU_C = 8.0
N_ATT_HEADS = 8
ATT_HEAD_DIM = D_ATT // N_ATT_HEADS
QK_DIM = ATT_HEAD_DIM // 2
ROPE_THETA = 10000.0
Q_BLOCK = 128
N_KEYS = 128
N_EXPERTS = N_KEYS * N_KEYS
PEER_HEADS = 8
PEER_KEY_DIM = 256
PEER_HALF = PEER_KEY_DIM // 2
PEER_TOPK = 16
TOKEN_BLOCK = 128
LN_EPS = 1e-5
NEG_INF = -1e30

kernel_name = 'hymba_rglru_diffattn_peer_step'


def layer_norm(x, g, b):
    x32 = x.astype(jnp.float32)
    mu = jnp.mean(x32, axis=-1, keepdims=True)
    xc = x32 - mu
    var = jnp.mean(xc * xc, axis=-1, keepdims=True)
    return (xc * lax.rsqrt(var + LN_EPS) * g.astype(jnp.float32) + b.astype(jnp.float32)).astype(x.dtype)


def rms_norm(x, g):
    x32 = x.astype(jnp.float32)
    return x32 * lax.rsqrt(jnp.mean(x32 * x32, axis=-1, keepdims=True) + LN_EPS) * g.astype(jnp.float32)


def rope(x, pos):
    dim = x.shape[-1]
    half = dim // 2
    inv = ROPE_THETA ** (-jnp.arange(half, dtype=jnp.float32) * 2.0 / dim)
    ang = pos.astype(jnp.float32)[:, None] * inv[None, :]
    cos = jnp.cos(ang)[:, None, :]
    sin = jnp.sin(ang)[:, None, :]
    x32 = x.astype(jnp.float32)
    x1, x2 = x32[..., :half], x32[..., half:]
    return jnp.concatenate([x1 * cos - x2 * sin, x2 * cos + x1 * sin], axis=-1).astype(x.dtype)


def causal_conv(x, buf, w, b):
    xp = jnp.concatenate([buf.astype(jnp.float32), x.astype(jnp.float32)], axis=1)
    t = x.shape[1]
    w32 = w.astype(jnp.float32)
    out = b.astype(jnp.float32) + sum(xp[:, j:j + t] * w32[j] for j in range(CONV_W))
    return out, xp[:, -(CONV_W - 1):]


def rg_lru(x, h0, w_a, b_a, w_x, b_x, lam):
    bsz, t, _ = x.shape
    xb = x.reshape(bsz, t, N_REC_BLOCKS, REC_BLOCK)
    r = jax.nn.sigmoid(jnp.einsum('btni,nij->btnj', xb, w_a.astype(jnp.float32)).reshape(bsz, t, D_REC) + b_a.astype(jnp.float32))
    i = jax.nn.sigmoid(jnp.einsum('btni,nij->btnj', xb, w_x.astype(jnp.float32)).reshape(bsz, t, D_REC) + b_x.astype(jnp.float32))
    log_a = -LRU_C * r * jax.nn.softplus(-lam.astype(jnp.float32))
    a = jnp.exp(log_a)
    bterm = jnp.sqrt(-jnp.expm1(2.0 * log_a)) * (i * x)
    bterm = bterm.at[:, 0].add(a[:, 0] * h0.astype(jnp.float32))

    def combine(c1, c2):
        a1, b1 = c1
        a2, b2 = c2
        return a1 * a2, a2 * b1 + b2

    _, h = lax.associative_scan(combine, (a, bterm), axis=1)
    return h, h[:, -1]


def diff_attn_block(q, k, v, q_pos, k_pos, lam):
    bsz, tq = q.shape[:2]
    tk = k.shape[1]
    qh = q.reshape(bsz, tq, N_ATT_HEADS, 2, QK_DIM)
    kh = k.reshape(bsz, tk, N_ATT_HEADS, 2, QK_DIM)
    s = jnp.einsum('bqhcd,bkhcd->bhcqk', qh, kh, preferred_element_type=jnp.float32) * (QK_DIM ** -0.5)
    mask = k_pos[None, :] <= q_pos[:, None]
    s = jnp.where(mask, s, NEG_INF)
    p = jax.nn.softmax(s, axis=-1)
    w = p[:, :, 0] - lam * p[:, :, 1]
    return jnp.einsum('bhqk,bkhd->bqhd', w, v.astype(jnp.float32))


def diff_attention(q, k, v, q_pos, k_pos, lam):
    bsz, t = q.shape[:2]
    if t <= Q_BLOCK or t % Q_BLOCK != 0:
        return diff_attn_block(q, k, v, q_pos, k_pos, lam)
    nb = t // Q_BLOCK
    qb = q.reshape(bsz, nb, Q_BLOCK, q.shape[2], q.shape[3]).swapaxes(0, 1)
    pb = q_pos.reshape(nb, Q_BLOCK)
    ob = lax.map(lambda a: diff_attn_block(a[0], k, v, a[1], k_pos, lam), (qb, pb))
    return ob.swapaxes(0, 1).reshape(bsz, t, N_ATT_HEADS, ATT_HEAD_DIM)


def peer_block(xb, wq, sub_k1, sub_k2, u_tab, v_tab):
    n = xb.shape[0]
    q = jnp.dot(xb, wq, preferred_element_type=jnp.float32).reshape(n, PEER_HEADS, 2, PEER_HALF)
    s1 = jnp.einsum('nhd,hkd->nhk', q[:, :, 0], sub_k1.astype(jnp.float32))
    s2 = jnp.einsum('nhd,hkd->nhk', q[:, :, 1], sub_k2.astype(jnp.float32))
    v1, i1 = lax.top_k(s1, PEER_TOPK)
    v2, i2 = lax.top_k(s2, PEER_TOPK)
    comb = (v1[..., :, None] + v2[..., None, :]).reshape(n, PEER_HEADS, PEER_TOPK * PEER_TOPK)
    sv, si = lax.top_k(comb, PEER_TOPK)
    e = (jnp.take_along_axis(i1, si // PEER_TOPK, axis=-1) * N_KEYS
         + jnp.take_along_axis(i2, si % PEER_TOPK, axis=-1))
    g = jax.nn.softmax(sv, axis=-1)
    ug = u_tab[e]
    act = jax.nn.gelu(jnp.einsum('nd,nhkd->nhk', xb, ug, preferred_element_type=jnp.float32))
    vg = v_tab[e]
    out = jnp.einsum('nhk,nhkd->nd', (g * act).astype(vg.dtype), vg, preferred_element_type=jnp.float32)
    return out.astype(xb.dtype)


def peer(x, wq, sub_k1, sub_k2, u_tab, v_tab):
    bsz, t, d = x.shape
    n = bsz * t
    n_pad = (-n) % TOKEN_BLOCK
    xt = jnp.pad(x.reshape(n, d), ((0, n_pad), (0, 0)))
    xblk = xt.reshape(-1, TOKEN_BLOCK, d)
    out = lax.map(lambda xb: peer_block(xb, wq, sub_k1, sub_k2, u_tab, v_tab), xblk)
    return out.reshape(-1, d)[:n].reshape(bsz, t, d)


def layer(x, q_pos, k_past, v_past, conv_buf, h0, lam_init, p):
    bsz, t, _ = x.shape
    alpha = (2.0 * DEPTH) ** 0.25
    proj = x @ p['w_in']
    rec_x, rec_g, q, k, v = jnp.split(proj, [D_REC, 2 * D_REC, 2 * D_REC + D_ATT, 2 * D_REC + 2 * D_ATT], axis=-1)
    conv_out, new_buf = causal_conv(rec_x, conv_buf, p['conv_w'], p['conv_b'])
    hs, h_last = rg_lru(conv_out, h0, p['lru_wa'], p['lru_ba'], p['lru_wx'], p['lru_bx'], p['lru_lambda'])
    rec_out = hs * jax.nn.gelu(rec_g.astype(jnp.float32))
    q = rope(q.reshape(bsz, t, 2 * N_ATT_HEADS, QK_DIM), q_pos)
    k = rope(k.reshape(bsz, t, 2 * N_ATT_HEADS, QK_DIM), q_pos)
    v = v.reshape(bsz, t, N_ATT_HEADS, ATT_HEAD_DIM)
    if k_past is None:
        k_all, v_all, k_pos = k, v, q_pos
    else:
        k_all = jnp.concatenate([k_past.astype(k.dtype), k], axis=1)
        v_all = jnp.concatenate([v_past.astype(v.dtype), v], axis=1)
        k_pos = jnp.arange(k_all.shape[1], dtype=jnp.int32)
    f32 = jnp.float32
    lam = (jnp.exp(jnp.sum(p['lambda_q1'].astype(f32) * p['lambda_k1'].astype(f32)))
           - jnp.exp(jnp.sum(p['lambda_q2'].astype(f32) * p['lambda_k2'].astype(f32))) + lam_init)
    att = diff_attention(q, k_all, v_all, q_pos, k_pos, lam)
    att = (rms_norm(att, p['subln_g']) * (1.0 - lam_init)).reshape(bsz, t, D_ATT)
    mix = jnp.concatenate([rec_out, att], axis=-1).astype(x.dtype) @ p['w_out']
    x1 = layer_norm(alpha * x + mix, p['ln1_g'], p['ln1_b'])
    ff = peer(x1, p['peer_wq'], p['peer_k1'], p['peer_k2'], p['peer_u'], p['peer_v'])
    x2 = layer_norm(alpha * x1 + ff, p['ln2_g'], p['ln2_b'])
    return x2, k, v, new_buf.astype(x.dtype), h_last.astype(x.dtype)


def setup_inputs(seed: int = 0) -> dict:
    key = jax.random.key(seed)
    ks = jax.random.split(key, 32)
    f32 = jnp.float32
    n_pages = PAST_LEN // PAGE_SIZE
    n_pool = (DEC_BATCH * n_pages * 5 + 3) // 4
    beta = (8.0 * DEPTH) ** -0.25

    def nrm(k, shape, s):
        return jax.random.normal(k, shape, f32) * s

    in_cols = 2 * D_REC + 3 * D_ATT
    col_scale = jnp.concatenate([jnp.ones((2 * D_REC + 2 * D_ATT,), f32), jnp.full((D_ATT,), beta, f32)])
    a0 = jax.random.uniform(ks[10], (DEPTH, D_REC), f32, minval=0.9, maxval=0.999)
    sa = a0 ** (1.0 / LRU_C)
    perm = jax.random.permutation(ks[31], n_pool)[:DEC_BATCH * n_pages]
    return {
        'x_prompt': nrm(ks[0], (BATCH, SEQ, D_MODEL), 1.0),
        'x_sample': nrm(ks[1], (DEC_BATCH, DEC_SEQ, D_MODEL), 1.0),
        'cache_k': nrm(ks[2], (DEPTH, n_pool, PAGE_SIZE, 2 * N_ATT_HEADS, QK_DIM), 1.0),
        'cache_v': nrm(ks[3], (DEPTH, n_pool, PAGE_SIZE, N_ATT_HEADS, ATT_HEAD_DIM), 1.0),
        'state_conv': nrm(ks[4], (DEPTH, DEC_BATCH, CONV_W - 1, D_REC), 1.0),
        'state_h': nrm(ks[5], (DEPTH, DEC_BATCH, D_REC), 0.5),
        'page_table': perm.reshape(DEC_BATCH, n_pages).astype(jnp.int32),
        'w_in': nrm(ks[6], (DEPTH, D_MODEL, in_cols), D_MODEL ** -0.5) * col_scale,
        'conv_w': nrm(ks[7], (DEPTH, CONV_W, D_REC), CONV_W ** -0.5),
        'conv_b': nrm(ks[8], (DEPTH, D_REC), 0.01),
        'lru_wa': nrm(ks[9], (DEPTH, N_REC_BLOCKS, REC_BLOCK, REC_BLOCK), REC_BLOCK ** -0.5),
        'lru_ba': nrm(ks[11], (DEPTH, D_REC), 0.01),
        'lru_wx': nrm(ks[12], (DEPTH, N_REC_BLOCKS, REC_BLOCK, REC_BLOCK), REC_BLOCK ** -0.5),
        'lru_bx': nrm(ks[13], (DEPTH, D_REC), 0.01),
        'lru_lambda': jnp.log(sa) - jnp.log1p(-sa),
        'lambda_q1': nrm(ks[14], (DEPTH, QK_DIM), 0.1),
        'lambda_k1': nrm(ks[15], (DEPTH, QK_DIM), 0.1),
        'lambda_q2': nrm(ks[16], (DEPTH, QK_DIM), 0.1),
        'lambda_k2': nrm(ks[17], (DEPTH, QK_DIM), 0.1),
        'subln_g': 1.0 + nrm(ks[18], (DEPTH, ATT_HEAD_DIM), 0.02),
        'w_out': nrm(ks[19], (DEPTH, D_MIX, D_MODEL), beta * D_MIX ** -0.5),
        'ln1_g': 1.0 + nrm(ks[20], (DEPTH, D_MODEL), 0.02),
        'ln1_b': nrm(ks[21], (DEPTH, D_MODEL), 0.01),
        'peer_wq': nrm(ks[22], (DEPTH, D_MODEL, PEER_HEADS * PEER_KEY_DIM), D_MODEL ** -0.5),
        'peer_k1': nrm(ks[23], (DEPTH, PEER_HEADS, N_KEYS, PEER_HALF), PEER_HALF ** -0.5),
        'peer_k2': nrm(ks[24], (DEPTH, PEER_HEADS, N_KEYS, PEER_HALF), PEER_HALF ** -0.5),
        'peer_u': nrm(ks[25], (DEPTH, N_EXPERTS, D_MODEL), D_MODEL ** -0.5),
        'peer_v': nrm(ks[26], (DEPTH, N_EXPERTS, D_MODEL), beta * PEER_HEADS ** -0.5),
        'ln2_g': 1.0 + nrm(ks[27], (DEPTH, D_MODEL), 0.02),
        'ln2_b': nrm(ks[28], (DEPTH, D_MODEL), 0.01),
    }


def reference(x_prompt, x_sample, cache_k, cache_v, state_conv, state_h, page_table,
              w_in, conv_w, conv_b, lru_wa, lru_ba, lru_wx, lru_bx, lru_lambda,
              lambda_q1, lambda_k1, lambda_q2, lambda_k2, subln_g, w_out, ln1_g, ln1_b,
              peer_wq, peer_k1, peer_k2, peer_u, peer_v, ln2_g, ln2_b):
    bsz, seq, _ = x_prompt.shape
    dbsz, dseq, _ = x_sample.shape
    n_pages = page_table.shape[1]
    page = cache_k.shape[2]
    past = n_pages * page
    pos_p = jnp.arange(seq, dtype=jnp.int32)
    pos_s = past + jnp.arange(dseq, dtype=jnp.int32)
    yp, ys = x_prompt, x_sample
    kp_l, vp_l, cp_l, hp_l, ks_l, vs_l, cs_l, hs_l = [], [], [], [], [], [], [], []
    for l in range(DEPTH):
        p = dict(w_in=w_in[l], conv_w=conv_w[l], conv_b=conv_b[l], lru_wa=lru_wa[l], lru_ba=lru_ba[l],
                 lru_wx=lru_wx[l], lru_bx=lru_bx[l], lru_lambda=lru_lambda[l],
                 lambda_q1=lambda_q1[l], lambda_k1=lambda_k1[l], lambda_q2=lambda_q2[l], lambda_k2=lambda_k2[l],
                 subln_g=subln_g[l], w_out=w_out[l], ln1_g=ln1_g[l], ln1_b=ln1_b[l],
                 peer_wq=peer_wq[l], peer_k1=peer_k1[l], peer_k2=peer_k2[l], peer_u=peer_u[l], peer_v=peer_v[l],
                 ln2_g=ln2_g[l], ln2_b=ln2_b[l])
        lam_init = 0.8 - 0.6 * math.exp(-0.3 * l)
        conv0 = jnp.zeros((bsz, CONV_W - 1, D_REC), x_prompt.dtype)
        h0 = jnp.zeros((bsz, D_REC), x_prompt.dtype)
        yp, kp, vp, cp, hp = layer(yp, pos_p, None, None, conv0, h0, lam_init, p)
        k_past = cache_k[l][page_table].reshape(dbsz, past, 2 * N_ATT_HEADS, QK_DIM)
        v_past = cache_v[l][page_table].reshape(dbsz, past, N_ATT_HEADS, ATT_HEAD_DIM)
        ys, ksn, vsn, csn, hsn = layer(ys, pos_s, k_past, v_past, state_conv[l], state_h[l], lam_init, p)
        kp_l.append(kp); vp_l.append(vp); cp_l.append(cp); hp_l.append(hp)
        ks_l.append(ksn); vs_l.append(vsn); cs_l.append(csn); hs_l.append(hsn)
    k_prompt = jnp.stack(kp_l)
    v_prompt = jnp.stack(vp_l)
    conv_prompt = jnp.stack(cp_l)
    h_prompt = jnp.stack(hp_l)
    k_sample = jnp.stack(ks_l)
    v_sample = jnp.stack(vs_l)
    conv_sample = jnp.stack(cs_l)
    h_sample = jnp.stack(hs_l)
    return (yp, ys, k_prompt, v_prompt, conv_prompt, h_prompt, k_sample, v_sample, conv_sample, h_sample)
```

```python
import contextlib
import numpy as np
import concourse.bass as bass
import concourse.mybir as mybir
from concourse.bass_utils import run_bass_kernel_spmd

F32 = mybir.dt.float32
BF16 = mybir.dt.bfloat16
I32 = mybir.dt.int32
U32 = mybir.dt.uint32
AF = mybir.ActivationFunctionType
ALU = mybir.AluOpType
AX = mybir.AxisListType

NCORES = 8
D = 2048
TOWN = 1024
NSLOT = 2048
NS = 4
TTOK = TOWN + NS
NPOOL = 2560
NPAGES = 64
EPS = 1e-5
NEG = -30000.0
ALPHA = 2.0 ** 0.25
LAM_INIT = 0.2

ENGS = ("sync", "scalar", "vector", "gpsimd", "tensor")


class Buf:
    __slots__ = ("name", "w", "r")

    def __init__(self, name):
        self.name = name
        self.w = None
        self.r = []


class Prog:
    def __init__(self, nc, n_dma_sems=12):
        self.nc = nc
        self.lists = {e: [] for e in ENGS}
        self.esem = {e: nc.alloc_semaphore(name="es_" + e) for e in ENGS}
        self.ecnt = {e: 0 for e in ENGS}
        self.dsem, self.dcnt, self.dpos = {}, {}, {}
        for q in ("sync", "scalar", "gpsimd"):
            self.dsem[q] = [nc.alloc_semaphore(name=f"ds_{q}{i}") for i in range(n_dma_sems)]
            self.dcnt[q] = [0] * n_dma_sems
            self.dpos[q] = 0
        self.waited = {e: {} for e in ENGS}
        self.final_events = []

    def _need(self, eng, ev):
        if ev is None:
            return
        sem, val, src = ev
        if src == eng and eng == "tensor":
            return
        key = id(sem)
        if self.waited[eng].get(key, 0) >= val:
            return
        self.waited[eng][key] = val
        self.lists[eng].append(("wait", sem, val))

    def _deps(self, eng, reads, writes):
        for b in reads:
            self._need(eng, b.w)
        for b in writes:
            self._need(eng, b.w)
            for ev in b.r:
                self._need(eng, ev)

    def _mark(self, ev, reads, writes):
        for b in reads:
            b.r.append(ev)
            if len(b.r) > 48:
                last = {}
                for e2 in b.r:
                    k = id(e2[0])
                    if k not in last or last[k][1] < e2[1]:
                        last[k] = e2
                b.r = list(last.values())
        for b in writes:
            b.w = ev
            b.r = []

    def op(self, eng, fn, reads=(), writes=()):
        self._deps(eng, reads, writes)
        self.ecnt[eng] += 1
        ev = (self.esem[eng], self.ecnt[eng], eng)
        self.lists[eng].append(("op", fn, self.esem[eng], 1))
        self._mark(ev, reads, writes)
        return ev

    def dma(self, q, fn, reads=(), writes=(), final=False):
        i = self.dpos[q]
        self.dpos[q] = (i + 1) % len(self.dsem[q])
        sem = self.dsem[q][i]
        if self.dcnt[q][i] > 0:
            self._need(q, (sem, self.dcnt[q][i], None))
        self._deps(q, reads, writes)
        self.dcnt[q][i] += 16
        ev = (sem, self.dcnt[q][i], None)
        self.lists[q].append(("op", fn, sem, 16))
        self._mark(ev, reads, writes)
        if final:
            self.final_events.append(ev)
        return ev

    def all_events(self):
        evs = [(self.esem[e], self.ecnt[e], e) for e in ENGS if self.ecnt[e] > 0]
        for q in self.dsem:
            for i, s in enumerate(self.dsem[q]):
                if self.dcnt[q][i] > 0:
                    evs.append((s, self.dcnt[q][i], None))
        return evs

    def barrier(self):
        evs = self.all_events()
        for e in ENGS:
            for ev in evs:
                if ev[2] == e:
                    continue
                self._need(e, ev)

    def flush(self, last=False):
        if last:
            for ev in self.all_events():
                self._need("sync", ev)
        lists = self.lists
        self.lists = {e: [] for e in ENGS}
        nc = self.nc

        def run(engobj, items):
            for it in items:
                if it[0] == "wait":
                    engobj.wait_ge(it[1], it[2])
                else:
                    it[1](engobj).then_inc(it[2], it[3])

        with nc.Block() as block:
            @block.sync
            def _(e):
                run(e, lists["sync"])

            @block.scalar
            def _(e):
                run(e, lists["scalar"])

            @block.vector
            def _(e):
                run(e, lists["vector"])

            @block.gpsimd
            def _(e):
                run(e, lists["gpsimd"])

            @block.tensor
            def _(e):
                run(e, lists["tensor"])


def build_program(do_rec=True, do_att=True, do_dec=True, do_p2=True, do_peer=True, dbg=False, npool=NPOOL, nexp=16384):
    nc = bass.Bass("TRN2", target_bir_lowering=False)

    def din(name, shape, dt=F32):
        return nc.dram_tensor(name, list(shape), dt, kind="ExternalInput").ap()

    def dout(name, shape, dt=F32):
        return nc.dram_tensor(name, list(shape), dt, kind="ExternalOutput").ap()

    xT_d = din("xT", [D, NSLOT])
    xsT_d = din("xsT", [D, NS])
    xtok_d = din("xtok", [TTOK, D])
    w_in_d = din("w_in", [D, 5120])
    w_out_d = din("w_out", [D, D])
    wq_d = din("wq", [D, D])
    chan_d = din("chan", [128, 8, 8])
    lwa_d = din("lwa", [128, 8, 128])
    lwx_d = din("lwx", [128, 8, 128])
    lamv_d = din("lamv", [128, 4, 64])
    subg_d = din("subg", [128, 128])
    rcos_d = din("rcos", [128, 16, 32])
    rsin_d = din("rsin", [128, 16, 2, 32])
    rsam_d = din("rsam", [NS, 96])
    pb_d = din("pb", [128, 2])
    pbrow_d = din("pbrow", [1, NSLOT])
    ident_d = din("ident", [128, 128])
    cmask_d = din("cmask", [128, 128])
    sel4_d = din("sel4", [NS, NS * 65])
    ones_d = din("ones65", [65, 65])
    iota_d = din("iota16", [128, 256])
    ck_d = din("cache_k", [8, npool, 16384])
    cv_d = din("cache_v", [8, npool, 16384])
    pt_d = din("ptT", [NPAGES, NS], I32)
    sconv_d = din("sconv", [128, 8, 3, NS])
    sh_d = din("sh", [128, 8, NS])
    ln_d = din("ln", [128, 4, D])
    k1T_d = din("k1T", [128, 8, 128])
    k2T_d = din("k2T", [128, 8, 128])
    pu_d = din("peer_u", [nexp, D])
    pv_d = din("peer_v", [nexp, D])

    y_o = dout("y", [TTOK, D])
    k_o = dout("k_o", [TOWN, 1024])
    v_o = dout("v_o", [TOWN, 1024])
    conv_o = dout("conv_o", [128, 8, 3])
    h_o = dout("h_o", [128, 8])
    ks_o = dout("ks_o", [NS, 1024])
    vs_o = dout("vs_o", [NS, 1024])
    convs_o = dout("convs_o", [128, 8, 3, NS])
    hs_o = dout("hs_o", [128, 8, NS])
    if dbg:
        mix_o = dout("mix_o", [128, 16, TTOK])
        x1_o = dout("x1_o", [TTOK, D])

    P = Prog(nc)

    def tt(eng, out, in0, in1, op, r, w):
        P.op(eng, lambda e: e.tensor_tensor(out=out, in0=in0, in1=in1, op=op), r, w)

    def ts(eng, out, in0, s1, s2, op0, op1, r, w):
        if op1 is None:
            P.op(eng, lambda e: e.tensor_scalar(out=out, in0=in0, scalar1=s1, scalar2=None, op0=op0), r, w)
        else:
            P.op(eng, lambda e: e.tensor_scalar(out=out, in0=in0, scalar1=s1, scalar2=s2, op0=op0, op1=op1), r, w)

    def stt(out, in0, scalar, in1, op0, op1, r, w, accum=None):
        if accum is None:
            P.op("vector", lambda e: e.scalar_tensor_tensor(out=out, in0=in0, scalar=scalar, in1=in1, op0=op0, op1=op1), r, w)
        else:
            P.op("vector", lambda e: e.scalar_tensor_tensor(out=out, in0=in0, scalar=scalar, in1=in1, op0=op0, op1=op1,
                                                            accum_out=accum), r, w)

    def act(out, in_, func, r, w, bias=None, scale=None, accum=None):
        kw = {}
        if bias is not None:
            kw["bias"] = bias
        if scale is not None:
            kw["scale"] = scale
        if accum is not None:
            kw["accum_out"] = accum
        P.op("scalar", lambda e: e.activation(out=out, in_=in_, func=func, **kw), r, w)

    def cp(eng, out, in_, r, w):
        if eng == "scalar":
            P.op("scalar", lambda e: e.copy(out=out, in_=in_), r, w)
        else:
            P.op(eng, lambda e: e.tensor_copy(out=out, in_=in_), r, w)

    def mm(out, lhsT, rhs, start, stop, r, w):
        P.op("tensor", lambda e: e.matmul(out, lhsT=lhsT, rhs=rhs, start=start, stop=stop), r, w)

    def tr(out, in_, ident, r, w):
        P.op("tensor", lambda e: e.transpose(out=out, in_=in_, identity=ident), r, w)

    def dma(q, out, in_, r, w, final=False):
        P.dma(q, lambda e: e.dma_start(out=out, in_=in_), r, w, final=final)

    def idma(out, in_, idx, r, w, eoff=0):
        P.dma("gpsimd", lambda e: e.indirect_dma_start(out=out, out_offset=None, in_=in_,
                                                      in_offset=bass.IndirectOffsetOnAxis(ap=idx, axis=0), element_offset=eoff), r, w)

    def memset(eng, ap, val, w):
        P.op(eng, lambda e: e.memset(ap, val), (), w)

    def red(eng, out, in_, op, r, w):
        P.op(eng, lambda e: e.tensor_reduce(out=out, in_=in_, axis=AX.X, op=op), r, w)

    def recip(out, in_, r, w):
        P.op("vector", lambda e: e.reciprocal(out=out, in_=in_), r, w)

    es_all = contextlib.ExitStack()
    with es_all:
        def T(es, name, shape, dt):
            return es.enter_context(nc.sbuf_tensor("s_" + name, list(shape), dt))

        def PSt(es, name, shape, dt):
            return es.enter_context(nc.psum_tensor("p_" + name, list(shape), dt))

        psS = PSt(es_all, "psS", [128, 2048], F32); bS = Buf("psS")
        psT = PSt(es_all, "psT", [128, 1024], BF16); bT = Buf("psT")
        psA = PSt(es_all, "psA", [128, 512], F32); bA = Buf("psA")
        psB = PSt(es_all, "psB", [128, 512], F32); bB = Buf("psB")
        psC = PSt(es_all, "psC", [128, 512], F32); bC = Buf("psC")
        psrot = [(psA, bA), (psB, bB), (psC, bC)]

        mix_dt = nc.dram_tensor("mix_d", [16, 128, TTOK], BF16)
        mix_d = mix_dt.ap()
        x1_dt = nc.dram_tensor("x1_d", [TTOK, D], F32)
        x1_d = x1_dt.ap()
        bmix = [Buf(f"mix{i}") for i in range(16)]
        pub_d = nc.dram_tensor("pub_d", [nexp, D], BF16).ap()
        pvb_d = nc.dram_tensor("pvb_d", [nexp, D], BF16).ap()
        mstg = T(es_all, "mstg", [128, TTOK], BF16); bmstg = Buf("mstg")
        identf = T(es_all, "identf", [128, 128], F32); bidf = Buf("identf")
        identb = T(es_all, "identb", [128, 128], BF16); bidb = Buf("identb")
        pbt = T(es_all, "pbt", [128, 2], F32); bpb = Buf("pb")

        dma("sync", identf[:], ident_d, (), [bidf])
        dma("gpsimd", identb[:], ident_d, (), [bidb])
        dma("sync", pbt[:], pb_d, (), [bpb])

        es1 = contextlib.ExitStack()
        with es1:
            xT = T(es1, "xTb", [128, 16, NSLOT], BF16); bxT = [Buf(f"xT{k}") for k in range(16)]
            xsT = T(es1, "xsTb", [128, 16, NS], BF16); bxs = Buf("xsT")
            chan = T(es1, "chan", [128, 8, 8], F32); bchan = Buf("chan")
            nsp = T(es1, "nsp", [128, 8, 2], F32); bnsp = Buf("nsp")
            cvf = [T(es1, f"cvf{i}", [128, 1024], F32) for i in range(2)]; bcvf = [Buf("cvf0"), Buf("cvf1")]
            cvh = [T(es1, f"cvh{i}", [128, 1024], BF16) for i in range(2)]; bcvh = [Buf("cvh0"), Buf("cvh1")]
            conv_list = [(tab, rt, hc) for tab in range(2) for rt in range(nexp // 128) for hc in range(2)] if (do_p2 and do_peer) else []
            conv_pos = [0]

            def conv_steps(n):
                for _ in range(n):
                    if conv_pos[0] >= len(conv_list):
                        return
                    tab, rt, hc = conv_list[conv_pos[0]]
                    bi = conv_pos[0] % 2
                    conv_pos[0] += 1
                    src = (pu_d if tab == 0 else pv_d)[rt * 128:(rt + 1) * 128, hc * 1024:(hc + 1) * 1024]
                    dst = (pub_d if tab == 0 else pvb_d)[rt * 128:(rt + 1) * 128, hc * 1024:(hc + 1) * 1024]
                    dma("scalar", cvf[bi][:], src, (), [bcvf[bi]])
                    cp("scalar", cvh[bi][:], cvf[bi][:], [bcvf[bi]], [bcvh[bi]])
                    dma("scalar", dst, cvh[bi][:], [bcvh[bi]], ())

            for k in range(16):
                dma("gpsimd", xT[:, k, :], xT_d[k * 128:(k + 1) * 128, :], (), [bxT[k]])
            dma("gpsimd", xsT[:], xsT_d.rearrange("(k p) s -> p k s", p=128), (), [bxs])
            dma("sync", chan[:], chan_d, (), [bchan])
            tmpl = T(es1, "tmpl", [128, 8], F32); btl = Buf("tmpl")
            act(tmpl[:], chan[:, :, 7], AF.Exp, [bchan], [btl], scale=-1.0)
            act(tmpl[:], tmpl[:], AF.Ln, [btl], [btl], bias=1.0)
            ts("vector", nsp[:, :, 0], tmpl[:], -8.0, None, ALU.mult, None, [btl], [bnsp])
            ts("vector", nsp[:, :, 1], tmpl[:], -16.0, None, ALU.mult, None, [btl], [bnsp])

            if do_rec:
                esr = contextlib.ExitStack()
                with esr:
                    wrx = T(esr, "wrx", [128, 16, 128], BF16); bwrx = Buf("wrx")
                    wrg = T(esr, "wrg", [128, 16, 128], BF16); bwrg = Buf("wrg")
                    wab = T(esr, "wab", [128, 128], BF16); bwab = Buf("wab")
                    wxb = T(esr, "wxb", [128, 128], BF16); bwxb = Buf("wxb")
                    rx = T(esr, "rx", [128, 3 + NSLOT], F32); brx = Buf("rx")
                    gg = T(esr, "gg", [128, TOWN], F32); bgg = Buf("gg")
                    cvt = T(esr, "cvt", [128, NSLOT], F32); bcv = Buf("cv")
                    cvb = T(esr, "cvb", [128, NSLOT], BF16); bcvb = Buf("cvb")
                    rgt = T(esr, "rgt", [128, NSLOT], F32); brgt = Buf("rgt")
                    igt = T(esr, "igt", [128, NSLOT], F32); bigt = Buf("igt")
                    at = T(esr, "at", [128, NSLOT], F32); bat = Buf("at")
                    a2t = T(esr, "a2t", [128, NSLOT], F32); ba2 = Buf("a2t")
                    hpre = T(esr, "hpre", [128, TOWN], F32); bhp = Buf("hpre")
                    hown = T(esr, "hown", [128, TOWN], F32); bho = Buf("hown")
                    h0 = T(esr, "h0", [128, 1], F32); bh0 = Buf("h0")
                    cst = T(esr, "cst", [128, 8, 3], F32); bcst = Buf("cst")
                    hst = T(esr, "hst", [128, 8], F32); bhst = Buf("hst")
                    sconv = T(esr, "sconv", [128, 8, 3, NS], F32); bsc = Buf("sconv")
                    sht = T(esr, "sht", [128, 8, NS], F32); bsh = Buf("sht")
                    cvsst = T(esr, "cvsst", [128, 8, 3, NS], F32); bcvs = Buf("cvsst")
                    hsst = T(esr, "hsst", [128, 8, NS], F32); bhss = Buf("hsst")
                    sm = T(esr, "sm", [128, 12, NS], F32); bsm = Buf("sm")
                    smb = T(esr, "smb", [128, NS], BF16); bsmb = Buf("smb")
                    dma("sync", sconv[:], sconv_d, (), [bsc])
                    dma("sync", sht[:], sh_d, (), [bsh])
                    memset("vector", rx[:, 0:3], 0.0, [brx])
                    for r in range(8):
                        dma("gpsimd", wrx[:], w_in_d[:, r * 128:(r + 1) * 128].rearrange("(k p) c -> p k c", p=128), (), [bwrx])
                        dma("gpsimd", wrg[:], w_in_d[:, 1024 + r * 128:1024 + (r + 1) * 128].rearrange("(k p) c -> p k c", p=128), (), [bwrg])
                        dma("gpsimd", wab[:], lwa_d[:, r, :], (), [bwab])
                        dma("gpsimd", wxb[:], lwx_d[:, r, :], (), [bwxb])
                        for blk in range(4):
                            ps, bp = psrot[blk % 3]
                            for k in range(16):
                                mm(ps[:, 0:512], wrx[:, k, :], xT[:, k, blk * 512:(blk + 1) * 512], k == 0, k == 15, [bwrx, bxT[k]], [bp])
                            cp("scalar", rx[:, 3 + blk * 512:3 + (blk + 1) * 512], ps[:, 0:512], [bp], [brx])
                            conv_steps(4)
                        for blk in range(2, 4):
                            ps, bp = psrot[(blk + 1) % 3]
                            for k in range(16):
                                mm(ps[:, 0:512], wrg[:, k, :], xT[:, k, blk * 512:(blk + 1) * 512], k == 0, k == 15, [bwrg, bxT[k]], [bp])
                            act(gg[:, (blk - 2) * 512:(blk - 1) * 512], ps[:, 0:512], AF.Gelu, [bp], [bgg])
                        ts("vector", cvt[:], rx[:, 0:NSLOT], chan[:, r, 0:1], chan[:, r, 4:5], ALU.mult, ALU.add, [brx, bchan], [bcv])
                        for j in range(1, 4):
                            stt(cvt[:], rx[:, j:j + NSLOT], chan[:, r, j:j + 1], cvt[:], ALU.mult, ALU.add, [brx, bchan, bcv], [bcv])
                        cp("gpsimd", cvb[:], cvt[:], [bcv], [bcvb])
                        for blk in range(4):
                            ps, bp = psrot[blk % 3]
                            mm(ps[:, 0:512], wab[:], cvb[:, blk * 512:(blk + 1) * 512], True, True, [bwab, bcvb], [bp])
                            act(rgt[:, blk * 512:(blk + 1) * 512], ps[:, 0:512], AF.Sigmoid, [bp, bchan], [brgt], bias=chan[:, r, 5:6])
                            ps2, bp2 = psrot[(blk + 1) % 3]
                            mm(ps2[:, 0:512], wxb[:], cvb[:, blk * 512:(blk + 1) * 512], True, True, [bwxb, bcvb], [bp2])
                            act(igt[:, blk * 512:(blk + 1) * 512], ps2[:, 0:512], AF.Sigmoid, [bp2, bchan], [bigt], bias=chan[:, r, 6:7])
                        act(at[:], rgt[:], AF.Exp, [brgt, bnsp], [bat], scale=nsp[:, r, 0:1])
                        act(a2t[:], rgt[:], AF.Exp, [brgt, bnsp], [ba2], scale=nsp[:, r, 1:2])
                        ts("vector", a2t[:], a2t[:], -1.0, 1.0, ALU.mult, ALU.add, [ba2], [ba2])
                        ts("gpsimd", a2t[:], a2t[:], 1e-30, None, ALU.max, None, [ba2], [ba2])
                        act(a2t[:], a2t[:], AF.Sqrt, [ba2], [ba2])
                        tt("gpsimd", igt[:], igt[:], cvt[:], ALU.mult, [bigt, bcv], [bigt])
                        tt("vector", a2t[:], a2t[:], igt[:], ALU.mult, [ba2, bigt], [ba2])
                        P.op("vector", lambda e: e.tensor_tensor_scan(out=hpre[:], data0=at[:, 0:TOWN], data1=a2t[:, 0:TOWN], initial=0.0,
                                                                    op0=ALU.mult, op1=ALU.add), [bat, ba2], [bhp])
                        ts("vector", h0[:], hpre[:, TOWN - 1:TOWN], pbt[:, 1:2], None, ALU.mult, None, [bhp, bpb], [bh0])
                        P.op("vector", lambda e: e.tensor_tensor_scan(out=hown[:], data0=at[:, TOWN:NSLOT], data1=a2t[:, TOWN:NSLOT], initial=h0[:, 0:1],
                                                                    op0=ALU.mult, op1=ALU.add), [bat, ba2, bh0], [bho])
                        tt("gpsimd", mstg[:, 0:TOWN], hown[:], gg[:], ALU.mult, [bho, bgg], [bmstg])
                        cp("gpsimd", cst[:, r, :], rx[:, NSLOT:NSLOT + 3], [brx], [bcst])
                        cp("gpsimd", hst[:, r:r + 1], hown[:, TOWN - 1:TOWN], [bho], [bhst])
                        ps, bp = psrot[0]
                        for k in range(16):
                            mm(ps[:, 0:NS], wrx[:, k, :], xsT[:, k, :], k == 0, k == 15, [bwrx, bxs], [bp])
                        cp("vector", sm[:, 0, :], ps[:, 0:NS], [bp], [bsm])
                        ps, bp = psrot[1]
                        for k in range(16):
                            mm(ps[:, 0:NS], wrg[:, k, :], xsT[:, k, :], k == 0, k == 15, [bwrg, bxs], [bp])
                        act(sm[:, 1, :], ps[:, 0:NS], AF.Gelu, [bp], [bsm])
                        ts("vector", sm[:, 2, :], sconv[:, r, 0, :], chan[:, r, 0:1], chan[:, r, 4:5], ALU.mult, ALU.add, [bsc, bchan, bsm], [bsm])
                        stt(sm[:, 2, :], sconv[:, r, 1, :], chan[:, r, 1:2], sm[:, 2, :], ALU.mult, ALU.add, [bsc, bchan, bsm], [bsm])
                        stt(sm[:, 2, :], sconv[:, r, 2, :], chan[:, r, 2:3], sm[:, 2, :], ALU.mult, ALU.add, [bsc, bchan, bsm], [bsm])
                        stt(sm[:, 2, :], sm[:, 0, :], chan[:, r, 3:4], sm[:, 2, :], ALU.mult, ALU.add, [bchan, bsm], [bsm])
                        cp("vector", smb[:], sm[:, 2, :], [bsm], [bsmb])
                        ps, bp = psrot[2]
                        mm(ps[:, 0:NS], wab[:], smb[:], True, True, [bwab, bsmb], [bp])
                        act(sm[:, 3, :], ps[:, 0:NS], AF.Sigmoid, [bp, bchan], [bsm], bias=chan[:, r, 5:6])
                        ps, bp = psrot[0]
                        mm(ps[:, 0:NS], wxb[:], smb[:], True, True, [bwxb, bsmb], [bp])
                        act(sm[:, 4, :], ps[:, 0:NS], AF.Sigmoid, [bp, bchan], [bsm], bias=chan[:, r, 6:7])
                        act(sm[:, 5, :], sm[:, 3, :], AF.Exp, [bsm, bnsp], [bsm], scale=nsp[:, r, 0:1])
                        act(sm[:, 6, :], sm[:, 3, :], AF.Exp, [bsm, bnsp], [bsm], scale=nsp[:, r, 1:2])
                        ts("vector", sm[:, 6, :], sm[:, 6, :], -1.0, 1.0, ALU.mult, ALU.add, [bsm], [bsm])
                        ts("vector", sm[:, 6, :], sm[:, 6, :], 1e-30, None, ALU.max, None, [bsm], [bsm])
                        act(sm[:, 6, :], sm[:, 6, :], AF.Sqrt, [bsm], [bsm])
                        tt("vector", sm[:, 6, :], sm[:, 6, :], sm[:, 4, :], ALU.mult, [bsm], [bsm])
                        tt("vector", sm[:, 6, :], sm[:, 6, :], sm[:, 2, :], ALU.mult, [bsm], [bsm])
                        tt("vector", sm[:, 7, :], sm[:, 5, :], sht[:, r, :], ALU.mult, [bsm, bsh], [bsm])
                        tt("vector", sm[:, 7, :], sm[:, 7, :], sm[:, 6, :], ALU.add, [bsm], [bsm])
                        tt("vector", mstg[:, TOWN:TTOK], sm[:, 7, :], sm[:, 1, :], ALU.mult, [bsm], [bmstg])
                        dma("sync", mix_d[r], mstg[:], [bmstg], [bmix[r]])
                        cp("vector", hsst[:, r, :], sm[:, 7, :], [bsm], [bhss])
                        cp("vector", cvsst[:, r, 0, :], sconv[:, r, 1, :], [bsc], [bcvs])
                        cp("vector", cvsst[:, r, 1, :], sconv[:, r, 2, :], [bsc], [bcvs])
                        cp("vector", cvsst[:, r, 2, :], sm[:, 0, :], [bsm], [bcvs])
                    dma("sync", conv_o, cst[:], [bcst], (), final=True)
                    dma("sync", h_o, hst[:], [bhst], (), final=True)
                    dma("sync", convs_o, cvsst[:], [bcvs], (), final=True)
                    dma("sync", hs_o, hsst[:], [bhss], (), final=True)
                    P.barrier()
                    P.flush()

            if do_att:
                esa = contextlib.ExitStack()
                with esa:
                    wq3 = T(esa, "wq3", [128, 16, 384], BF16); bw3 = Buf("wq3")
                    rcos = T(esa, "rcos", [128, 16, 32], F32); brc = Buf("rcos")
                    rsin = T(esa, "rsin", [128, 16, 2, 32], F32); brs = Buf("rsin")
                    rsam = T(esa, "rsam", [NS, 96], F32); brsm = Buf("rsam")
                    cmask = T(esa, "cmask", [128, 128], F32); bcm = Buf("cmask")
                    lamv = T(esa, "lamv", [128, 4, 64], F32); blv = Buf("lamv")
                    lam = T(esa, "lam", [128, 4], F32); blam = Buf("lam")
                    gsc = T(esa, "gsc", [128, 128], F32); bgsc = Buf("gsc")
                    t1 = T(esa, "t1", [128, 256], F32); bt1 = Buf("t1")
                    t2 = T(esa, "t2", [128, 256], F32); bt2 = Buf("t2")
                    kb16 = T(esa, "kb16", [128, 16, 128], BF16); bkb = Buf("kb16")
                    qb16 = T(esa, "qb16", [128, 8, 128], BF16); bqb = Buf("qb16")
                    vb = T(esa, "vb", [128, 16, 128], BF16); bvb = Buf("vb")
                    kst = T(esa, "kst", [128, 8, 128], F32); bkst = Buf("kst")
                    vst = T(esa, "vst", [128, 8, 128], F32); bvst = Buf("vst")
                    kT = [T(esa, f"kT{m}", [65, NSLOT], BF16) for m in range(2)]; bkT = [Buf("kT0"), Buf("kT1")]
                    qT = [T(esa, f"qT{m}", [65, TOWN], BF16) for m in range(2)]; bqT = [Buf("qT0"), Buf("qT1")]
                    Pp = [T(esa, f"Pp{i}", [128, 512], BF16) for i in range(3)]; bPp = [Buf(f"Pp{i}") for i in range(3)]
                    PTp = [T(esa, f"PTp{i}", [128, 4, 128], BF16) for i in range(3)]; bPTp = [Buf(f"PTp{i}") for i in range(3)]
                    cmaskb = T(esa, "cmaskb", [128, 128], BF16); bcmb = Buf("cmaskb")
                    nrm = T(esa, "nrm", [128, 16, 4], F32); bnrm = Buf("nrm")
                    nbq = T(esa, "nbq", [128, 8, 2], F32); bnbq = Buf("nbq")
                    kmx = T(esa, "kmx", [128, 4], F32); bkmx = Buf("kmx")
                    kmx2 = T(esa, "kmx2", [2, 132], F32); bkmx2 = Buf("kmx2")
                    zc = T(esa, "zc", [128, 2, 4], F32); bzc = Buf("zc")
                    bSp = [Buf(f"psS{i}") for i in range(4)]
                    bTh1 = Buf("psT_h1")
                    acnt = [0, 0, 0]
                    st = T(esa, "st", [128, 16], F32); bst = Buf("st")
                    att = T(esa, "att", [128, 128], F32); batt = Buf("att")
                    att2 = T(esa, "att2", [128, 128], F32); batt2 = Buf("att2")
                    attb = T(esa, "attb", [128, 128], BF16); battb = Buf("attb")
                    junk = T(esa, "junk", [128, 128], F32); bjunk = Buf("junk")
                    ksst = T(esa, "ksst", [NS, 8, 128], F32); bkss = Buf("ksst")
                    vsst = T(esa, "vsst", [NS, 8, 128], F32); bvss = Buf("vsst")
                    qs = T(esa, "qs", [NS, 8, 128], F32); bqs = Buf("qs")
                    msts = T(esa, "msts", [128, 8, NS], BF16); bmsts = Buf("msts")
                    ts1 = T(esa, "ts1", [NS, 256], F32); bts1 = Buf("ts1")
                    ts2 = T(esa, "ts2", [NS, 256], F32); bts2 = Buf("ts2")
                    ptT = T(esa, "ptT", [NPAGES, NS], I32); bpt = Buf("ptT")
                    ptf = T(esa, "ptf", [NPAGES, NS], F32); bptf = Buf("ptf")
                    ptc = T(esa, "ptc", [NPAGES, NS, 8], I32); bptc = Buf("ptc")
                    sel4 = T(esa, "sel4", [NS, NS * 65], F32); bsel = Buf("sel4")
                    ones65 = T(esa, "ones65", [65, 65], F32); bon = Buf("ones65")
                    CH = 16
                    NCH = 128 // CH
                    Kt = [T(esa, f"Kt{i}", [65, CH * 128], F32) for i in range(2)]; bKt = [Buf("Kt0"), Buf("Kt1")]
                    qbc = T(esa, "qbc", [65, NS, 128], F32); bqbc = Buf("qbc")
                    knew = T(esa, "knew", [65, NS, 128], F32); bknew = Buf("knew")
                    sc = T(esa, "sc", [65, NS, 128, 2], F32); bsc2 = Buf("sc")
                    ee = T(esa, "ee", [65, NS, 128, 2], F32); bee = Buf("ee")
                    wv = T(esa, "wv", [65, 128], F32); bwv = Buf("wv")
                    Bz = T(esa, "Bz", [65, NS, 128, 7], BF16); bBz = Buf("Bz")
                    Vb = [T(esa, f"Vb{i}", [65, CH * 128], BF16) for i in range(2)]; bVb = [Buf("Vb0"), Buf("Vb1")]
                    dsm = T(esa, "dsm", [65, 40], F32); bdsm = Buf("dsm")
                    dsm2 = T(esa, "dsm2", [8, 80], F32); bdsm2 = Buf("dsm2")
                    kcnt = [0]

                    dma("sync", rcos[:], rcos_d, (), [brc])
                    dma("sync", rsin[:], rsin_d, (), [brs])
                    dma("sync", rsam[:], rsam_d, (), [brsm])
                    dma("sync", cmask[:], cmask_d, (), [bcm])
                    dma("gpsimd", cmaskb[:], cmask_d, (), [bcmb])
                    for m in range(2):
                        dma("gpsimd", kT[m][64:65, :], pbrow_d, (), [bkT[m]])
                        memset("gpsimd", qT[m][64:65, :], 1.0, [bqT[m]])
                    memset("vector", nrm[:], 0.0, [bnrm])
                    dma("sync", lamv[:], lamv_d, (), [blv])
                    dma("sync", gsc[:], subg_d, (), [bgsc])
                    dma("sync", ptT[:], pt_d, (), [bpt])
                    dma("sync", sel4[:], sel4_d, (), [bsel])
                    dma("sync", ones65[:], ones_d, (), [bon])
                    tt("vector", t1[:, 0:64], lamv[:, 0, :], lamv[:, 1, :], ALU.mult, [blv], [bt1])
                    tt("vector", t1[:, 64:128], lamv[:, 2, :], lamv[:, 3, :], ALU.mult, [blv], [bt1])
                    red("vector", lam[:, 2:4], t1[:, 0:128].rearrange("p (a b) -> p a b", a=2), ALU.add, [bt1], [blam])
                    act(lam[:, 2:4], lam[:, 2:4], AF.Exp, [blam], [blam])
                    tt("vector", lam[:, 0:1], lam[:, 2:3], lam[:, 3:4], ALU.subtract, [blam], [blam])
                    ts("vector", lam[:, 0:1], lam[:, 0:1], LAM_INIT, None, ALU.add, None, [blam], [blam])
                    ts("vector", lam[:, 1:2], lam[:, 0:1], -1.0, None, ALU.mult, None, [blam], [blam])
                    ts("vector", gsc[:], gsc[:], 1.0 - LAM_INIT, None, ALU.mult, None, [bgsc], [bgsc])
                    cp("vector", ptf[:], ptT[:], [bpt], [bptf])
                    for c in range(8):
                        ts("vector", ptc[:, :, c], ptf[:], 8.0, float(c), ALU.mult, ALU.add, [bptf], [bptc])
                    for i in range(2):
                        memset("vector", Kt[i][:], 0.0, [bKt[i]])
                    memset("vector", Bz[:], 0.0, [bBz])
                    memset("vector", knew[:], 0.0, [bknew])

                    def rope(ps, G, cos_ap, sin0_ap, sin1_ap, np_, o1, o2, bo1, bo2, rdeps):
                        pv = ps[0:np_, 0:G * 64].rearrange("p (g h f) -> p g h f", g=G, h=2)
                        o1v = o1[0:np_, 0:G * 64].rearrange("p (g h f) -> p g h f", g=G, h=2)
                        o2v = o2[0:np_, 0:G * 64].rearrange("p (g h f) -> p g h f", g=G, h=2)
                        cb = cos_ap.unsqueeze(1).unsqueeze(1).to_broadcast([np_, G, 2, 32])
                        tt("vector", o1v, pv, cb, ALU.mult, rdeps, [bo1])
                        tt("vector", o2v[:, :, 0, :], pv[:, :, 1, :], sin0_ap.unsqueeze(1).to_broadcast([np_, G, 32]), ALU.mult, rdeps, [bo2])
                        tt("vector", o2v[:, :, 1, :], pv[:, :, 0, :], sin1_ap.unsqueeze(1).to_broadcast([np_, G, 32]), ALU.mult, rdeps, [bo2])
                        tt("vector", o1[0:np_, 0:G * 64], o1[0:np_, 0:G * 64], o2[0:np_, 0:G * 64], ALU.add, [bo1, bo2], [bo1])

                    def rmsnorm_to_mix(src_ps_or_sb, np_, h, col0, ncol, rdeps, dest=None, bdest=None):
                        act(junk[0:np_, :], att[0:np_, :], AF.Square, [batt], [bjunk, bst], accum=st[0:np_, 8:9])
                        act(st[0:np_, 9:10], st[0:np_, 8:9], AF.Sqrt, [bst], [bst], bias=EPS, scale=1.0 / 128.0)
                        recip(st[0:np_, 10:11], st[0:np_, 9:10], [bst], [bst])
                        stt(attb[0:np_, :], att[0:np_, :], st[0:np_, 10:11], gsc[0:np_, :], ALU.mult, ALU.mult, [batt, bst, bgsc], [battb])
                        tr(psT[:, 0:np_], attb[0:np_, :], identb[0:np_, 0:np_], [battb, bidb], [bT])
                        if dest is None:
                            cp("scalar", mstg[:, col0:col0 + ncol], psT[:, 0:ncol], [bT], [bmstg])
                        else:
                            cp("scalar", dest, psT[:, 0:ncol], [bT], [bdest])

                    def dec_gen(h):
                        for s in range(NS):
                            mm(psC[0:65, 0:128], sel4[:, s * 65:(s + 1) * 65], qs[:, h, :], True, True, [bsel, bqs], [bC])
                            cp("scalar", qbc[:, s, :], psC[0:65, 0:128], [bC], [bqbc])
                            dma("sync", knew[64:65, s, :], ksst[s:s + 1, h, :], [bkss], [bknew])
                        for s in range(NS):
                            for c in range(NCH):
                                bi = kcnt[0] % 2
                                kcnt[0] += 1
                                idma(Kt[bi][0:64, :], bass.AP(tensor=ck_d.tensor, offset=0, ap=[[CH * 128, npool * 8], [1, CH * 128]]), ptc[:, s, c:c + 1],
                                     [bptc], [bKt[bi]], eoff=h * npool * 16384)
                                kv = Kt[bi][0:64, :].rearrange("p (t f) -> p t f", t=CH)
                                tt("vector", kv, kv, qbc[0:64, s, :].unsqueeze(1).to_broadcast([64, CH, 128]), ALU.mult, [bKt[bi], bqbc], [bKt[bi]])
                                red("vector", sc[0:64, s, c * CH:(c + 1) * CH, :].rearrange("p t m -> p (t m)"),
                                    Kt[bi][0:64, :].rearrange("p (a d) -> p a d", d=64), ALU.add, [bKt[bi]], [bsc2])
                                yield 1
                        tt("vector", knew[64:65, :, :], knew[64:65, :, :], qbc[64:65, :, :], ALU.mult, [bknew, bqbc], [bknew])
                        red("vector", sc[64:65, :, 0, :], knew[64:65, :, :].rearrange("p s (m d) -> p s m d", d=64), ALU.add, [bknew], [bsc2])
                        memset("gpsimd", sc[64:65, :, 1:, :], -1e30, [bsc2])
                        red("vector", dsm[:, 0:8].rearrange("p (s m) -> p s m", m=2), sc[:].rearrange("p s t m -> p s m t"), ALU.max, [bsc2], [bdsm])
                        P.op("tensor", lambda e: e.transpose(out=psC[0:8, 128:193], in_=dsm[:, 0:8], identity=identf[0:65, 0:65]), [bdsm, bidf], [bC])
                        red("vector", dsm2[:, 0:1], psC[0:8, 128:193], ALU.max, [bC], [bdsm2])
                        cp("vector", dsm2[:, 8:73], dsm2[:, 0:1].to_broadcast([8, 65]), [bdsm2], [bdsm2])
                        P.op("tensor", lambda e: e.transpose(out=psC[0:65, 256:264], in_=dsm2[:, 8:73], identity=identf[0:8, 0:8]), [bdsm2, bidf], [bC])
                        cp("vector", dsm[:, 8:16], psC[0:65, 256:264], [bC], [bdsm])
                        tt("vector", ee[:], sc[:], dsm[:, 8:16].rearrange("p (s m) -> p s m", m=2).unsqueeze(2).to_broadcast([65, NS, 128, 2]), ALU.subtract,
                           [bsc2, bdsm], [bee])
                        act(ee[:], ee[:], AF.Exp, [bee], [bee], scale=0.125)
                        red("vector", dsm[:, 16:24].rearrange("p (s m) -> p s m", m=2), ee[:].rearrange("p s t m -> p s m t"), ALU.add, [bee], [bdsm])
                        mm(psC[0:65, 320:328], ones65[:], dsm[:, 16:24], True, True, [bon, bdsm], [bC])
                        recip(dsm[:, 24:32], psC[0:65, 320:328], [bC], [bdsm])
                        rzv = dsm[:, 24:32].rearrange("p (s m) -> p s m", m=2)
                        ts("vector", dsm[:, 32:36], rzv[:, :, 1], lam[0:65, 1:2], None, ALU.mult, None, [bdsm, blam], [bdsm])
                        for s in range(NS):
                            ts("vector", wv[:], ee[:, s, :, 0], rzv[:, s, 0:1], None, ALU.mult, None, [bee, bdsm], [bwv])
                            stt(Bz[:, s, :, 3], ee[:, s, :, 1], dsm[:, 32 + s:33 + s], wv[:], ALU.mult, ALU.add, [bee, bdsm, bwv], [bBz])
                        yield 1
                        first_pv = True
                        for s in range(NS):
                            for c in range(NCH):
                                bi = kcnt[0] % 2
                                kcnt[0] += 1
                                idma(Kt[bi][0:64, :], bass.AP(tensor=cv_d.tensor, offset=0, ap=[[CH * 128, npool * 8], [1, CH * 128]]), ptc[:, s, c:c + 1],
                                     [bptc], [bKt[bi]], eoff=h * npool * 16384)
                                if c == 0:
                                    dma("sync", Kt[bi][64:65, 0:128], vsst[s:s + 1, h, :], [bvss], [bKt[bi]])
                                cp("scalar", Vb[bi][:], Kt[bi][:], [bKt[bi]], [bVb[bi]])
                                for t in range(CH):
                                    tok = c * CH + t
                                    last = (s == NS - 1 and c == NCH - 1 and t == CH - 1)
                                    mm(psB[0:NS, 0:128], Bz[:, s, tok, 3 - s:7 - s], Vb[bi][:, t * 128:(t + 1) * 128], first_pv, last, [bBz, bVb[bi]], [bB])
                                    first_pv = False
                                yield 1
                        cp("vector", att[0:NS, :], psB[0:NS, 0:128], [bB], [batt])
                        rmsnorm_to_mix(None, NS, h, TOWN, NS, None, dest=msts[:, h, :], bdest=bmsts)
                        yield 1

                    pending = None
                    for h in range(8):
                        qc, kc, vc = 2048 + h * 128, 3072 + h * 128, 4096 + h * 128
                        for ci, c0 in enumerate((qc, kc, vc)):
                            dma("gpsimd", wq3[:, :, ci * 128:(ci + 1) * 128], w_in_d[:, c0:c0 + 128].rearrange("(k p) c -> p k c", p=128), (), [bw3])
                        for tti in range(16):
                            own = tti >= 8
                            ps, bp = psrot[tti % 3]
                            if own:
                                for k in range(16):
                                    mm(ps[:, 0:384], xT[:, k, tti * 128:(tti + 1) * 128], wq3[:, k, 0:384], k == 0, k == 15, [bxT[k], bw3], [bp])
                                rope(ps, 4, rcos[:, tti, :], rsin[:, tti, 0, :], rsin[:, tti, 1, :], 128, t1, t2, bt1, bt2, [bp, brc, brs])
                                tt("gpsimd", t2[:, 0:256], t1[:, 0:256], t1[:, 0:256], ALU.mult, [bt1, bt2], [bt2])
                                red("vector", nrm[:, tti, 0:4], t2[:, 0:256].rearrange("p (g d) -> p g d", d=64), ALU.add, [bt2], [bnrm])
                                cp("scalar", qb16[:, tti - 8, :], t1[:, 0:128], [bt1], [bqb])
                                cp("scalar", kb16[:, tti, :], t1[:, 128:256], [bt1], [bkb])
                                cp("gpsimd", kst[:, tti - 8, :], t1[:, 128:256], [bt1], [bkst])
                                cp("scalar", vb[:, tti, :], ps[:, 256:384], [bp], [bvb])
                                cp("scalar", vst[:, tti - 8, :], ps[:, 256:384], [bp], [bvst])
                            else:
                                for k in range(16):
                                    mm(ps[:, 0:256], xT[:, k, tti * 128:(tti + 1) * 128], wq3[:, k, 128:384], k == 0, k == 15, [bxT[k], bw3], [bp])
                                rope(ps, 2, rcos[:, tti, :], rsin[:, tti, 0, :], rsin[:, tti, 1, :], 128, t1, t2, bt1, bt2, [bp, brc, brs])
                                tt("gpsimd", t2[:, 0:128], t1[:, 0:128], t1[:, 0:128], ALU.mult, [bt1, bt2], [bt2])
                                red("vector", nrm[:, tti, 2:4], t2[:, 0:128].rearrange("p (g d) -> p g d", d=64), ALU.add, [bt2], [bnrm])
                                cp("scalar", kb16[:, tti, :], t1[:, 0:128], [bt1], [bkb])
                                cp("scalar", vb[:, tti, :], ps[:, 128:256], [bp], [bvb])
                            conv_steps(1)
                        dma("sync", k_o[:, h * 128:(h + 1) * 128].rearrange("(n p) f -> p n f", p=128), kst[:], [bkst], (), final=True)
                        dma("sync", v_o[:, h * 128:(h + 1) * 128].rearrange("(n p) f -> p n f", p=128), vst[:], [bvst], (), final=True)
                        for m in range(2):
                            for g in range(2):
                                for j in range(8):
                                    tr(psT[0:64, j * 128:(j + 1) * 128], kb16[:, g * 8 + j, m * 64:(m + 1) * 64], identb[:], [bkb, bidb], [bT, bTh1])
                                cp("vector" if g == 0 else "scalar", kT[m][0:64, g * 1024:(g + 1) * 1024], psT[0:64, :], [bT, bTh1], [bkT[m]])
                            for j in range(8):
                                tr(psT[0:64, j * 128:(j + 1) * 128], qb16[:, j, m * 64:(m + 1) * 64], identb[:], [bqb, bidb], [bT, bTh1])
                            cp("vector", qT[m][0:64, :], psT[0:64, :], [bT, bTh1], [bqT[m]])
                        red("vector", kmx[:, 0:2], nrm[:, :, 2:4].rearrange("p t m -> p m t"), ALU.max, [bnrm], [bkmx])
                        P.op("tensor", lambda e: e.transpose(out=psC[0:2, 0:128], in_=kmx[:, 0:2], identity=identf[:]), [bkmx, bidf], [bC])
                        red("vector", kmx2[:, 0:1], psC[0:2, 0:128], ALU.max, [bC], [bkmx2])
                        cp("vector", kmx2[:, 4:132], kmx2[:, 0:1].to_broadcast([2, 128]), [bkmx2], [bkmx2])
                        P.op("tensor", lambda e: e.transpose(out=psC[:, 128:130], in_=kmx2[:, 4:132], identity=identf[0:2, 0:2]), [bkmx2, bidf], [bC])
                        cp("vector", kmx[:, 2:4], psC[:, 128:130], [bC], [bkmx])
                        tt("vector", nbq[:], nrm[:, 8:16, 0:2], kmx[:, 2:4].unsqueeze(1).to_broadcast([128, 8, 2]), ALU.mult, [bnrm, bkmx], [bnbq])
                        act(nbq[:], nbq[:], AF.Sqrt, [bnbq], [bnbq])
                        ts("vector", nbq[:], nbq[:], -0.125, None, ALU.mult, None, [bnbq], [bnbq])
                        for i in range(8):
                            nk = 1024 + (i + 1) * 128
                            pieces = [(c0, min(512, nk - c0)) for c0 in range(0, nk, 512)]
                            npc = len(pieces)
                            for m in range(2):
                                for pi, (c0, w_) in enumerate(pieces):
                                    sb = acnt[0] % 4
                                    acnt[0] += 1
                                    lastp = (pi == npc - 1)
                                    S = psS[:, sb * 512:sb * 512 + w_]
                                    mm(S, qT[m][:, i * 128:(i + 1) * 128], kT[m][:, c0:c0 + w_], True, not lastp, [bqT[m], bkT[m]], [bSp[sb]])
                                    if lastp:
                                        mm(psS[:, sb * 512 + w_ - 128:sb * 512 + w_], identb[:], cmaskb[:], False, True, [bidb, bcmb], [bSp[sb]])
                                    pb_ = acnt[1] % 3
                                    acnt[1] += 1
                                    act(Pp[pb_][:, 0:w_], S, AF.Exp, [bSp[sb], bnbq], [bPp[pb_], bzc], bias=nbq[:, i, m:m + 1], scale=0.125, accum=zc[:, m, pi:pi + 1])
                                    nblk = w_ // 128
                                    tb = acnt[2] % 2
                                    acnt[2] += 1
                                    btb = bT if tb == 0 else bTh1
                                    for j in range(nblk):
                                        tr(psT[:, tb * 512 + j * 128:tb * 512 + (j + 1) * 128], Pp[pb_][:, j * 128:(j + 1) * 128], identb[:], [bPp[pb_], bidb], [btb])
                                    cp("scalar" if (acnt[2] % 3 == 0) else "vector", PTp[pb_][:, 0:nblk, :],
                                       psT[:, tb * 512:tb * 512 + nblk * 128].rearrange("p (a b) -> p a b", a=nblk), [btb], [bPTp[pb_]])
                                    for j in range(nblk):
                                        kb = c0 // 128 + j
                                        mm(psA[:, m * 128:(m + 1) * 128], PTp[pb_][:, j, :], vb[:, kb, :], (pi == 0 and j == 0), (lastp and j == nblk - 1),
                                           [bPTp[pb_], bvb], [bA])
                            red("vector", st[:, 5:7], zc[:, :, 0:npc], ALU.add, [bzc], [bst])
                            recip(st[:, 11:13], st[:, 5:7], [bst], [bst])
                            ts("vector", att2[:], psA[:, 0:128], st[:, 11:12], None, ALU.mult, None, [bA, bst], [batt2])
                            tt("vector", st[:, 13:14], st[:, 12:13], lam[:, 1:2], ALU.mult, [bst, blam], [bst])
                            stt(att[:], psA[:, 128:256], st[:, 13:14], att2[:], ALU.mult, ALU.add, [bA, bst, batt2], [batt])
                            rmsnorm_to_mix(None, 128, h, i * 128, 128, None)
                            conv_steps(4)
                            if pending is not None:
                                for _ in range(10):
                                    if next(pending, "done") == "done":
                                        pending = None
                                        break

                        ps, bp = psrot[2]
                        for k in range(16):
                            mm(ps[0:NS, 0:384], xsT[:, k, :], wq3[:, k, 0:384], k == 0, k == 15, [bxs, bw3], [bp])
                        rope(ps, 4, rsam[:, 0:32], rsam[:, 32:64], rsam[:, 64:96], NS, ts1, ts2, bts1, bts2, [bp, brsm])
                        cp("vector", qs[:, h, :], ts1[:, 0:128], [bts1], [bqs])
                        cp("vector", ksst[:, h, :], ts1[:, 128:256], [bts1], [bkss])
                        cp("vector", vsst[:, h, :], ps[0:NS, 256:384], [bp], [bvss])
                        if do_dec:
                            if pending is not None:
                                for _ in pending:
                                    pass
                            pending = dec_gen(h)
                        dma("sync", mix_d[8 + h][:, 0:TOWN], mstg[:, 0:TOWN], [bmstg], [bmix[8 + h]])
                    if pending is not None:
                        for _ in pending:
                            pass
                    if not do_dec:
                        memset("vector", msts[:], 0.0, [bmsts])
                    with nc.allow_non_contiguous_dma(reason="tiny sample columns"):
                        for hh in range(8):
                            dma("sync", mix_d[8 + hh][:, TOWN:TTOK], msts[:, hh, :], [bmsts], [bmix[8 + hh]])
                    conv_steps(100000)
                    dma("sync", ks_o.rearrange("s (h f) -> s h f", h=8), ksst[:], [bkss], (), final=True)
                    dma("sync", vs_o.rearrange("s (h f) -> s h f", h=8), vsst[:], [bvss], (), final=True)
                    P.barrier()
                    P.flush()
        if dbg:
            esd = contextlib.ExitStack()
            with esd:
                mixb = T(esd, "mixb", [128, TTOK], BF16); bmb = Buf("mixb")
                mixf = T(esd, "mixf", [128, TTOK], F32); bmf = Buf("mixf")
                for kk in range(16):
                    dma("sync", mixb[:], mix_d[kk], [bmix[kk]], [bmb])
                    cp("vector", mixf[:], mixb[:], [bmb], [bmf])
                    dma("sync", mix_o[:, kk, :], mixf[:], [bmf], (), final=True)
                P.barrier()
                P.flush()

        NT = 9
        bx1d = [Buf(f"x1d{t}") for t in range(NT)]
        if do_p2:
            es2 = contextlib.ExitStack()
            with es2:
                lnp = T(es2, "lnp", [128, 2, D], F32); blnp = Buf("lnp")
                xt = T(es2, "xt", [128, D], F32); bxt = Buf("xt")
                st2 = T(es2, "st2", [128, 16], F32); bst2 = Buf("st2")
                E = T(es2, "E", [128, NT, 128], I32); bE = [Buf(f"E{t}") for t in range(NT)]
                G = T(es2, "G", [128, NT, 128], F32); bG = [Buf(f"G{t}") for t in range(NT)]

                def layer_norm(buf_ap, bbuf, np_):
                    red("vector", st2[0:np_, 0:1], buf_ap, ALU.add, [bbuf], [bst2])
                    ts("vector", st2[0:np_, 1:2], st2[0:np_, 0:1], -1.0 / D, None, ALU.mult, None, [bst2], [bst2])
                    ts("vector", buf_ap, buf_ap, st2[0:np_, 1:2], None, ALU.add, None, [bbuf, bst2], [bbuf])
                    act(xt[0:np_, :], buf_ap, AF.Square, [bbuf], [bxt, bst2], accum=st2[0:np_, 2:3])
                    act(st2[0:np_, 3:4], st2[0:np_, 2:3], AF.Sqrt, [bst2], [bst2], bias=EPS, scale=1.0 / D)
                    recip(st2[0:np_, 4:5], st2[0:np_, 3:4], [bst2], [bst2])
                    stt(buf_ap, buf_ap, st2[0:np_, 4:5], lnp[0:np_, 0, :], ALU.mult, ALU.mult, [bbuf, bst2, blnp], [bbuf])
                    tt("gpsimd", buf_ap, buf_ap, lnp[0:np_, 1, :], ALU.add, [bbuf, blnp], [bbuf])

                esA = contextlib.ExitStack()
                with esA:
                    wbuf = T(esA, "wbufA", [128, 16, D], BF16); bwb = Buf("wbufA")
                    mixt = [T(esA, f"mixt{i}", [128, 16, 128], BF16) for i in range(2)]; bmt = [Buf("mixt0"), Buf("mixt1")]
                    x1t = [T(esA, f"x1tA{i}", [128, D], F32) for i in range(2)]; bx1t = [Buf("x1tA0"), Buf("x1tA1")]
                    xin = [T(esA, f"xin{i}", [128, D], F32) for i in range(2)]; bxin = [Buf("xin0"), Buf("xin1")]
                    for k in range(16):
                        dma("gpsimd", wbuf[:, k, :], w_out_d[k * 128:(k + 1) * 128, :], (), [bwb])
                    dma("sync", lnp[:], ln_d[:, 0:2, :], (), [blnp])
                    for t in range(NT):
                        np_ = 128 if t < 8 else NS
                        bi = t % 2
                        dma("sync", xin[bi][0:np_, :], xtok_d[t * 128:t * 128 + np_, :], (), [bxin[bi]])
                        dma("sync", mixt[bi][:, :, 0:np_], mix_d[:, :, t * 128:t * 128 + np_].rearrange("k p t -> p k t"), bmix, [bmt[bi]])
                        for nb in range(4):
                            ps, bp = psrot[nb % 3]
                            for kk in range(16):
                                mm(ps[0:np_, 0:512], mixt[bi][:, kk, 0:np_], wbuf[:, kk, nb * 512:(nb + 1) * 512], kk == 0, kk == 15,
                                   [bmt[bi], bwb], [bp])
                            stt(x1t[bi][0:np_, nb * 512:(nb + 1) * 512], xin[bi][0:np_, nb * 512:(nb + 1) * 512], ALPHA, ps[0:np_, 0:512], ALU.mult, ALU.add,
                                [bxin[bi], bp], [bx1t[bi]])
                        layer_norm(x1t[bi][0:np_, :], bx1t[bi], np_)
                        dma("sync", x1_d[t * 128:t * 128 + np_, :], x1t[bi][0:np_, :], [bx1t[bi]], [bx1d[t]])
                        if dbg:
                            dma("sync", x1_o[t * 128:t * 128 + np_, :], x1t[bi][0:np_, :], [bx1t[bi]], (), final=True)
                    P.barrier()
                    P.flush()

                if do_peer:
                    esB = contextlib.ExitStack()
                    with esB:
                        wbuf = T(esB, "wbufB", [128, 16, D], BF16); bwb = Buf("wbufB")
                        k1T = T(esB, "k1T", [128, 8, 128], F32); bk1 = Buf("k1T")
                        k2T = T(esB, "k2T", [128, 8, 128], F32); bk2 = Buf("k2T")
                        iota16 = T(esB, "iota16", [128, 16, 16], F32); bio = Buf("iota")
                        x1f = T(esB, "x1f", [128, D], F32); bx1f = Buf("x1f")
                        x1T = T(esB, "x1T", [128, 16, 128], BF16); bx1T = Buf("x1T")
                        x1b = T(esB, "x1b", [128, D], BF16); bx1b = Buf("x1b")
                        qTf = T(esB, "qTf", [128, 16, 128], F32); bqTf = Buf("qTf")
                        S12 = T(esB, "S12", [128, 16, 128], F32); bS12 = Buf("S12")
                        wk = T(esB, "wk", [128, 256], F32); bwk = Buf("wk")
                        V12 = T(esB, "V12", [128, 16, 16], F32); bV12 = Buf("V12")
                        I12 = T(esB, "I12", [128, 16, 16], U32); bI12 = Buf("I12")
                        I12f = T(esB, "I12f", [128, 16, 16], F32); bI12f = Buf("I12f")
                        comb = T(esB, "comb", [128, 8, 256], F32); bcomb = Buf("comb")
                        sv = T(esB, "sv", [128, 8, 16], F32); bsv = Buf("sv")
                        svx = T(esB, "svx", [128, 8, 16], F32); bsvx = Buf("svx")
                        si = T(esB, "si", [128, 8, 16], U32); bsi = Buf("si")
                        sif = T(esB, "sif", [128, 8, 16], F32); bsif = Buf("sif")
                        sab = T(esB, "sab", [128, 2, 8, 16], F32); bsab = Buf("sab")
                        oh = T(esB, "oh", [128, 8, 16, 16], F32); boh = Buf("oh")
                        isel = T(esB, "isel", [128, 2, 8, 16], F32); bisel = Buf("isel")
                        ef = T(esB, "ef", [128, 128], F32); bef = Buf("ef")

                        for k in range(16):
                            dma("gpsimd", wbuf[:, k, :], wq_d[k * 128:(k + 1) * 128, :], (), [bwb])
                        dma("sync", k1T[:], k1T_d, (), [bk1])
                        dma("sync", k2T[:], k2T_d, (), [bk2])
                        dma("sync", iota16[:], iota_d.rearrange("p (a b) -> p a b", a=16), (), [bio])
                        memset("vector", x1f[:], 0.0, [bx1f])

                        def bc4(ap3, h0):
                            return ap3.unsqueeze(1).to_broadcast([128, 4, 16, 16])

                        for t in range(NT):
                            np_ = 128 if t < 8 else NS
                            dma("sync", x1f[0:np_, :], x1_d[t * 128:t * 128 + np_, :], [bx1d[t]], [bx1f])
                            cp("scalar", x1b[:], x1f[:], [bx1f], [bx1b])
                            for g in range(2):
                                for j in range(8):
                                    tr(psT[:, j * 128:(j + 1) * 128], x1b[:, (g * 8 + j) * 128:(g * 8 + j + 1) * 128], identb[:], [bx1b, bidb], [bT])
                                cp("vector", x1T[:, g * 8:(g + 1) * 8, :], psT[:, :].rearrange("p (a b) -> p a b", a=8), [bT], [bx1T])
                            for c in range(16):
                                ps, bp = psrot[c % 3]
                                for kk in range(16):
                                    mm(ps[:, 0:128], wbuf[:, kk, c * 128:(c + 1) * 128], x1T[:, kk, :], kk == 0, kk == 15, [bwb, bx1T], [bp])
                                cp("scalar" if c % 2 == 0 else "vector", qTf[:, c, :], ps[:, 0:128], [bp], [bqTf])
                            for c in range(16):
                                hh, half = c // 2, c % 2
                                ps, bp = psrot[c % 3]
                                kk_ap = k1T[:, hh, :] if half == 0 else k2T[:, hh, :]
                                mm(ps[:, 0:128], qTf[:, c, :], kk_ap, True, True, [bqTf, bk1, bk2], [bp])
                                cp("scalar" if c % 2 == 0 else "vector", S12[:, c, :], ps[:, 0:128], [bp], [bS12])
                            for c in range(16):
                                P.op("vector", lambda e, c=c: e.max(out=V12[:, c, 0:8], in_=S12[:, c, :]), [bS12], [bV12])
                                P.op("vector", lambda e, c=c: e.max_index(out=I12[:, c, 0:8], in_max=V12[:, c, 0:8], in_values=S12[:, c, :]), [bS12, bV12], [bI12])
                                P.op("vector", lambda e, c=c: e.match_replace(out=wk[:, 0:128], in_to_replace=V12[:, c, 0:8], in_values=S12[:, c, :], imm_value=-1e30),
                                     [bS12, bV12], [bwk])
                                P.op("vector", lambda e, c=c: e.max(out=V12[:, c, 8:16], in_=wk[:, 0:128]), [bwk], [bV12])
                                P.op("vector", lambda e, c=c: e.max_index(out=I12[:, c, 8:16], in_max=V12[:, c, 8:16], in_values=wk[:, 0:128]), [bwk, bV12], [bI12])
                            cp("vector", I12f[:], I12[:], [bI12], [bI12f])
                            V4 = V12[:].rearrange("p (h two) k -> p h two k", two=2)
                            I4 = I12f[:].rearrange("p (h two) k -> p h two k", two=2)
                            cv4 = comb[:].rearrange("p h (a b) -> p h a b", a=16)
                            for hq in (0, 4):
                                tt("vector", cv4[:, hq:hq + 4], V4[:, hq:hq + 4, 0, :].unsqueeze(3).to_broadcast([128, 4, 16, 16]),
                                   V4[:, hq:hq + 4, 1, :].unsqueeze(2).to_broadcast([128, 4, 16, 16]), ALU.add, [bV12], [bcomb])
                            for hh in range(8):
                                P.op("vector", lambda e, hh=hh: e.max(out=sv[:, hh, 0:8], in_=comb[:, hh, :]), [bcomb], [bsv])
                                P.op("vector", lambda e, hh=hh: e.max_index(out=si[:, hh, 0:8], in_max=sv[:, hh, 0:8], in_values=comb[:, hh, :]), [bcomb, bsv], [bsi])
                                P.op("vector", lambda e, hh=hh: e.match_replace(out=wk[:], in_to_replace=sv[:, hh, 0:8], in_values=comb[:, hh, :], imm_value=-1e30),
                                     [bcomb, bsv], [bwk])
                                P.op("vector", lambda e, hh=hh: e.max(out=sv[:, hh, 8:16], in_=wk[:]), [bwk], [bsv])
                                P.op("vector", lambda e, hh=hh: e.max_index(out=si[:, hh, 8:16], in_max=sv[:, hh, 8:16], in_values=wk[:]), [bwk, bsv], [bsi])
                            cp("vector", sif[:], si[:], [bsi], [bsif])
                            ts("vector", svx[:], sif[:], 0.0625, -1.0, ALU.mult, ALU.add, [bsif], [bsvx])
                            for hq in (0, 4):
                                tt("vector", oh[:, hq:hq + 4], svx[:, hq:hq + 4, :].unsqueeze(3).to_broadcast([128, 4, 16, 16]), bc4(iota16[:], hq), ALU.is_ge,
                                   [bsvx, bio], [boh])
                            red("vector", sab[:, 0].rearrange("p h j -> p (h j)"), oh[:].rearrange("p h j a -> p (h j) a"), ALU.add, [boh], [bsab])
                            stt(sab[:, 1].rearrange("p h j -> p (h j)"), sab[:, 0].rearrange("p h j -> p (h j)"), -16.0, sif[:].rearrange("p h j -> p (h j)"),
                                ALU.mult, ALU.add, [bsab, bsif], [bsab])
                            for side in range(2):
                                for hq in (0, 4):
                                    tt("vector", oh[:, hq:hq + 4], bc4(iota16[:], hq), sab[:, side, hq:hq + 4, :].unsqueeze(3).to_broadcast([128, 4, 16, 16]),
                                       ALU.is_equal, [bio, bsab], [boh])
                                    tt("vector", oh[:, hq:hq + 4], oh[:, hq:hq + 4], I4[:, hq:hq + 4, side, :].unsqueeze(2).to_broadcast([128, 4, 16, 16]),
                                       ALU.mult, [boh, bI12f], [boh])
                                red("vector", isel[:, side].rearrange("p h j -> p (h j)"), oh[:].rearrange("p h j a -> p (h j) a"), ALU.add, [boh], [bisel])
                            stt(ef[:], isel[:, 0].rearrange("p h j -> p (h j)"), 128.0, isel[:, 1].rearrange("p h j -> p (h j)"), ALU.mult, ALU.add, [bisel], [bef])
                            cp("vector", E[:, t, :], ef[:], [bef], [bE[t]])
                            tt("vector", svx[:], sv[:], sv[:, :, 0:1].to_broadcast([128, 8, 16]), ALU.subtract, [bsv, bsvx], [bsvx])
                            act(svx[:], svx[:], AF.Exp, [bsvx], [bsvx])
                            red("vector", st2[:, 8:16], svx[:], ALU.add, [bsvx], [bst2])
                            recip(st2[:, 8:16], st2[:, 8:16], [bst2], [bst2])
                            tt("vector", G[:, t, :].rearrange("p (h j) -> p h j", h=8), svx[:], st2[:, 8:16].unsqueeze(2).to_broadcast([128, 8, 16]), ALU.mult,
                               [bsvx, bst2], [bG[t]])
                        P.barrier()
                        P.flush()

                    esC = contextlib.ExitStack()
                    with esC:
                        NG = 20
                        gsl = [T(esC, f"gsl{i}", [128, D], BF16) for i in range(NG)]; bg = [Buf(f"g{i}") for i in range(NG)]
                        x1c = [T(esC, f"x1c{i}", [128, D], F32) for i in range(2)]; bx1c = [Buf("x1c0"), Buf("x1c1")]
                        accs = [T(esC, f"acc{i}", [128, D], F32) for i in range(2)]; bacc = [Buf("acc0"), Buf("acc1")]
                        junk = T(esC, "junkC", [128, D], BF16); bjunk = Buf("junkC")
                        dg = [T(esC, f"dg{i}", [128, 128], BF16) for i in range(4)]; bdg = [Buf(f"dg{i}") for i in range(4)]
                        actv = T(esC, "actv", [128, 128], F32); bactv = Buf("actv")
                        coef = T(esC, "coef", [128, 128], F32); bcoef = Buf("coef")
                        dma("sync", lnp[:], ln_d[:, 2:4, :], (), [blnp])
                        gi = 0
                        di = 0
                        for t in range(NT):
                            np_ = 128 if t < 8 else NS
                            bi = t % 2
                            dma("sync", x1c[bi][0:np_, :], x1_d[t * 128:t * 128 + np_, :], [bx1d[t]], [bx1c[bi]])
                            for s in range(128):
                                g_ap = gsl[gi][0:np_, :]
                                idma(g_ap, pub_d, E[0:np_, t, s:s + 1], [bE[t]], [bg[gi]])
                                stt(junk[0:np_, :], g_ap, 1.0, x1c[bi][0:np_, :], ALU.mult, ALU.mult, [bg[gi], bx1c[bi]], [bjunk, bactv], accum=actv[0:np_, s:s + 1])
                                gi = (gi + 1) % NG
                            act(coef[0:np_, :], actv[0:np_, :], AF.Gelu, [bactv], [bcoef])
                            tt("vector", coef[0:np_, :], coef[0:np_, :], G[0:np_, t, :], ALU.mult, [bcoef, bG[t]], [bcoef])
                            for s in range(128):
                                g_ap = gsl[gi][0:np_, :]
                                idma(g_ap, pvb_d, E[0:np_, t, s:s + 1], [bE[t]], [bg[gi]])
                                act(dg[di][0:np_, 0:np_], identb[0:np_, 0:np_], AF.Copy, [bidb, bcoef], [bdg[di]], scale=coef[0:np_, s:s + 1])
                                for nb in range(4):
                                    mm(psS[0:np_, nb * 512:(nb + 1) * 512], dg[di][0:np_, 0:np_], gsl[gi][0:np_, nb * 512:(nb + 1) * 512], s == 0, s == 127,
                                       [bdg[di], bg[gi]], [bS])
                                gi = (gi + 1) % NG
                                di = (di + 1) % 4
                            stt(accs[bi][0:np_, :], x1c[bi][0:np_, :], ALPHA, psS[0:np_, :], ALU.mult, ALU.add, [bx1c[bi], bS], [bacc[bi]])
                            layer_norm(accs[bi][0:np_, :], bacc[bi], np_)
                            dma("sync", y_o[t * 128:t * 128 + np_, :], accs[bi][0:np_, :], [bacc[bi]], (), final=True)
                        P.flush(last=True)
                else:
                    esC = contextlib.ExitStack()
                    with esC:
                        x1c = T(esC, "x1c", [128, D], F32); bx1c = Buf("x1c")
                        for t in range(NT):
                            np_ = 128 if t < 8 else NS
                            dma("sync", x1c[0:np_, :], x1_d[t * 128:t * 128 + np_, :], [bx1d[t]], [bx1c])
                            dma("sync", y_o[t * 128:t * 128 + np_, :], x1c[0:np_, :], [bx1c], (), final=True)
                        P.flush(last=True)
        else:
            P.flush(last=True)
    return nc


def _rope_tables(pos):
    half = 32
    inv = (10000.0 ** (-np.arange(half, dtype=np.float32) * 2.0 / 64.0)).astype(np.float32)
    ang = pos.astype(np.float32)[:, None] * inv[None, :]
    return np.cos(ang).astype(np.float32), np.sin(ang).astype(np.float32)


_CACHE = {}


def kernel(x_prompt, x_sample, cache_k, cache_v, state_conv, state_h, page_table,
           w_in, conv_w, conv_b, lru_wa, lru_ba, lru_wx, lru_bx, lru_lambda,
           lambda_q1, lambda_k1, lambda_q2, lambda_k2, subln_g, w_out, ln1_g, ln1_b,
           peer_wq, peer_k1, peer_k2, peer_u, peer_v, ln2_g, ln2_b, _flags=None, _trace=False):
    f = np.float32
    A = lambda a: np.ascontiguousarray(np.asarray(a))
    flags = _flags or {}
    key = tuple(sorted(flags.items()))
    if key not in _CACHE:
        _CACHE[key] = build_program(**flags)
    nc = _CACHE[key]

    x_prompt = A(x_prompt); x_sample = A(x_sample)
    npool = flags.get("npool", NPOOL); nexp = flags.get("nexp", 16384)
    ck = A(np.asarray(cache_k)[0][:npool].reshape(npool, 128, 8, 128).transpose(2, 0, 1, 3).reshape(8, npool, 16384))
    cv = A(np.asarray(cache_v)[0][:npool].transpose(2, 0, 1, 3).reshape(8, npool, 16384))
    w_in0 = A(np.asarray(w_in)[0]); w_out0 = A(np.asarray(w_out)[0]); wq0 = A(np.asarray(peer_wq)[0])
    chan = np.stack([np.asarray(conv_w)[0][0], np.asarray(conv_w)[0][1], np.asarray(conv_w)[0][2], np.asarray(conv_w)[0][3],
                     np.asarray(conv_b)[0], np.asarray(lru_ba)[0], np.asarray(lru_bx)[0], np.asarray(lru_lambda)[0]], axis=-1)
    chan = A(chan.reshape(8, 128, 8).transpose(1, 0, 2))
    lwa = A(np.asarray(lru_wa)[0].transpose(1, 0, 2))
    lwx = A(np.asarray(lru_wx)[0].transpose(1, 0, 2))
    lamv = A(np.broadcast_to(np.stack([np.asarray(lambda_q1)[0], np.asarray(lambda_k1)[0], np.asarray(lambda_q2)[0], np.asarray(lambda_k2)[0]])[None], (128, 4, 64)))
    subg = A(np.broadcast_to(np.asarray(subln_g)[0][None], (128, 128)))
    ln = A(np.broadcast_to(np.stack([np.asarray(ln1_g)[0], np.asarray(ln1_b)[0], np.asarray(ln2_g)[0], np.asarray(ln2_b)[0]])[None], (128, 4, D)))
    k1T = A(np.asarray(peer_k1)[0].transpose(2, 0, 1))
    k2T = A(np.asarray(peer_k2)[0].transpose(2, 0, 1))
    pu = A(np.asarray(peer_u)[0][:nexp]); pv = A(np.asarray(peer_v)[0][:nexp])
    ident = np.eye(128, dtype=f)
    cmask = np.where(np.arange(128)[None, :] <= np.arange(128)[:, None], 0.0, 8.0 * NEG).astype(f)
    sel4 = np.zeros((NS, NS, 65), f)
    for s in range(NS):
        sel4[s, s, :] = 1.0
    sel4 = sel4.reshape(NS, NS * 65)
    ones65 = np.ones((65, 65), f)
    iota16 = A(np.broadcast_to(np.arange(16, dtype=f)[None, None, :], (128, 16, 16)).reshape(128, 256))
    cs, sn = _rope_tables(np.array([8192]))
    rsam = A(np.broadcast_to(np.concatenate([cs[0], -sn[0], sn[0]])[None], (NS, 96)))
    sc0 = np.asarray(state_conv)[0]
    sh0 = np.asarray(state_h)[0]
    pt = np.asarray(page_table)

    in_maps = []
    for j in range(NCORES):
        b, hf = j // 2, j % 2
        xs = x_prompt[b]
        xT = np.zeros((D, NSLOT), f)
        if hf == 1:
            xT[:, :] = xs.T
        else:
            xT[:, TOWN:] = xs[0:TOWN].T
        own = xs[hf * TOWN:(hf + 1) * TOWN]
        smp = x_sample[NS * j:NS * (j + 1), 0, :]
        pos = np.concatenate([np.arange(TOWN), hf * TOWN + np.arange(TOWN)])
        c_, s_ = _rope_tables(pos)
        rcos = A(c_.reshape(16, 128, 32).transpose(1, 0, 2))
        rsin = A(np.stack([-s_, s_], axis=1).reshape(16, 128, 2, 32).transpose(1, 0, 2, 3))
        pbrow = np.zeros((1, NSLOT), f)
        if hf == 0:
            pbrow[0, 0:TOWN] = 8.0 * NEG
        pbv = np.zeros((128, 2), f)
        pbv[:, 0] = 0.0 if hf == 1 else NEG
        pbv[:, 1] = 1.0 if hf == 1 else 0.0
        scj = sc0[NS * j:NS * (j + 1)]
        sconv = A(scj.reshape(NS, 3, 8, 128).transpose(3, 2, 1, 0))
        shj = A(sh0[NS * j:NS * (j + 1)].reshape(NS, 8, 128).transpose(2, 1, 0))
        in_maps.append({
            "xT": xT, "xsT": A(smp.T), "xtok": A(np.concatenate([own, smp], 0)),
            "w_in": w_in0, "w_out": w_out0, "wq": wq0, "chan": chan, "lwa": lwa, "lwx": lwx, "lamv": lamv, "subg": subg,
            "rcos": rcos, "rsin": rsin, "rsam": rsam, "pb": pbv, "pbrow": pbrow, "ident": ident, "cmask": cmask, "sel4": sel4, "ones65": ones65,
            "iota16": iota16, "cache_k": ck, "cache_v": cv, "ptT": A(pt[NS * j:NS * (j + 1)].T.astype(np.int32)),
            "sconv": sconv, "sh": shj, "ln": ln, "k1T": k1T, "k2T": k2T, "peer_u": pu, "peer_v": pv,
        })
    res = run_bass_kernel_spmd(nc, in_maps, core_ids=list(range(NCORES)), **({"trace": True} if _trace else {}))
    R = res.results
    yp = np.zeros((4, 2048, D), f); ys = np.zeros((32, 1, D), f)
    kp = np.zeros((1, 4, 2048, 16, 64), f); vp = np.zeros((1, 4, 2048, 8, 128), f)
    cp_ = np.zeros((1, 4, 3, 1024), f); hp = np.zeros((1, 4, 1024), f)
    ksn = np.zeros((1, 32, 1, 16, 64), f); vsn = np.zeros((1, 32, 1, 8, 128), f)
    csn = np.zeros((1, 32, 3, 1024), f); hsn = np.zeros((1, 32, 1024), f)
    for j in range(NCORES):
        b, hf = j // 2, j % 2
        r = R[j]
        yp[b, hf * TOWN:(hf + 1) * TOWN] = r["y"][0:TOWN]
        ys[NS * j:NS * (j + 1), 0] = r["y"][TOWN:TTOK]
        kp[0, b, hf * TOWN:(hf + 1) * TOWN] = r["k_o"].reshape(TOWN, 16, 64)
        vp[0, b, hf * TOWN:(hf + 1) * TOWN] = r["v_o"].reshape(TOWN, 8, 128)
        if hf == 1:
            cp_[0, b] = r["conv_o"].transpose(2, 1, 0).reshape(3, 1024)
            hp[0, b] = r["h_o"].transpose(1, 0).reshape(1024)
        ksn[0, NS * j:NS * (j + 1), 0] = r["ks_o"].reshape(NS, 16, 64)
        vsn[0, NS * j:NS * (j + 1), 0] = r["vs_o"].reshape(NS, 8, 128)
        csn[0, NS * j:NS * (j + 1)] = r["convs_o"].transpose(3, 2, 1, 0).reshape(NS, 3, 1024)
        hsn[0, NS * j:NS * (j + 1)] = r["hs_o"].transpose(2, 1, 0).reshape(NS, 1024)
    if flags.get("dbg"):
        kernel._dbg = [(R[j]["mix_o"], R[j]["x1_o"]) for j in range(NCORES)]
    if _trace:
        kernel._exec_ns = res.exec_time_ns
    return (yp, ys, kp, vp, cp_, hp, ksn, vsn, csn, hsn)
```

```python
import contextlib
import numpy as np
import concourse.bass as bass
import concourse.mybir as mybir
from concourse.bass_utils import run_bass_kernel_spmd

F32 = mybir.dt.float32
BF16 = mybir.dt.bfloat16
I32 = mybir.dt.int32
U32 = mybir.dt.uint32
AF = mybir.ActivationFunctionType
ALU = mybir.AluOpType
AX = mybir.AxisListType

NCORES = 8
D = 2048
TOWN = 1024
NSLOT = 2048
NS = 4
TTOK = TOWN + NS
NPOOL = 2560
NPAGES = 64
EPS = 1e-5
NEG = -30000.0
ALPHA = 2.0 ** 0.25
LAM_INIT = 0.2

ENGS = ("sync", "scalar", "vector", "gpsimd", "tensor")


class Buf:
    __slots__ = ("name", "w", "r")

    def __init__(self, name):
        self.name = name
        self.w = None
        self.r = []


class Prog:
    def __init__(self, nc, n_dma_sems=12):
        self.nc = nc
        self.lists = {e: [] for e in ENGS}
        self.esem = {e: nc.alloc_semaphore(name="es_" + e) for e in ENGS}
        self.ecnt = {e: 0 for e in ENGS}
        self.dsem, self.dcnt, self.dpos = {}, {}, {}
        for q in ("sync", "scalar", "gpsimd"):
            self.dsem[q] = [nc.alloc_semaphore(name=f"ds_{q}{i}") for i in range(n_dma_sems)]
            self.dcnt[q] = [0] * n_dma_sems
            self.dpos[q] = 0
        self.waited = {e: {} for e in ENGS}
        self.final_events = []

    def _need(self, eng, ev):
        if ev is None:
            return
        sem, val, src = ev
        if src == eng and eng == "tensor":
            return
        key = id(sem)
        if self.waited[eng].get(key, 0) >= val:
            return
        self.waited[eng][key] = val
        self.lists[eng].append(("wait", sem, val))

    def _deps(self, eng, reads, writes):
        for b in reads:
            self._need(eng, b.w)
        for b in writes:
            self._need(eng, b.w)
            for ev in b.r:
                self._need(eng, ev)

    def _mark(self, ev, reads, writes):
        for b in reads:
            b.r.append(ev)
            if len(b.r) > 48:
                last = {}
                for e2 in b.r:
                    k = id(e2[0])
                    if k not in last or last[k][1] < e2[1]:
                        last[k] = e2
                b.r = list(last.values())
        for b in writes:
            b.w = ev
            b.r = []

    def op(self, eng, fn, reads=(), writes=()):
        self._deps(eng, reads, writes)
        self.ecnt[eng] += 1
        ev = (self.esem[eng], self.ecnt[eng], eng)
        self.lists[eng].append(("op", fn, self.esem[eng], 1))
        self._mark(ev, reads, writes)
        return ev

    def dma(self, q, fn, reads=(), writes=(), final=False):
        i = self.dpos[q]
        self.dpos[q] = (i + 1) % len(self.dsem[q])
        sem = self.dsem[q][i]
        if self.dcnt[q][i] > 0:
            self._need(q, (sem, self.dcnt[q][i], None))
        self._deps(q, reads, writes)
        self.dcnt[q][i] += 16
        ev = (sem, self.dcnt[q][i], None)
        self.lists[q].append(("op", fn, sem, 16))
        self._mark(ev, reads, writes)
        if final:
            self.final_events.append(ev)
        return ev

    def all_events(self):
        evs = [(self.esem[e], self.ecnt[e], e) for e in ENGS if self.ecnt[e] > 0]
        for q in self.dsem:
            for i, s in enumerate(self.dsem[q]):
                if self.dcnt[q][i] > 0:
                    evs.append((s, self.dcnt[q][i], None))
        return evs

    def barrier(self):
        evs = self.all_events()
        for e in ENGS:
            for ev in evs:
                if ev[2] == e:
                    continue
                self._need(e, ev)

    def flush(self, last=False):
        if last:
            for ev in self.all_events():
                self._need("sync", ev)
        lists = self.lists
        self.lists = {e: [] for e in ENGS}
        nc = self.nc

        def run(engobj, items):
            for it in items:
                if it[0] == "wait":
                    engobj.wait_ge(it[1], it[2])
                else:
                    it[1](engobj).then_inc(it[2], it[3])

        with nc.Block() as block:
            @block.sync
            def _(e):
                run(e, lists["sync"])

            @block.scalar
            def _(e):
                run(e, lists["scalar"])

            @block.vector
            def _(e):
                run(e, lists["vector"])

            @block.gpsimd
            def _(e):
                run(e, lists["gpsimd"])

            @block.tensor
            def _(e):
                run(e, lists["tensor"])


def build_program(do_rec=True, do_att=True, do_dec=True, do_p2=True, do_peer=True, dbg=False, npool=NPOOL, nexp=16384):
    nc = bass.Bass("TRN2", target_bir_lowering=False)

    def din(name, shape, dt=F32):
        return nc.dram_tensor(name, list(shape), dt, kind="ExternalInput").ap()

    def dout(name, shape, dt=F32):
        return nc.dram_tensor(name, list(shape), dt, kind="ExternalOutput").ap()

    xT_d = din("xT", [D, NSLOT])
    xsT_d = din("xsT", [D, NS])
    xtok_d = din("xtok", [TTOK, D])
    w_in_d = din("w_in", [D, 5120])
    w_out_d = din("w_out", [D, D])
    wq_d = din("wq", [D, D])
    chan_d = din("chan", [128, 8, 8])
    lwa_d = din("lwa", [128, 8, 128])
    lwx_d = din("lwx", [128, 8, 128])
    lamv_d = din("lamv", [128, 4, 64])
    subg_d = din("subg", [128, 128])
    rcos_d = din("rcos", [128, 16, 32])
    rsin_d = din("rsin", [128, 16, 2, 32])
    rsam_d = din("rsam", [NS, 96])
    pb_d = din("pb", [128, 2])
    pbrow_d = din("pbrow", [1, NSLOT])
    ident_d = din("ident", [128, 128])
    cmask_d = din("cmask", [128, 128])
    sel4_d = din("sel4", [NS, NS * 65])
    ones_d = din("ones65", [65, 65])
    iota_d = din("iota16", [128, 256])
    ck_d = din("cache_k", [8, npool, 16384])
    cv_d = din("cache_v", [8, npool, 16384])
    pt_d = din("ptT", [NPAGES, NS], I32)
    sconv_d = din("sconv", [128, 8, 3, NS])
    sh_d = din("sh", [128, 8, NS])
    ln_d = din("ln", [128, 4, D])
    k1T_d = din("k1T", [128, 8, 128])
    k2T_d = din("k2T", [128, 8, 128])
    pu_d = din("peer_u", [nexp, D])
    pv_d = din("peer_v", [nexp, D])

    y_o = dout("y", [TTOK, D])
    k_o = dout("k_o", [TOWN, 1024])
    v_o = dout("v_o", [TOWN, 1024])
    conv_o = dout("conv_o", [128, 8, 3])
    h_o = dout("h_o", [128, 8])
    ks_o = dout("ks_o", [NS, 1024])
    vs_o = dout("vs_o", [NS, 1024])
    convs_o = dout("convs_o", [128, 8, 3, NS])
    hs_o = dout("hs_o", [128, 8, NS])
    if dbg:
        mix_o = dout("mix_o", [128, 16, TTOK])
        x1_o = dout("x1_o", [TTOK, D])

    P = Prog(nc)

    def tt(eng, out, in0, in1, op, r, w):
        P.op(eng, lambda e: e.tensor_tensor(out=out, in0=in0, in1=in1, op=op), r, w)

    def ts(eng, out, in0, s1, s2, op0, op1, r, w):
        if op1 is None:
            P.op(eng, lambda e: e.tensor_scalar(out=out, in0=in0, scalar1=s1, scalar2=None, op0=op0), r, w)
        else:
            P.op(eng, lambda e: e.tensor_scalar(out=out, in0=in0, scalar1=s1, scalar2=s2, op0=op0, op1=op1), r, w)

    def stt(out, in0, scalar, in1, op0, op1, r, w, accum=None):
        if accum is None:
            P.op("vector", lambda e: e.scalar_tensor_tensor(out=out, in0=in0, scalar=scalar, in1=in1, op0=op0, op1=op1), r, w)
        else:
            P.op("vector", lambda e: e.scalar_tensor_tensor(out=out, in0=in0, scalar=scalar, in1=in1, op0=op0, op1=op1,
                                                            accum_out=accum), r, w)

    def act(out, in_, func, r, w, bias=None, scale=None, accum=None):
        kw = {}
        if bias is not None:
            kw["bias"] = bias
        if scale is not None:
            kw["scale"] = scale
        if accum is not None:
            kw["accum_out"] = accum
        P.op("scalar", lambda e: e.activation(out=out, in_=in_, func=func, **kw), r, w)

    def cp(eng, out, in_, r, w):
        if eng == "scalar":
            P.op("scalar", lambda e: e.copy(out=out, in_=in_), r, w)
        else:
            P.op(eng, lambda e: e.tensor_copy(out=out, in_=in_), r, w)

    def mm(out, lhsT, rhs, start, stop, r, w):
        P.op("tensor", lambda e: e.matmul(out, lhsT=lhsT, rhs=rhs, start=start, stop=stop), r, w)

    def tr(out, in_, ident, r, w):
        P.op("tensor", lambda e: e.transpose(out=out, in_=in_, identity=ident), r, w)

    def dma(q, out, in_, r, w, final=False):
        P.dma(q, lambda e: e.dma_start(out=out, in_=in_), r, w, final=final)

    def idma(out, in_, idx, r, w, eoff=0):
        P.dma("gpsimd", lambda e: e.indirect_dma_start(out=out, out_offset=None, in_=in_,
                                                      in_offset=bass.IndirectOffsetOnAxis(ap=idx, axis=0), element_offset=eoff), r, w)

    def memset(eng, ap, val, w):
        P.op(eng, lambda e: e.memset(ap, val), (), w)

    def red(eng, out, in_, op, r, w):
        P.op(eng, lambda e: e.tensor_reduce(out=out, in_=in_, axis=AX.X, op=op), r, w)

    def recip(out, in_, r, w):
        P.op("vector", lambda e: e.reciprocal(out=out, in_=in_), r, w)

    es_all = contextlib.ExitStack()
    with es_all:
        def T(es, name, shape, dt):
            return es.enter_context(nc.sbuf_tensor("s_" + name, list(shape), dt))

        def PSt(es, name, shape, dt):
            return es.enter_context(nc.psum_tensor("p_" + name, list(shape), dt))

        psS = PSt(es_all, "psS", [128, 2048], F32); bS = Buf("psS")
        psT = PSt(es_all, "psT", [128, 1024], BF16); bT = Buf("psT")
        psA = PSt(es_all, "psA", [128, 512], F32); bA = Buf("psA")
        psB = PSt(es_all, "psB", [128, 512], F32); bB = Buf("psB")
        psC = PSt(es_all, "psC", [128, 512], F32); bC = Buf("psC")
        psrot = [(psA, bA), (psB, bB), (psC, bC)]

        mix_dt = nc.dram_tensor("mix_d", [16, 128, TTOK], BF16)
        mix_d = mix_dt.ap()
        x1_dt = nc.dram_tensor("x1_d", [TTOK, D], F32)
        x1_d = x1_dt.ap()
        bmix = [Buf(f"mix{i}") for i in range(16)]
        pub_d = nc.dram_tensor("pub_d", [nexp, D], BF16).ap()
        pvb_d = nc.dram_tensor("pvb_d", [nexp, D], BF16).ap()
        mstg = T(es_all, "mstg", [128, TTOK], BF16); bmstg = Buf("mstg")
        identf = T(es_all, "identf", [128, 128], F32); bidf = Buf("identf")
        identb = T(es_all, "identb", [128, 128], BF16); bidb = Buf("identb")
        pbt = T(es_all, "pbt", [128, 2], F32); bpb = Buf("pb")

        dma("sync", identf[:], ident_d, (), [bidf])
        dma("gpsimd", identb[:], ident_d, (), [bidb])
        dma("sync", pbt[:], pb_d, (), [bpb])

        es1 = contextlib.ExitStack()
        with es1:
            xT = T(es1, "xTb", [128, 16, NSLOT], BF16); bxT = [Buf(f"xT{k}") for k in range(16)]
            xsT = T(es1, "xsTb", [128, 16, NS], BF16); bxs = Buf("xsT")
            chan = T(es1, "chan", [128, 8, 8], F32); bchan = Buf("chan")
            nsp = T(es1, "nsp", [128, 8, 2], F32); bnsp = Buf("nsp")
            for k in range(16):
                dma("gpsimd", xT[:, k, :], xT_d[k * 128:(k + 1) * 128, :], (), [bxT[k]])
            dma("gpsimd", xsT[:], xsT_d.rearrange("(k p) s -> p k s", p=128), (), [bxs])
            CR = 512
            conv_list = [(tab, r0) for tab in range(2) for r0 in range(0, nexp, CR)] if (do_p2 and do_peer) else []
            conv_pos = [0]

            def conv_steps(n):
                for _ in range(n):
                    if conv_pos[0] >= len(conv_list):
                        return
                    tab, r0 = conv_list[conv_pos[0]]
                    conv_pos[0] += 1
                    src = (pu_d if tab == 0 else pv_d)[r0:r0 + CR, :]
                    dst = (pub_d if tab == 0 else pvb_d)[r0:r0 + CR, :]
                    dma("gpsimd", dst, src, (), ())

            dma("sync", chan[:], chan_d, (), [bchan])
            tmpl = T(es1, "tmpl", [128, 8], F32); btl = Buf("tmpl")
            act(tmpl[:], chan[:, :, 7], AF.Exp, [bchan], [btl], scale=-1.0)
            act(tmpl[:], tmpl[:], AF.Ln, [btl], [btl], bias=1.0)
            ts("vector", nsp[:, :, 0], tmpl[:], -8.0, None, ALU.mult, None, [btl], [bnsp])
            ts("vector", nsp[:, :, 1], tmpl[:], -16.0, None, ALU.mult, None, [btl], [bnsp])

            if do_rec:
                esr = contextlib.ExitStack()
                with esr:
                    wrx = T(esr, "wrx", [128, 16, 128], BF16); bwrx = Buf("wrx")
                    wrg = T(esr, "wrg", [128, 16, 128], BF16); bwrg = Buf("wrg")
                    wab = T(esr, "wab", [128, 128], BF16); bwab = Buf("wab")
                    wxb = T(esr, "wxb", [128, 128], BF16); bwxb = Buf("wxb")
                    rx = T(esr, "rx", [128, 3 + NSLOT], F32); brx = Buf("rx")
                    gg = T(esr, "gg", [128, TOWN], F32); bgg = Buf("gg")
                    cvt = T(esr, "cvt", [128, NSLOT], F32); bcv = Buf("cv")
                    cvb = T(esr, "cvb", [128, NSLOT], BF16); bcvb = Buf("cvb")
                    rgt = T(esr, "rgt", [128, NSLOT], F32); brgt = Buf("rgt")
                    igt = T(esr, "igt", [128, NSLOT], F32); bigt = Buf("igt")
                    at = T(esr, "at", [128, NSLOT], F32); bat = Buf("at")
                    a2t = T(esr, "a2t", [128, NSLOT], F32); ba2 = Buf("a2t")
                    hpre = T(esr, "hpre", [128, TOWN], F32); bhp = Buf("hpre")
                    hown = T(esr, "hown", [128, TOWN], F32); bho = Buf("hown")
                    h0 = T(esr, "h0", [128, 1], F32); bh0 = Buf("h0")
                    cst = T(esr, "cst", [128, 8, 3], F32); bcst = Buf("cst")
                    hst = T(esr, "hst", [128, 8], F32); bhst = Buf("hst")
                    sconv = T(esr, "sconv", [128, 8, 3, NS], F32); bsc = Buf("sconv")
                    sht = T(esr, "sht", [128, 8, NS], F32); bsh = Buf("sht")
                    cvsst = T(esr, "cvsst", [128, 8, 3, NS], F32); bcvs = Buf("cvsst")
                    hsst = T(esr, "hsst", [128, 8, NS], F32); bhss = Buf("hsst")
                    sm = T(esr, "sm", [128, 12, NS], F32); bsm = Buf("sm")
                    smb = T(esr, "smb", [128, NS], BF16); bsmb = Buf("smb")
                    dma("sync", sconv[:], sconv_d, (), [bsc])
                    dma("sync", sht[:], sh_d, (), [bsh])
                    memset("vector", rx[:, 0:3], 0.0, [brx])
                    for r in range(8):
                        dma("gpsimd", wrx[:], w_in_d[:, r * 128:(r + 1) * 128].rearrange("(k p) c -> p k c", p=128), (), [bwrx])
                        dma("gpsimd", wrg[:], w_in_d[:, 1024 + r * 128:1024 + (r + 1) * 128].rearrange("(k p) c -> p k c", p=128), (), [bwrg])
                        dma("gpsimd", wab[:], lwa_d[:, r, :], (), [bwab])
                        dma("gpsimd", wxb[:], lwx_d[:, r, :], (), [bwxb])
                        conv_steps(8)
                        for blk in range(4):
                            ps, bp = psrot[blk % 3]
                            for k in range(16):
                                mm(ps[:, 0:512], wrx[:, k, :], xT[:, k, blk * 512:(blk + 1) * 512], k == 0, k == 15, [bwrx, bxT[k]], [bp])
                            cp("scalar", rx[:, 3 + blk * 512:3 + (blk + 1) * 512], ps[:, 0:512], [bp], [brx])
                        for blk in range(2, 4):
                            ps, bp = psrot[(blk + 1) % 3]
                            for k in range(16):
                                mm(ps[:, 0:512], wrg[:, k, :], xT[:, k, blk * 512:(blk + 1) * 512], k == 0, k == 15, [bwrg, bxT[k]], [bp])
                            act(gg[:, (blk - 2) * 512:(blk - 1) * 512], ps[:, 0:512], AF.Gelu, [bp], [bgg])
                        ts("vector", cvt[:], rx[:, 0:NSLOT], chan[:, r, 0:1], chan[:, r, 4:5], ALU.mult, ALU.add, [brx, bchan], [bcv])
                        for j in range(1, 4):
                            stt(cvt[:], rx[:, j:j + NSLOT], chan[:, r, j:j + 1], cvt[:], ALU.mult, ALU.add, [brx, bchan, bcv], [bcv])
                        cp("gpsimd", cvb[:], cvt[:], [bcv], [bcvb])
                        for blk in range(4):
                            ps, bp = psrot[blk % 3]
                            mm(ps[:, 0:512], wab[:], cvb[:, blk * 512:(blk + 1) * 512], True, True, [bwab, bcvb], [bp])
                            act(rgt[:, blk * 512:(blk + 1) * 512], ps[:, 0:512], AF.Sigmoid, [bp, bchan], [brgt], bias=chan[:, r, 5:6])
                            ps2, bp2 = psrot[(blk + 1) % 3]
                            mm(ps2[:, 0:512], wxb[:], cvb[:, blk * 512:(blk + 1) * 512], True, True, [bwxb, bcvb], [bp2])
                            act(igt[:, blk * 512:(blk + 1) * 512], ps2[:, 0:512], AF.Sigmoid, [bp2, bchan], [bigt], bias=chan[:, r, 6:7])
                        act(at[:], rgt[:], AF.Exp, [brgt, bnsp], [bat], scale=nsp[:, r, 0:1])
                        act(a2t[:], rgt[:], AF.Exp, [brgt, bnsp], [ba2], scale=nsp[:, r, 1:2])
                        ts("vector", a2t[:], a2t[:], -1.0, 1.0, ALU.mult, ALU.add, [ba2], [ba2])
                        ts("gpsimd", a2t[:], a2t[:], 1e-30, None, ALU.max, None, [ba2], [ba2])
                        act(a2t[:], a2t[:], AF.Sqrt, [ba2], [ba2])
                        tt("gpsimd", igt[:], igt[:], cvt[:], ALU.mult, [bigt, bcv], [bigt])
                        tt("vector", a2t[:], a2t[:], igt[:], ALU.mult, [ba2, bigt], [ba2])
                        P.op("vector", lambda e: e.tensor_tensor_scan(out=hpre[:], data0=at[:, 0:TOWN], data1=a2t[:, 0:TOWN], initial=0.0,
                                                                    op0=ALU.mult, op1=ALU.add), [bat, ba2], [bhp])
                        ts("vector", h0[:], hpre[:, TOWN - 1:TOWN], pbt[:, 1:2], None, ALU.mult, None, [bhp, bpb], [bh0])
                        P.op("vector", lambda e: e.tensor_tensor_scan(out=hown[:], data0=at[:, TOWN:NSLOT], data1=a2t[:, TOWN:NSLOT], initial=h0[:, 0:1],
                                                                    op0=ALU.mult, op1=ALU.add), [bat, ba2, bh0], [bho])
                        tt("gpsimd", mstg[:, 0:TOWN], hown[:], gg[:], ALU.mult, [bho, bgg], [bmstg])
                        cp("gpsimd", cst[:, r, :], rx[:, NSLOT:NSLOT + 3], [brx], [bcst])
                        cp("gpsimd", hst[:, r:r + 1], hown[:, TOWN - 1:TOWN], [bho], [bhst])
                        ps, bp = psrot[0]
                        for k in range(16):
                            mm(ps[:, 0:NS], wrx[:, k, :], xsT[:, k, :], k == 0, k == 15, [bwrx, bxs], [bp])
                        cp("vector", sm[:, 0, :], ps[:, 0:NS], [bp], [bsm])
                        ps, bp = psrot[1]
                        for k in range(16):
                            mm(ps[:, 0:NS], wrg[:, k, :], xsT[:, k, :], k == 0, k == 15, [bwrg, bxs], [bp])
                        act(sm[:, 1, :], ps[:, 0:NS], AF.Gelu, [bp], [bsm])
                        ts("vector", sm[:, 2, :], sconv[:, r, 0, :], chan[:, r, 0:1], chan[:, r, 4:5], ALU.mult, ALU.add, [bsc, bchan, bsm], [bsm])
                        stt(sm[:, 2, :], sconv[:, r, 1, :], chan[:, r, 1:2], sm[:, 2, :], ALU.mult, ALU.add, [bsc, bchan, bsm], [bsm])
                        stt(sm[:, 2, :], sconv[:, r, 2, :], chan[:, r, 2:3], sm[:, 2, :], ALU.mult, ALU.add, [bsc, bchan, bsm], [bsm])
                        stt(sm[:, 2, :], sm[:, 0, :], chan[:, r, 3:4], sm[:, 2, :], ALU.mult, ALU.add, [bchan, bsm], [bsm])
                        cp("vector", smb[:], sm[:, 2, :], [bsm], [bsmb])
                        ps, bp = psrot[2]
                        mm(ps[:, 0:NS], wab[:], smb[:], True, True, [bwab, bsmb], [bp])
                        act(sm[:, 3, :], ps[:, 0:NS], AF.Sigmoid, [bp, bchan], [bsm], bias=chan[:, r, 5:6])
                        ps, bp = psrot[0]
                        mm(ps[:, 0:NS], wxb[:], smb[:], True, True, [bwxb, bsmb], [bp])
                        act(sm[:, 4, :], ps[:, 0:NS], AF.Sigmoid, [bp, bchan], [bsm], bias=chan[:, r, 6:7])
                        act(sm[:, 5, :], sm[:, 3, :], AF.Exp, [bsm, bnsp], [bsm], scale=nsp[:, r, 0:1])
                        act(sm[:, 6, :], sm[:, 3, :], AF.Exp, [bsm, bnsp], [bsm], scale=nsp[:, r, 1:2])
                        ts("vector", sm[:, 6, :], sm[:, 6, :], -1.0, 1.0, ALU.mult, ALU.add, [bsm], [bsm])
                        ts("vector", sm[:, 6, :], sm[:, 6, :], 1e-30, None, ALU.max, None, [bsm], [bsm])
                        act(sm[:, 6, :], sm[:, 6, :], AF.Sqrt, [bsm], [bsm])
                        tt("vector", sm[:, 6, :], sm[:, 6, :], sm[:, 4, :], ALU.mult, [bsm], [bsm])
                        tt("vector", sm[:, 6, :], sm[:, 6, :], sm[:, 2, :], ALU.mult, [bsm], [bsm])
                        tt("vector", sm[:, 7, :], sm[:, 5, :], sht[:, r, :], ALU.mult, [bsm, bsh], [bsm])
                        tt("vector", sm[:, 7, :], sm[:, 7, :], sm[:, 6, :], ALU.add, [bsm], [bsm])
                        tt("vector", mstg[:, TOWN:TTOK], sm[:, 7, :], sm[:, 1, :], ALU.mult, [bsm], [bmstg])
                        dma("sync", mix_d[r], mstg[:], [bmstg], [bmix[r]])
                        cp("vector", hsst[:, r, :], sm[:, 7, :], [bsm], [bhss])
                        cp("vector", cvsst[:, r, 0, :], sconv[:, r, 1, :], [bsc], [bcvs])
                        cp("vector", cvsst[:, r, 1, :], sconv[:, r, 2, :], [bsc], [bcvs])
                        cp("vector", cvsst[:, r, 2, :], sm[:, 0, :], [bsm], [bcvs])
                    dma("sync", conv_o, cst[:], [bcst], (), final=True)
                    dma("sync", h_o, hst[:], [bhst], (), final=True)
                    dma("sync", convs_o, cvsst[:], [bcvs], (), final=True)
                    dma("sync", hs_o, hsst[:], [bhss], (), final=True)
                    P.barrier()
                    P.flush()

            if do_att:
                esa = contextlib.ExitStack()
                with esa:
                    wq3 = T(esa, "wq3", [128, 16, 384], BF16); bw3 = Buf("wq3")
                    rcos = T(esa, "rcos", [128, 16, 32], F32); brc = Buf("rcos")
                    rsin = T(esa, "rsin", [128, 16, 2, 32], F32); brs = Buf("rsin")
                    rsam = T(esa, "rsam", [NS, 96], F32); brsm = Buf("rsam")
                    cmask = T(esa, "cmask", [128, 128], F32); bcm = Buf("cmask")
                    lamv = T(esa, "lamv", [128, 4, 64], F32); blv = Buf("lamv")
                    lam = T(esa, "lam", [128, 4], F32); blam = Buf("lam")
                    gsc = T(esa, "gsc", [128, 128], F32); bgsc = Buf("gsc")
                    t1 = T(esa, "t1", [128, 256], F32); bt1 = Buf("t1")
                    t2 = T(esa, "t2", [128, 256], F32); bt2 = Buf("t2")
                    kb16 = T(esa, "kb16", [128, 16, 128], BF16); bkb = Buf("kb16")
                    qb16 = T(esa, "qb16", [128, 8, 128], BF16); bqb = Buf("qb16")
                    vb = T(esa, "vb", [128, 16, 128], BF16); bvb = Buf("vb")
                    kst = T(esa, "kst", [128, 8, 128], F32); bkst = Buf("kst")
                    vst = T(esa, "vst", [128, 8, 128], F32); bvst = Buf("vst")
                    kT = [T(esa, f"kT{m}", [65, NSLOT], BF16) for m in range(2)]; bkT = [Buf("kT0"), Buf("kT1")]
                    qT = [T(esa, f"qT{m}", [65, TOWN], BF16) for m in range(2)]; bqT = [Buf("qT0"), Buf("qT1")]
                    Pp = [T(esa, f"Pp{i}", [128, 512], BF16) for i in range(3)]; bPp = [Buf(f"Pp{i}") for i in range(3)]
                    PTp = [T(esa, f"PTp{i}", [128, 4, 128], BF16) for i in range(3)]; bPTp = [Buf(f"PTp{i}") for i in range(3)]
                    cmaskb = T(esa, "cmaskb", [128, 128], BF16); bcmb = Buf("cmaskb")
                    nrm = T(esa, "nrm", [128, 16, 4], F32); bnrm = Buf("nrm")
                    nbq = T(esa, "nbq", [128, 8, 2], F32); bnbq = Buf("nbq")
                    kmx = T(esa, "kmx", [128, 4], F32); bkmx = Buf("kmx")
                    kmx2 = T(esa, "kmx2", [2, 132], F32); bkmx2 = Buf("kmx2")
                    zc = T(esa, "zc", [128, 2, 4], F32); bzc = Buf("zc")
                    bSp = [Buf(f"psS{i}") for i in range(4)]
                    bTh1 = Buf("psT_h1")
                    acnt = [0, 0, 0]
                    st = T(esa, "st", [128, 16], F32); bst = Buf("st")
                    att = T(esa, "att", [128, 128], F32); batt = Buf("att")
                    att2 = T(esa, "att2", [128, 128], F32); batt2 = Buf("att2")
                    attb = T(esa, "attb", [128, 128], BF16); battb = Buf("attb")
                    junk = T(esa, "junk", [128, 128], F32); bjunk = Buf("junk")
                    ksst = T(esa, "ksst", [NS, 2, 128], F32); bkss = Buf("ksst")
                    vsst = T(esa, "vsst", [NS, 2, 128], F32); bvss = Buf("vsst")
                    qs = T(esa, "qs", [NS, 8, 128], F32); bqs = Buf("qs")
                    msts = T(esa, "msts", [128, 8, NS], BF16); bmsts = Buf("msts")
                    ts1 = T(esa, "ts1", [NS, 256], F32); bts1 = Buf("ts1")
                    ts2 = T(esa, "ts2", [NS, 256], F32); bts2 = Buf("ts2")
                    ptT = T(esa, "ptT", [NPAGES, NS], I32); bpt = Buf("ptT")
                    ptf = T(esa, "ptf", [NPAGES, NS], F32); bptf = Buf("ptf")
                    ptc = T(esa, "ptc", [NPAGES, NS, 8], I32); bptc = Buf("ptc")
                    sel4 = T(esa, "sel4", [NS, NS * 65], F32); bsel = Buf("sel4")
                    ones65 = T(esa, "ones65", [65, 65], F32); bon = Buf("ones65")
                    CH = 16
                    NCH = 128 // CH
                    NKT = 4
                    Kt = [T(esa, f"Kt{i}", [65, CH * 128], F32) for i in range(NKT)]; bKt = [Buf(f"Kt{i}") for i in range(NKT)]
                    qbc = T(esa, "qbc", [65, NS, 128], F32); bqbc = Buf("qbc")
                    knew = T(esa, "knew", [65, NS, 128], F32); bknew = Buf("knew")
                    sc = T(esa, "sc", [65, NS, 128, 2], F32); bsc2 = Buf("sc")
                    ee = sc; bee = bsc2
                    wv = T(esa, "wv", [65, 128], F32); bwv = Buf("wv")
                    Bz = T(esa, "Bz", [65, NS, 128, 7], BF16); bBz = Buf("Bz")
                    Vb = [T(esa, f"Vb{i}", [65, CH * 128], BF16) for i in range(2)]; bVb = [Buf("Vb0"), Buf("Vb1")]
                    dsm = T(esa, "dsm", [65, 40], F32); bdsm = Buf("dsm")
                    dsm2 = T(esa, "dsm2", [8, 80], F32); bdsm2 = Buf("dsm2")
                    kcnt = [0]

                    dma("sync", rcos[:], rcos_d, (), [brc])
                    dma("sync", rsin[:], rsin_d, (), [brs])
                    dma("sync", rsam[:], rsam_d, (), [brsm])
                    dma("sync", cmask[:], cmask_d, (), [bcm])
                    dma("gpsimd", cmaskb[:], cmask_d, (), [bcmb])
                    for m in range(2):
                        dma("gpsimd", kT[m][64:65, :], pbrow_d, (), [bkT[m]])
                        memset("gpsimd", qT[m][64:65, :], 1.0, [bqT[m]])
                    memset("vector", nrm[:], 0.0, [bnrm])
                    dma("sync", lamv[:], lamv_d, (), [blv])
                    dma("sync", gsc[:], subg_d, (), [bgsc])
                    dma("sync", ptT[:], pt_d, (), [bpt])
                    dma("sync", sel4[:], sel4_d, (), [bsel])
                    dma("sync", ones65[:], ones_d, (), [bon])
                    tt("vector", t1[:, 0:64], lamv[:, 0, :], lamv[:, 1, :], ALU.mult, [blv], [bt1])
                    tt("vector", t1[:, 64:128], lamv[:, 2, :], lamv[:, 3, :], ALU.mult, [blv], [bt1])
                    red("vector", lam[:, 2:4], t1[:, 0:128].rearrange("p (a b) -> p a b", a=2), ALU.add, [bt1], [blam])
                    act(lam[:, 2:4], lam[:, 2:4], AF.Exp, [blam], [blam])
                    tt("vector", lam[:, 0:1], lam[:, 2:3], lam[:, 3:4], ALU.subtract, [blam], [blam])
                    ts("vector", lam[:, 0:1], lam[:, 0:1], LAM_INIT, None, ALU.add, None, [blam], [blam])
                    ts("vector", lam[:, 1:2], lam[:, 0:1], -1.0, None, ALU.mult, None, [blam], [blam])
                    ts("vector", gsc[:], gsc[:], 1.0 - LAM_INIT, None, ALU.mult, None, [bgsc], [bgsc])
                    cp("vector", ptf[:], ptT[:], [bpt], [bptf])
                    for c in range(8):
                        ts("vector", ptc[:, :, c], ptf[:], 8.0, float(c), ALU.mult, ALU.add, [bptf], [bptc])
                    for i in range(NKT):
                        memset("vector", Kt[i][:], 0.0, [bKt[i]])
                    memset("vector", Bz[:], 0.0, [bBz])
                    memset("vector", knew[:], 0.0, [bknew])

                    def rope(ps, G, cos_ap, sin0_ap, sin1_ap, np_, o1, o2, bo1, bo2, rdeps):
                        pv = ps[0:np_, 0:G * 64].rearrange("p (g h f) -> p g h f", g=G, h=2)
                        o1v = o1[0:np_, 0:G * 64].rearrange("p (g h f) -> p g h f", g=G, h=2)
                        o2v = o2[0:np_, 0:G * 64].rearrange("p (g h f) -> p g h f", g=G, h=2)
                        cb = cos_ap.unsqueeze(1).unsqueeze(1).to_broadcast([np_, G, 2, 32])
                        tt("vector", o1v, pv, cb, ALU.mult, rdeps, [bo1])
                        tt("vector", o2v[:, :, 0, :], pv[:, :, 1, :], sin0_ap.unsqueeze(1).to_broadcast([np_, G, 32]), ALU.mult, rdeps, [bo2])
                        tt("vector", o2v[:, :, 1, :], pv[:, :, 0, :], sin1_ap.unsqueeze(1).to_broadcast([np_, G, 32]), ALU.mult, rdeps, [bo2])
                        tt("vector", o1[0:np_, 0:G * 64], o1[0:np_, 0:G * 64], o2[0:np_, 0:G * 64], ALU.add, [bo1, bo2], [bo1])

                    def rmsnorm_to_mix(src_ps_or_sb, np_, h, col0, ncol, rdeps, dest=None, bdest=None):
                        act(junk[0:np_, :], att[0:np_, :], AF.Square, [batt], [bjunk, bst], accum=st[0:np_, 8:9])
                        act(st[0:np_, 9:10], st[0:np_, 8:9], AF.Sqrt, [bst], [bst], bias=EPS, scale=1.0 / 128.0)
                        recip(st[0:np_, 10:11], st[0:np_, 9:10], [bst], [bst])
                        stt(attb[0:np_, :], att[0:np_, :], st[0:np_, 10:11], gsc[0:np_, :], ALU.mult, ALU.mult, [batt, bst, bgsc], [battb])
                        tr(psT[:, 0:np_], attb[0:np_, :], identb[0:np_, 0:np_], [battb, bidb], [bT])
                        if dest is None:
                            cp("scalar", mstg[:, col0:col0 + ncol], psT[:, 0:ncol], [bT], [bmstg])
                        else:
                            cp("scalar", dest, psT[:, 0:ncol], [bT], [bdest])

                    def dec_gen(h):
                        for s in range(NS):
                            mm(psC[0:65, 0:128], sel4[:, s * 65:(s + 1) * 65], qs[:, h, :], True, True, [bsel, bqs], [bC])
                            cp("scalar", qbc[:, s, :], psC[0:65, 0:128], [bC], [bqbc])
                            dma("sync", knew[64:65, s, :], ksst[s:s + 1, h % 2, :], [bkss], [bknew])
                        for s in range(NS):
                            for c in range(NCH):
                                bi = kcnt[0] % NKT
                                kcnt[0] += 1
                                idma(Kt[bi][0:64, :], bass.AP(tensor=ck_d.tensor, offset=0, ap=[[CH * 128, npool * 8], [1, CH * 128]]), ptc[:, s, c:c + 1],
                                     [bptc], [bKt[bi]], eoff=h * npool * 16384)
                                kv = Kt[bi][0:64, :].rearrange("p (t f) -> p t f", t=CH)
                                tt("vector", kv, kv, qbc[0:64, s, :].unsqueeze(1).to_broadcast([64, CH, 128]), ALU.mult, [bKt[bi], bqbc], [bKt[bi]])
                                red("vector", sc[0:64, s, c * CH:(c + 1) * CH, :].rearrange("p t m -> p (t m)"),
                                    Kt[bi][0:64, :].rearrange("p (a d) -> p a d", d=64), ALU.add, [bKt[bi]], [bsc2])
                                yield 1
                        tt("vector", knew[64:65, :, :], knew[64:65, :, :], qbc[64:65, :, :], ALU.mult, [bknew, bqbc], [bknew])
                        red("vector", sc[64:65, :, 0, :], knew[64:65, :, :].rearrange("p s (m d) -> p s m d", d=64), ALU.add, [bknew], [bsc2])
                        memset("gpsimd", sc[64:65, :, 1:, :], -1e30, [bsc2])
                        red("vector", dsm[:, 0:8].rearrange("p (s m) -> p s m", m=2), sc[:].rearrange("p s t m -> p s m t"), ALU.max, [bsc2], [bdsm])
                        P.op("tensor", lambda e: e.transpose(out=psC[0:8, 128:193], in_=dsm[:, 0:8], identity=identf[0:65, 0:65]), [bdsm, bidf], [bC])
                        red("vector", dsm2[:, 0:1], psC[0:8, 128:193], ALU.max, [bC], [bdsm2])
                        cp("vector", dsm2[:, 8:73], dsm2[:, 0:1].to_broadcast([8, 65]), [bdsm2], [bdsm2])
                        P.op("tensor", lambda e: e.transpose(out=psC[0:65, 256:264], in_=dsm2[:, 8:73], identity=identf[0:8, 0:8]), [bdsm2, bidf], [bC])
                        cp("vector", dsm[:, 8:16], psC[0:65, 256:264], [bC], [bdsm])
                        tt("vector", ee[:], sc[:], dsm[:, 8:16].rearrange("p (s m) -> p s m", m=2).unsqueeze(2).to_broadcast([65, NS, 128, 2]), ALU.subtract,
                           [bsc2, bdsm], [bee])
                        act(ee[:], ee[:], AF.Exp, [bee], [bee], scale=0.125)
                        red("vector", dsm[:, 16:24].rearrange("p (s m) -> p s m", m=2), ee[:].rearrange("p s t m -> p s m t"), ALU.add, [bee], [bdsm])
                        mm(psC[0:65, 320:328], ones65[:], dsm[:, 16:24], True, True, [bon, bdsm], [bC])
                        recip(dsm[:, 24:32], psC[0:65, 320:328], [bC], [bdsm])
                        rzv = dsm[:, 24:32].rearrange("p (s m) -> p s m", m=2)
                        ts("vector", dsm[:, 32:36], rzv[:, :, 1], lam[0:65, 1:2], None, ALU.mult, None, [bdsm, blam], [bdsm])
                        for s in range(NS):
                            ts("vector", wv[:], ee[:, s, :, 0], rzv[:, s, 0:1], None, ALU.mult, None, [bee, bdsm], [bwv])
                            stt(Bz[:, s, :, 3], ee[:, s, :, 1], dsm[:, 32 + s:33 + s], wv[:], ALU.mult, ALU.add, [bee, bdsm, bwv], [bBz])
                        yield 1
                        first_pv = True
                        for s in range(NS):
                            for c in range(NCH):
                                bi = kcnt[0] % NKT
                                kcnt[0] += 1
                                idma(Kt[bi][0:64, :], bass.AP(tensor=cv_d.tensor, offset=0, ap=[[CH * 128, npool * 8], [1, CH * 128]]), ptc[:, s, c:c + 1],
                                     [bptc], [bKt[bi]], eoff=h * npool * 16384)
                                if c == 0:
                                    dma("sync", Kt[bi][64:65, 0:128], vsst[s:s + 1, h % 2, :], [bvss], [bKt[bi]])
                                vi = bi % 2
                                cp("scalar", Vb[vi][:], Kt[bi][:], [bKt[bi]], [bVb[vi]])
                                for t in range(CH):
                                    tok = c * CH + t
                                    last = (s == NS - 1 and c == NCH - 1 and t == CH - 1)
                                    mm(psB[0:NS, 0:128], Bz[:, s, tok, 3 - s:7 - s], Vb[vi][:, t * 128:(t + 1) * 128], first_pv, last, [bBz, bVb[vi]], [bB])
                                    first_pv = False
                                yield 1
                        cp("vector", att[0:NS, :], psB[0:NS, 0:128], [bB], [batt])
                        rmsnorm_to_mix(None, NS, h, TOWN, NS, None, dest=msts[:, h, :], bdest=bmsts)
                        yield 1

                    pending = None
                    for h in range(8):
                        qc, kc, vc = 2048 + h * 128, 3072 + h * 128, 4096 + h * 128
                        for ci, c0 in enumerate((qc, kc, vc)):
                            dma("gpsimd", wq3[:, :, ci * 128:(ci + 1) * 128], w_in_d[:, c0:c0 + 128].rearrange("(k p) c -> p k c", p=128), (), [bw3])
                        for tti in range(16):
                            own = tti >= 8
                            ps, bp = psrot[tti % 3]
                            if own:
                                for k in range(16):
                                    mm(ps[:, 0:384], xT[:, k, tti * 128:(tti + 1) * 128], wq3[:, k, 0:384], k == 0, k == 15, [bxT[k], bw3], [bp])
                                rope(ps, 4, rcos[:, tti, :], rsin[:, tti, 0, :], rsin[:, tti, 1, :], 128, t1, t2, bt1, bt2, [bp, brc, brs])
                                tt("gpsimd", t2[:, 0:256], t1[:, 0:256], t1[:, 0:256], ALU.mult, [bt1, bt2], [bt2])
                                red("vector", nrm[:, tti, 0:4], t2[:, 0:256].rearrange("p (g d) -> p g d", d=64), ALU.add, [bt2], [bnrm])
                                cp("scalar", qb16[:, tti - 8, :], t1[:, 0:128], [bt1], [bqb])
                                cp("scalar", kb16[:, tti, :], t1[:, 128:256], [bt1], [bkb])
                                cp("gpsimd", kst[:, tti - 8, :], t1[:, 128:256], [bt1], [bkst])
                                cp("scalar", vb[:, tti, :], ps[:, 256:384], [bp], [bvb])
                                cp("scalar", vst[:, tti - 8, :], ps[:, 256:384], [bp], [bvst])
                            else:
                                for k in range(16):
                                    mm(ps[:, 0:256], xT[:, k, tti * 128:(tti + 1) * 128], wq3[:, k, 128:384], k == 0, k == 15, [bxT[k], bw3], [bp])
                                rope(ps, 2, rcos[:, tti, :], rsin[:, tti, 0, :], rsin[:, tti, 1, :], 128, t1, t2, bt1, bt2, [bp, brc, brs])
                                tt("gpsimd", t2[:, 0:128], t1[:, 0:128], t1[:, 0:128], ALU.mult, [bt1, bt2], [bt2])
                                red("vector", nrm[:, tti, 2:4], t2[:, 0:128].rearrange("p (g d) -> p g d", d=64), ALU.add, [bt2], [bnrm])
                                cp("scalar", kb16[:, tti, :], t1[:, 0:128], [bt1], [bkb])
                                cp("scalar", vb[:, tti, :], ps[:, 128:256], [bp], [bvb])
                        dma("sync", k_o[:, h * 128:(h + 1) * 128].rearrange("(n p) f -> p n f", p=128), kst[:], [bkst], (), final=True)
                        dma("sync", v_o[:, h * 128:(h + 1) * 128].rearrange("(n p) f -> p n f", p=128), vst[:], [bvst], (), final=True)
                        for m in range(2):
                            for g in range(2):
                                for j in range(8):
                                    tr(psT[0:64, j * 128:(j + 1) * 128], kb16[:, g * 8 + j, m * 64:(m + 1) * 64], identb[:], [bkb, bidb], [bT, bTh1])
                                cp("vector" if g == 0 else "scalar", kT[m][0:64, g * 1024:(g + 1) * 1024], psT[0:64, :], [bT, bTh1], [bkT[m]])
                            for j in range(8):
                                tr(psT[0:64, j * 128:(j + 1) * 128], qb16[:, j, m * 64:(m + 1) * 64], identb[:], [bqb, bidb], [bT, bTh1])
                            cp("vector", qT[m][0:64, :], psT[0:64, :], [bT, bTh1], [bqT[m]])
                        red("vector", kmx[:, 0:2], nrm[:, :, 2:4].rearrange("p t m -> p m t"), ALU.max, [bnrm], [bkmx])
                        P.op("tensor", lambda e: e.transpose(out=psC[0:2, 0:128], in_=kmx[:, 0:2], identity=identf[:]), [bkmx, bidf], [bC])
                        red("vector", kmx2[:, 0:1], psC[0:2, 0:128], ALU.max, [bC], [bkmx2])
                        cp("vector", kmx2[:, 4:132], kmx2[:, 0:1].to_broadcast([2, 128]), [bkmx2], [bkmx2])
                        P.op("tensor", lambda e: e.transpose(out=psC[:, 128:130], in_=kmx2[:, 4:132], identity=identf[0:2, 0:2]), [bkmx2, bidf], [bC])
                        cp("vector", kmx[:, 2:4], psC[:, 128:130], [bC], [bkmx])
                        tt("vector", nbq[:], nrm[:, 8:16, 0:2], kmx[:, 2:4].unsqueeze(1).to_broadcast([128, 8, 2]), ALU.mult, [bnrm, bkmx], [bnbq])
                        act(nbq[:], nbq[:], AF.Sqrt, [bnbq], [bnbq])
                        ts("vector", nbq[:], nbq[:], -0.125, None, ALU.mult, None, [bnbq], [bnbq])
                        for i in range(8):
                            nk = 1024 + (i + 1) * 128
                            pieces = [(c0, min(512, nk - c0)) for c0 in range(0, nk, 512)]
                            npc = len(pieces)
                            for m in range(2):
                                for pi, (c0, w_) in enumerate(pieces):
                                    sb = acnt[0] % 4
                                    acnt[0] += 1
                                    lastp = (pi == npc - 1)
                                    S = psS[:, sb * 512:sb * 512 + w_]
                                    mm(S, qT[m][:, i * 128:(i + 1) * 128], kT[m][:, c0:c0 + w_], True, not lastp, [bqT[m], bkT[m]], [bSp[sb]])
                                    if lastp:
                                        mm(psS[:, sb * 512 + w_ - 128:sb * 512 + w_], identb[:], cmaskb[:], False, True, [bidb, bcmb], [bSp[sb]])
                                    pb_ = acnt[1] % 3
                                    acnt[1] += 1
                                    act(Pp[pb_][:, 0:w_], S, AF.Exp, [bSp[sb], bnbq], [bPp[pb_], bzc], bias=nbq[:, i, m:m + 1], scale=0.125, accum=zc[:, m, pi:pi + 1])
                                    nblk = w_ // 128
                                    tb = acnt[2] % 2
                                    acnt[2] += 1
                                    btb = bT if tb == 0 else bTh1
                                    for j in range(nblk):
                                        tr(psT[:, tb * 512 + j * 128:tb * 512 + (j + 1) * 128], Pp[pb_][:, j * 128:(j + 1) * 128], identb[:], [bPp[pb_], bidb], [btb])
                                    cp("scalar" if (acnt[2] % 3 == 0) else "vector", PTp[pb_][:, 0:nblk, :],
                                       psT[:, tb * 512:tb * 512 + nblk * 128].rearrange("p (a b) -> p a b", a=nblk), [btb], [bPTp[pb_]])
                                    for j in range(nblk):
                                        kb = c0 // 128 + j
                                        mm(psA[:, m * 128:(m + 1) * 128], PTp[pb_][:, j, :], vb[:, kb, :], (pi == 0 and j == 0), (lastp and j == nblk - 1),
                                           [bPTp[pb_], bvb], [bA])
                            red("vector", st[:, 5:7], zc[:, :, 0:npc], ALU.add, [bzc], [bst])
                            recip(st[:, 11:13], st[:, 5:7], [bst], [bst])
                            ts("vector", att2[:], psA[:, 0:128], st[:, 11:12], None, ALU.mult, None, [bA, bst], [batt2])
                            tt("vector", st[:, 13:14], st[:, 12:13], lam[:, 1:2], ALU.mult, [bst, blam], [bst])
                            stt(att[:], psA[:, 128:256], st[:, 13:14], att2[:], ALU.mult, ALU.add, [bA, bst, batt2], [batt])
                            rmsnorm_to_mix(None, 128, h, i * 128, 128, None)
                            if pending is not None:
                                for _ in range(10):
                                    if next(pending, "done") == "done":
                                        pending = None
                                        break

                        ps, bp = psrot[2]
                        for k in range(16):
                            mm(ps[0:NS, 0:384], xsT[:, k, :], wq3[:, k, 0:384], k == 0, k == 15, [bxs, bw3], [bp])
                        rope(ps, 4, rsam[:, 0:32], rsam[:, 32:64], rsam[:, 64:96], NS, ts1, ts2, bts1, bts2, [bp, brsm])
                        cp("vector", qs[:, h, :], ts1[:, 0:128], [bts1], [bqs])
                        cp("vector", ksst[:, h % 2, :], ts1[:, 128:256], [bts1], [bkss])
                        cp("vector", vsst[:, h % 2, :], ps[0:NS, 256:384], [bp], [bvss])
                        dma("sync", ks_o[:, h * 128:(h + 1) * 128], ksst[:, h % 2, :], [bkss], (), final=True)
                        dma("sync", vs_o[:, h * 128:(h + 1) * 128], vsst[:, h % 2, :], [bvss], (), final=True)
                        if do_dec:
                            for _ in dec_gen(h):
                                pass
                        dma("sync", mix_d[8 + h][:, 0:TOWN], mstg[:, 0:TOWN], [bmstg], [bmix[8 + h]])
                    if pending is not None:
                        for _ in pending:
                            pass
                    if not do_dec:
                        memset("vector", msts[:], 0.0, [bmsts])
                    with nc.allow_non_contiguous_dma(reason="tiny sample columns"):
                        for hh in range(8):
                            dma("sync", mix_d[8 + hh][:, TOWN:TTOK], msts[:, hh, :], [bmsts], [bmix[8 + hh]])
                    conv_steps(100000)
                    P.barrier()
                    P.flush()
        if dbg:
            esd = contextlib.ExitStack()
            with esd:
                mixb = T(esd, "mixb", [128, TTOK], BF16); bmb = Buf("mixb")
                mixf = T(esd, "mixf", [128, TTOK], F32); bmf = Buf("mixf")
                for kk in range(16):
                    dma("sync", mixb[:], mix_d[kk], [bmix[kk]], [bmb])
                    cp("vector", mixf[:], mixb[:], [bmb], [bmf])
                    dma("sync", mix_o[:, kk, :], mixf[:], [bmf], (), final=True)
                P.barrier()
                P.flush()

        NT = 9
        bx1d = [Buf(f"x1d{t}") for t in range(NT)]
        if do_p2:
            es2 = contextlib.ExitStack()
            with es2:
                lnp = T(es2, "lnp", [128, 2, D], F32); blnp = Buf("lnp")
                xt = T(es2, "xt", [128, D], F32); bxt = Buf("xt")
                st2 = T(es2, "st2", [128, 16], F32); bst2 = Buf("st2")
                E = T(es2, "E", [128, NT, 128], I32); bE = [Buf(f"E{t}") for t in range(NT)]
                G = T(es2, "G", [128, NT, 128], F32); bG = [Buf(f"G{t}") for t in range(NT)]

                def layer_norm(buf_ap, bbuf, np_):
                    red("vector", st2[0:np_, 0:1], buf_ap, ALU.add, [bbuf], [bst2])
                    ts("vector", st2[0:np_, 1:2], st2[0:np_, 0:1], -1.0 / D, None, ALU.mult, None, [bst2], [bst2])
                    ts("vector", buf_ap, buf_ap, st2[0:np_, 1:2], None, ALU.add, None, [bbuf, bst2], [bbuf])
                    act(xt[0:np_, :], buf_ap, AF.Square, [bbuf], [bxt, bst2], accum=st2[0:np_, 2:3])
                    act(st2[0:np_, 3:4], st2[0:np_, 2:3], AF.Sqrt, [bst2], [bst2], bias=EPS, scale=1.0 / D)
                    recip(st2[0:np_, 4:5], st2[0:np_, 3:4], [bst2], [bst2])
                    stt(buf_ap, buf_ap, st2[0:np_, 4:5], lnp[0:np_, 0, :], ALU.mult, ALU.mult, [bbuf, bst2, blnp], [bbuf])
                    tt("gpsimd", buf_ap, buf_ap, lnp[0:np_, 1, :], ALU.add, [bbuf, blnp], [bbuf])

                esA = contextlib.ExitStack()
                with esA:
                    wbuf = T(esA, "wbufA", [128, 16, D], BF16); bwb = Buf("wbufA")
                    mixt = [T(esA, f"mixt{i}", [128, 16, 128], BF16) for i in range(2)]; bmt = [Buf("mixt0"), Buf("mixt1")]
                    x1t = [T(esA, f"x1tA{i}", [128, D], F32) for i in range(2)]; bx1t = [Buf("x1tA0"), Buf("x1tA1")]
                    xin = [T(esA, f"xin{i}", [128, D], F32) for i in range(2)]; bxin = [Buf("xin0"), Buf("xin1")]
                    for k in range(16):
                        dma("gpsimd", wbuf[:, k, :], w_out_d[k * 128:(k + 1) * 128, :], (), [bwb])
                    dma("sync", lnp[:], ln_d[:, 0:2, :], (), [blnp])
                    for t in range(NT):
                        np_ = 128 if t < 8 else NS
                        bi = t % 2
                        dma("sync", xin[bi][0:np_, :], xtok_d[t * 128:t * 128 + np_, :], (), [bxin[bi]])
                        dma("sync", mixt[bi][:, :, 0:np_], mix_d[:, :, t * 128:t * 128 + np_].rearrange("k p t -> p k t"), bmix, [bmt[bi]])
                        for nb in range(4):
                            ps, bp = psrot[nb % 3]
                            for kk in range(16):
                                mm(ps[0:np_, 0:512], mixt[bi][:, kk, 0:np_], wbuf[:, kk, nb * 512:(nb + 1) * 512], kk == 0, kk == 15,
                                   [bmt[bi], bwb], [bp])
                            stt(x1t[bi][0:np_, nb * 512:(nb + 1) * 512], xin[bi][0:np_, nb * 512:(nb + 1) * 512], ALPHA, ps[0:np_, 0:512], ALU.mult, ALU.add,
                                [bxin[bi], bp], [bx1t[bi]])
                        layer_norm(x1t[bi][0:np_, :], bx1t[bi], np_)
                        dma("sync", x1_d[t * 128:t * 128 + np_, :], x1t[bi][0:np_, :], [bx1t[bi]], [bx1d[t]])
                        if dbg:
                            dma("sync", x1_o[t * 128:t * 128 + np_, :], x1t[bi][0:np_, :], [bx1t[bi]], (), final=True)
                    P.barrier()
                    P.flush()

                if do_peer:
                    esB = contextlib.ExitStack()
                    with esB:
                        wbuf = T(esB, "wbufB", [128, 16, D], BF16); bwb = Buf("wbufB")
                        k1T = T(esB, "k1T", [128, 8, 128], F32); bk1 = Buf("k1T")
                        k2T = T(esB, "k2T", [128, 8, 128], F32); bk2 = Buf("k2T")
                        iota16 = T(esB, "iota16", [128, 16, 16], F32); bio = Buf("iota")
                        x1f = T(esB, "x1f", [128, D], F32); bx1f = Buf("x1f")
                        x1T = T(esB, "x1T", [128, 16, 128], BF16); bx1T = Buf("x1T")
                        x1b = T(esB, "x1b", [128, D], BF16); bx1b = Buf("x1b")
                        qTf = T(esB, "qTf", [128, 16, 128], F32); bqTf = Buf("qTf")
                        S12 = T(esB, "S12", [128, 16, 128], F32); bS12 = Buf("S12")
                        wk = T(esB, "wk", [128, 256], F32); bwk = Buf("wk")
                        V12 = T(esB, "V12", [128, 16, 16], F32); bV12 = Buf("V12")
                        I12 = T(esB, "I12", [128, 16, 16], U32); bI12 = Buf("I12")
                        I12f = T(esB, "I12f", [128, 16, 16], F32); bI12f = Buf("I12f")
                        comb = T(esB, "comb", [128, 8, 256], F32); bcomb = Buf("comb")
                        sv = T(esB, "sv", [128, 8, 16], F32); bsv = Buf("sv")
                        svx = T(esB, "svx", [128, 8, 16], F32); bsvx = Buf("svx")
                        si = T(esB, "si", [128, 8, 16], U32); bsi = Buf("si")
                        sif = T(esB, "sif", [128, 8, 16], F32); bsif = Buf("sif")
                        sab = T(esB, "sab", [128, 2, 8, 16], F32); bsab = Buf("sab")
                        oh = T(esB, "oh", [128, 8, 16, 16], F32); boh = Buf("oh")
                        isel = T(esB, "isel", [128, 2, 8, 16], F32); bisel = Buf("isel")
                        ef = T(esB, "ef", [128, 128], F32); bef = Buf("ef")

                        for k in range(16):
                            dma("gpsimd", wbuf[:, k, :], wq_d[k * 128:(k + 1) * 128, :], (), [bwb])
                        dma("sync", k1T[:], k1T_d, (), [bk1])
                        dma("sync", k2T[:], k2T_d, (), [bk2])
                        dma("sync", iota16[:], iota_d.rearrange("p (a b) -> p a b", a=16), (), [bio])
                        memset("vector", x1f[:], 0.0, [bx1f])

                        def bc4(ap3, h0):
                            return ap3.unsqueeze(1).to_broadcast([128, 4, 16, 16])

                        for t in range(NT):
                            np_ = 128 if t < 8 else NS
                            dma("sync", x1f[0:np_, :], x1_d[t * 128:t * 128 + np_, :], [bx1d[t]], [bx1f])
                            cp("scalar", x1b[:], x1f[:], [bx1f], [bx1b])
                            for g in range(2):
                                for j in range(8):
                                    tr(psT[:, j * 128:(j + 1) * 128], x1b[:, (g * 8 + j) * 128:(g * 8 + j + 1) * 128], identb[:], [bx1b, bidb], [bT])
                                cp("vector", x1T[:, g * 8:(g + 1) * 8, :], psT[:, :].rearrange("p (a b) -> p a b", a=8), [bT], [bx1T])
                            for c in range(16):
                                ps, bp = psrot[c % 3]
                                for kk in range(16):
                                    mm(ps[:, 0:128], wbuf[:, kk, c * 128:(c + 1) * 128], x1T[:, kk, :], kk == 0, kk == 15, [bwb, bx1T], [bp])
                                cp("scalar" if c % 2 == 0 else "vector", qTf[:, c, :], ps[:, 0:128], [bp], [bqTf])
                            for c in range(16):
                                hh, half = c // 2, c % 2
                                ps, bp = psrot[c % 3]
                                kk_ap = k1T[:, hh, :] if half == 0 else k2T[:, hh, :]
                                mm(ps[:, 0:128], qTf[:, c, :], kk_ap, True, True, [bqTf, bk1, bk2], [bp])
                                cp("scalar" if c % 2 == 0 else "vector", S12[:, c, :], ps[:, 0:128], [bp], [bS12])
                            for c in range(16):
                                P.op("vector", lambda e, c=c: e.max(out=V12[:, c, 0:8], in_=S12[:, c, :]), [bS12], [bV12])
                                P.op("vector", lambda e, c=c: e.max_index(out=I12[:, c, 0:8], in_max=V12[:, c, 0:8], in_values=S12[:, c, :]), [bS12, bV12], [bI12])
                                P.op("vector", lambda e, c=c: e.match_replace(out=wk[:, 0:128], in_to_replace=V12[:, c, 0:8], in_values=S12[:, c, :], imm_value=-1e30),
                                     [bS12, bV12], [bwk])
                                P.op("vector", lambda e, c=c: e.max(out=V12[:, c, 8:16], in_=wk[:, 0:128]), [bwk], [bV12])
                                P.op("vector", lambda e, c=c: e.max_index(out=I12[:, c, 8:16], in_max=V12[:, c, 8:16], in_values=wk[:, 0:128]), [bwk, bV12], [bI12])
                            cp("vector", I12f[:], I12[:], [bI12], [bI12f])
                            V4 = V12[:].rearrange("p (h two) k -> p h two k", two=2)
                            I4 = I12f[:].rearrange("p (h two) k -> p h two k", two=2)
                            cv4 = comb[:].rearrange("p h (a b) -> p h a b", a=16)
                            for hq in (0, 4):
                                tt("vector", cv4[:, hq:hq + 4], V4[:, hq:hq + 4, 0, :].unsqueeze(3).to_broadcast([128, 4, 16, 16]),
                                   V4[:, hq:hq + 4, 1, :].unsqueeze(2).to_broadcast([128, 4, 16, 16]), ALU.add, [bV12], [bcomb])
                            for hh in range(8):
                                P.op("vector", lambda e, hh=hh: e.max(out=sv[:, hh, 0:8], in_=comb[:, hh, :]), [bcomb], [bsv])
                                P.op("vector", lambda e, hh=hh: e.max_index(out=si[:, hh, 0:8], in_max=sv[:, hh, 0:8], in_values=comb[:, hh, :]), [bcomb, bsv], [bsi])
                                P.op("vector", lambda e, hh=hh: e.match_replace(out=wk[:], in_to_replace=sv[:, hh, 0:8], in_values=comb[:, hh, :], imm_value=-1e30),
                                     [bcomb, bsv], [bwk])
                                P.op("vector", lambda e, hh=hh: e.max(out=sv[:, hh, 8:16], in_=wk[:]), [bwk], [bsv])
                                P.op("vector", lambda e, hh=hh: e.max_index(out=si[:, hh, 8:16], in_max=sv[:, hh, 8:16], in_values=wk[:]), [bwk, bsv], [bsi])
                            cp("vector", sif[:], si[:], [bsi], [bsif])
                            ts("vector", svx[:], sif[:], 0.0625, -1.0, ALU.mult, ALU.add, [bsif], [bsvx])
                            for hq in (0, 4):
                                tt("vector", oh[:, hq:hq + 4], svx[:, hq:hq + 4, :].unsqueeze(3).to_broadcast([128, 4, 16, 16]), bc4(iota16[:], hq), ALU.is_ge,
                                   [bsvx, bio], [boh])
                            red("vector", sab[:, 0].rearrange("p h j -> p (h j)"), oh[:].rearrange("p h j a -> p (h j) a"), ALU.add, [boh], [bsab])
                            stt(sab[:, 1].rearrange("p h j -> p (h j)"), sab[:, 0].rearrange("p h j -> p (h j)"), -16.0, sif[:].rearrange("p h j -> p (h j)"),
                                ALU.mult, ALU.add, [bsab, bsif], [bsab])
                            for side in range(2):
                                for hq in (0, 4):
                                    tt("vector", oh[:, hq:hq + 4], bc4(iota16[:], hq), sab[:, side, hq:hq + 4, :].unsqueeze(3).to_broadcast([128, 4, 16, 16]),
                                       ALU.is_equal, [bio, bsab], [boh])
                                    tt("vector", oh[:, hq:hq + 4], oh[:, hq:hq + 4], I4[:, hq:hq + 4, side, :].unsqueeze(2).to_broadcast([128, 4, 16, 16]),
                                       ALU.mult, [boh, bI12f], [boh])
                                red("vector", isel[:, side].rearrange("p h j -> p (h j)"), oh[:].rearrange("p h j a -> p (h j) a"), ALU.add, [boh], [bisel])
                            stt(ef[:], isel[:, 0].rearrange("p h j -> p (h j)"), 128.0, isel[:, 1].rearrange("p h j -> p (h j)"), ALU.mult, ALU.add, [bisel], [bef])
                            cp("vector", E[:, t, :], ef[:], [bef], [bE[t]])
                            tt("vector", svx[:], sv[:], sv[:, :, 0:1].to_broadcast([128, 8, 16]), ALU.subtract, [bsv, bsvx], [bsvx])
                            act(svx[:], svx[:], AF.Exp, [bsvx], [bsvx])
                            red("vector", st2[:, 8:16], svx[:], ALU.add, [bsvx], [bst2])
                            recip(st2[:, 8:16], st2[:, 8:16], [bst2], [bst2])
                            tt("vector", G[:, t, :].rearrange("p (h j) -> p h j", h=8), svx[:], st2[:, 8:16].unsqueeze(2).to_broadcast([128, 8, 16]), ALU.mult,
                               [bsvx, bst2], [bG[t]])
                        P.barrier()
                        P.flush()

                    esC = contextlib.ExitStack()
                    with esC:
                        NG = 20
                        gsl = [T(esC, f"gsl{i}", [128, D], BF16) for i in range(NG)]; bg = [Buf(f"g{i}") for i in range(NG)]
                        x1c = [T(esC, f"x1c{i}", [128, D], F32) for i in range(2)]; bx1c = [Buf("x1c0"), Buf("x1c1")]
                        accs = [T(esC, f"acc{i}", [128, D], F32) for i in range(2)]; bacc = [Buf("acc0"), Buf("acc1")]
                        junk = T(esC, "junkC", [128, D], BF16); bjunk = Buf("junkC")
                        dg = [T(esC, f"dg{i}", [128, 128], BF16) for i in range(4)]; bdg = [Buf(f"dg{i}") for i in range(4)]
                        actv = T(esC, "actv", [128, 128], F32); bactv = Buf("actv")
                        coef = T(esC, "coef", [128, 128], F32); bcoef = Buf("coef")
                        dma("sync", lnp[:], ln_d[:, 2:4, :], (), [blnp])
                        gi = 0
                        di = 0
                        for t in range(NT):
                            np_ = 128 if t < 8 else NS
                            bi = t % 2
                            dma("sync", x1c[bi][0:np_, :], x1_d[t * 128:t * 128 + np_, :], [bx1d[t]], [bx1c[bi]])
                            for s in range(128):
                                g_ap = gsl[gi][0:np_, :]
                                idma(g_ap, pub_d, E[0:np_, t, s:s + 1], [bE[t]], [bg[gi]])
                                stt(junk[0:np_, :], g_ap, 1.0, x1c[bi][0:np_, :], ALU.mult, ALU.mult, [bg[gi], bx1c[bi]], [bjunk, bactv], accum=actv[0:np_, s:s + 1])
                                gi = (gi + 1) % NG
                            act(coef[0:np_, :], actv[0:np_, :], AF.Gelu, [bactv], [bcoef])
                            tt("vector", coef[0:np_, :], coef[0:np_, :], G[0:np_, t, :], ALU.mult, [bcoef, bG[t]], [bcoef])
                            for s in range(128):
                                g_ap = gsl[gi][0:np_, :]
                                idma(g_ap, pvb_d, E[0:np_, t, s:s + 1], [bE[t]], [bg[gi]])
                                act(dg[di][0:np_, 0:np_], identb[0:np_, 0:np_], AF.Copy, [bidb, bcoef], [bdg[di]], scale=coef[0:np_, s:s + 1])
                                for nb in range(4):
                                    mm(psS[0:np_, nb * 512:(nb + 1) * 512], dg[di][0:np_, 0:np_], gsl[gi][0:np_, nb * 512:(nb + 1) * 512], s == 0, s == 127,
                                       [bdg[di], bg[gi]], [bS])
                                gi = (gi + 1) % NG
                                di = (di + 1) % 4
                            stt(accs[bi][0:np_, :], x1c[bi][0:np_, :], ALPHA, psS[0:np_, :], ALU.mult, ALU.add, [bx1c[bi], bS], [bacc[bi]])
                            layer_norm(accs[bi][0:np_, :], bacc[bi], np_)
                            dma("sync", y_o[t * 128:t * 128 + np_, :], accs[bi][0:np_, :], [bacc[bi]], (), final=True)
                        P.flush(last=True)
                else:
                    esC = contextlib.ExitStack()
                    with esC:
                        x1c = T(esC, "x1c", [128, D], F32); bx1c = Buf("x1c")
                        for t in range(NT):
                            np_ = 128 if t < 8 else NS
                            dma("sync", x1c[0:np_, :], x1_d[t * 128:t * 128 + np_, :], [bx1d[t]], [bx1c])
                            dma("sync", y_o[t * 128:t * 128 + np_, :], x1c[0:np_, :], [bx1c], (), final=True)
                        P.flush(last=True)
        else:
            P.flush(last=True)
    return nc


def _rope_tables(pos):
    half = 32
    inv = (10000.0 ** (-np.arange(half, dtype=np.float32) * 2.0 / 64.0)).astype(np.float32)
    ang = pos.astype(np.float32)[:, None] * inv[None, :]
    return np.cos(ang).astype(np.float32), np.sin(ang).astype(np.float32)


_CACHE = {}


def kernel(x_prompt, x_sample, cache_k, cache_v, state_conv, state_h, page_table,
           w_in, conv_w, conv_b, lru_wa, lru_ba, lru_wx, lru_bx, lru_lambda,
           lambda_q1, lambda_k1, lambda_q2, lambda_k2, subln_g, w_out, ln1_g, ln1_b,
           peer_wq, peer_k1, peer_k2, peer_u, peer_v, ln2_g, ln2_b, _flags=None, _trace=False):
    f = np.float32
    A = lambda a: np.ascontiguousarray(np.asarray(a))
    flags = _flags or {}
    key = tuple(sorted(flags.items()))
    if key not in _CACHE:
        _CACHE[key] = build_program(**flags)
    nc = _CACHE[key]

    x_prompt = A(x_prompt); x_sample = A(x_sample)
    npool = flags.get("npool", NPOOL); nexp = flags.get("nexp", 16384)
    ck = A(np.asarray(cache_k)[0][:npool].reshape(npool, 128, 8, 128).transpose(2, 0, 1, 3).reshape(8, npool, 16384))
    cv = A(np.asarray(cache_v)[0][:npool].transpose(2, 0, 1, 3).reshape(8, npool, 16384))
    w_in0 = A(np.asarray(w_in)[0]); w_out0 = A(np.asarray(w_out)[0]); wq0 = A(np.asarray(peer_wq)[0])
    chan = np.stack([np.asarray(conv_w)[0][0], np.asarray(conv_w)[0][1], np.asarray(conv_w)[0][2], np.asarray(conv_w)[0][3],
                     np.asarray(conv_b)[0], np.asarray(lru_ba)[0], np.asarray(lru_bx)[0], np.asarray(lru_lambda)[0]], axis=-1)
    chan = A(chan.reshape(8, 128, 8).transpose(1, 0, 2))
    lwa = A(np.asarray(lru_wa)[0].transpose(1, 0, 2))
    lwx = A(np.asarray(lru_wx)[0].transpose(1, 0, 2))
    lamv = A(np.broadcast_to(np.stack([np.asarray(lambda_q1)[0], np.asarray(lambda_k1)[0], np.asarray(lambda_q2)[0], np.asarray(lambda_k2)[0]])[None], (128, 4, 64)))
    subg = A(np.broadcast_to(np.asarray(subln_g)[0][None], (128, 128)))
    ln = A(np.broadcast_to(np.stack([np.asarray(ln1_g)[0], np.asarray(ln1_b)[0], np.asarray(ln2_g)[0], np.asarray(ln2_b)[0]])[None], (128, 4, D)))
    k1T = A(np.asarray(peer_k1)[0].transpose(2, 0, 1))
    k2T = A(np.asarray(peer_k2)[0].transpose(2, 0, 1))
    pu = A(np.asarray(peer_u)[0][:nexp]); pv = A(np.asarray(peer_v)[0][:nexp])
    ident = np.eye(128, dtype=f)
    cmask = np.where(np.arange(128)[None, :] <= np.arange(128)[:, None], 0.0, 8.0 * NEG).astype(f)
    sel4 = np.zeros((NS, NS, 65), f)
    for s in range(NS):
        sel4[s, s, :] = 1.0
    sel4 = sel4.reshape(NS, NS * 65)
    ones65 = np.ones((65, 65), f)
    iota16 = A(np.broadcast_to(np.arange(16, dtype=f)[None, None, :], (128, 16, 16)).reshape(128, 256))
    cs, sn = _rope_tables(np.array([8192]))
    rsam = A(np.broadcast_to(np.concatenate([cs[0], -sn[0], sn[0]])[None], (NS, 96)))
    sc0 = np.asarray(state_conv)[0]
    sh0 = np.asarray(state_h)[0]
    pt = np.asarray(page_table)

    in_maps = []
    for j in range(NCORES):
        b, hf = j // 2, j % 2
        xs = x_prompt[b]
        xT = np.zeros((D, NSLOT), f)
        if hf == 1:
            xT[:, :] = xs.T
        else:
            xT[:, TOWN:] = xs[0:TOWN].T
        own = xs[hf * TOWN:(hf + 1) * TOWN]
        smp = x_sample[NS * j:NS * (j + 1), 0, :]
        pos = np.concatenate([np.arange(TOWN), hf * TOWN + np.arange(TOWN)])
        c_, s_ = _rope_tables(pos)
        rcos = A(c_.reshape(16, 128, 32).transpose(1, 0, 2))
        rsin = A(np.stack([-s_, s_], axis=1).reshape(16, 128, 2, 32).transpose(1, 0, 2, 3))
        pbrow = np.zeros((1, NSLOT), f)
        if hf == 0:
            pbrow[0, 0:TOWN] = 8.0 * NEG
        pbv = np.zeros((128, 2), f)
        pbv[:, 0] = 0.0 if hf == 1 else NEG
        pbv[:, 1] = 1.0 if hf == 1 else 0.0
        scj = sc0[NS * j:NS * (j + 1)]
        sconv = A(scj.reshape(NS, 3, 8, 128).transpose(3, 2, 1, 0))
        shj = A(sh0[NS * j:NS * (j + 1)].reshape(NS, 8, 128).transpose(2, 1, 0))
        in_maps.append({
            "xT": xT, "xsT": A(smp.T), "xtok": A(np.concatenate([own, smp], 0)),
            "w_in": w_in0, "w_out": w_out0, "wq": wq0, "chan": chan, "lwa": lwa, "lwx": lwx, "lamv": lamv, "subg": subg,
            "rcos": rcos, "rsin": rsin, "rsam": rsam, "pb": pbv, "pbrow": pbrow, "ident": ident, "cmask": cmask, "sel4": sel4, "ones65": ones65,
            "iota16": iota16, "cache_k": ck, "cache_v": cv, "ptT": A(pt[NS * j:NS * (j + 1)].T.astype(np.int32)),
            "sconv": sconv, "sh": shj, "ln": ln, "k1T": k1T, "k2T": k2T, "peer_u": pu, "peer_v": pv,
        })
    res = run_bass_kernel_spmd(nc, in_maps, core_ids=list(range(NCORES)), **({"trace": True} if _trace else {}))
    R = res.results
    yp = np.zeros((4, 2048, D), f); ys = np.zeros((32, 1, D), f)
    kp = np.zeros((1, 4, 2048, 16, 64), f); vp = np.zeros((1, 4, 2048, 8, 128), f)
    cp_ = np.zeros((1, 4, 3, 1024), f); hp = np.zeros((1, 4, 1024), f)
    ksn = np.zeros((1, 32, 1, 16, 64), f); vsn = np.zeros((1, 32, 1, 8, 128), f)
    csn = np.zeros((1, 32, 3, 1024), f); hsn = np.zeros((1, 32, 1024), f)
    for j in range(NCORES):
        b, hf = j // 2, j % 2
        r = R[j]
        yp[b, hf * TOWN:(hf + 1) * TOWN] = r["y"][0:TOWN]
        ys[NS * j:NS * (j + 1), 0] = r["y"][TOWN:TTOK]
        kp[0, b, hf * TOWN:(hf + 1) * TOWN] = r["k_o"].reshape(TOWN, 16, 64)
        vp[0, b, hf * TOWN:(hf + 1) * TOWN] = r["v_o"].reshape(TOWN, 8, 128)
        if hf == 1:
            cp_[0, b] = r["conv_o"].transpose(2, 1, 0).reshape(3, 1024)
            hp[0, b] = r["h_o"].transpose(1, 0).reshape(1024)
        ksn[0, NS * j:NS * (j + 1), 0] = r["ks_o"].reshape(NS, 16, 64)
        vsn[0, NS * j:NS * (j + 1), 0] = r["vs_o"].reshape(NS, 8, 128)
        csn[0, NS * j:NS * (j + 1)] = r["convs_o"].transpose(3, 2, 1, 0).reshape(NS, 3, 1024)
        hsn[0, NS * j:NS * (j + 1)] = r["hs_o"].transpose(2, 1, 0).reshape(NS, 1024)
    if flags.get("dbg"):
        kernel._dbg = [(R[j]["mix_o"], R[j]["x1_o"]) for j in range(NCORES)]
    if _trace:
        kernel._exec_ns = res.exec_time_ns
    return (yp, ys, kp, vp, cp_, hp, ksn, vsn, csn, hsn)
```

```python
import contextlib
import numpy as np
import concourse.bass as bass
import concourse.mybir as mybir
from concourse.bass_utils import run_bass_kernel_spmd

F32 = mybir.dt.float32
BF16 = mybir.dt.bfloat16
I32 = mybir.dt.int32
U32 = mybir.dt.uint32
AF = mybir.ActivationFunctionType
ALU = mybir.AluOpType
AX = mybir.AxisListType

NCORES = 8
D = 2048
TOWN = 1024
NSLOT = 2048
NS = 4
TTOK = TOWN + NS
NPOOL = 2560
NPAGES = 64
EPS = 1e-5
NEG = -30000.0
ALPHA = 2.0 ** 0.25
LAM_INIT = 0.2

ENGS = ("sync", "scalar", "vector", "gpsimd", "tensor")


class Buf:
    __slots__ = ("name", "w", "r")

    def __init__(self, name):
        self.name = name
        self.w = None
        self.r = []


class Prog:
    def __init__(self, nc, n_dma_sems=12):
        self.nc = nc
        self.lists = {e: [] for e in ENGS}
        self.esem = {e: nc.alloc_semaphore(name="es_" + e) for e in ENGS}
        self.ecnt = {e: 0 for e in ENGS}
        self.dsem, self.dcnt, self.dpos = {}, {}, {}
        for q in ("sync", "scalar", "gpsimd"):
            self.dsem[q] = [nc.alloc_semaphore(name=f"ds_{q}{i}") for i in range(n_dma_sems)]
            self.dcnt[q] = [0] * n_dma_sems
            self.dpos[q] = 0
        self.waited = {e: {} for e in ENGS}
        self.final_events = []

    def _need(self, eng, ev):
        if ev is None:
            return
        sem, val, src = ev
        if src == eng and eng == "tensor":
            return
        key = id(sem)
        if self.waited[eng].get(key, 0) >= val:
            return
        self.waited[eng][key] = val
        self.lists[eng].append(("wait", sem, val))

    def _deps(self, eng, reads, writes):
        for b in reads:
            self._need(eng, b.w)
        for b in writes:
            self._need(eng, b.w)
            for ev in b.r:
                self._need(eng, ev)

    def _mark(self, ev, reads, writes):
        for b in reads:
            b.r.append(ev)
            if len(b.r) > 48:
                last = {}
                for e2 in b.r:
                    k = id(e2[0])
                    if k not in last or last[k][1] < e2[1]:
                        last[k] = e2
                b.r = list(last.values())
        for b in writes:
            b.w = ev
            b.r = []

    def op(self, eng, fn, reads=(), writes=()):
        self._deps(eng, reads, writes)
        self.ecnt[eng] += 1
        ev = (self.esem[eng], self.ecnt[eng], eng)
        self.lists[eng].append(("op", fn, self.esem[eng], 1))
        self._mark(ev, reads, writes)
        return ev

    def dma(self, q, fn, reads=(), writes=(), final=False):
        i = self.dpos[q]
        self.dpos[q] = (i + 1) % len(self.dsem[q])
        sem = self.dsem[q][i]
        if self.dcnt[q][i] > 0:
            self._need(q, (sem, self.dcnt[q][i], None))
        self._deps(q, reads, writes)
        self.dcnt[q][i] += 16
        ev = (sem, self.dcnt[q][i], None)
        self.lists[q].append(("op", fn, sem, 16))
        self._mark(ev, reads, writes)
        if final:
            self.final_events.append(ev)
        return ev

    def all_events(self):
        evs = [(self.esem[e], self.ecnt[e], e) for e in ENGS if self.ecnt[e] > 0]
        for q in self.dsem:
            for i, s in enumerate(self.dsem[q]):
                if self.dcnt[q][i] > 0:
                    evs.append((s, self.dcnt[q][i], None))
        return evs

    def barrier(self):
        evs = self.all_events()
        for e in ENGS:
            for ev in evs:
                if ev[2] == e:
                    continue
                self._need(e, ev)

    def flush(self, last=False):
        if last:
            for ev in self.all_events():
                self._need("sync", ev)
        lists = self.lists
        self.lists = {e: [] for e in ENGS}
        nc = self.nc

        def run(engobj, items):
            for it in items:
                if it[0] == "wait":
                    engobj.wait_ge(it[1], it[2])
                else:
                    it[1](engobj).then_inc(it[2], it[3])

        with nc.Block() as block:
            @block.sync
            def _(e):
                run(e, lists["sync"])

            @block.scalar
            def _(e):
                run(e, lists["scalar"])

            @block.vector
            def _(e):
                run(e, lists["vector"])

            @block.gpsimd
            def _(e):
                run(e, lists["gpsimd"])

            @block.tensor
            def _(e):
                run(e, lists["tensor"])


def build_program(do_rec=True, do_att=True, do_dec=True, do_p2=True, do_peer=True, dbg=False, npool=NPOOL, nexp=16384):
    nc = bass.Bass("TRN2", target_bir_lowering=False)

    def din(name, shape, dt=F32):
        return nc.dram_tensor(name, list(shape), dt, kind="ExternalInput").ap()

    def dout(name, shape, dt=F32):
        return nc.dram_tensor(name, list(shape), dt, kind="ExternalOutput").ap()

    xT_d = din("xT", [D, NSLOT])
    xsT_d = din("xsT", [D, NS])
    xtok_d = din("xtok", [TTOK, D])
    w_in_d = din("w_in", [D, 5120])
    w_out_d = din("w_out", [D, D])
    wq_d = din("wq", [D, D])
    chan_d = din("chan", [128, 8, 8])
    lwa_d = din("lwa", [128, 8, 128])
    lwx_d = din("lwx", [128, 8, 128])
    lamv_d = din("lamv", [128, 4, 64])
    subg_d = din("subg", [128, 128])
    rcos_d = din("rcos", [128, 16, 32])
    rsin_d = din("rsin", [128, 16, 2, 32])
    rsam_d = din("rsam", [NS, 96])
    pb_d = din("pb", [128, 2])
    pbrow_d = din("pbrow", [1, NSLOT])
    ident_d = din("ident", [128, 128])
    cmask_d = din("cmask", [128, 128])
    sel4_d = din("sel128", [NS, NS * 128])
    ones_d = din("ones128", [128, 128])
    iota_d = din("iota16", [128, 256])
    ck_d = din("cache_k", [8, npool, 16384])
    cv_d = din("cache_v", [8, npool, 16384])
    pt_d = din("ptT", [128, NS], I32)
    sconv_d = din("sconv", [128, 8, 3, NS])
    sh_d = din("sh", [128, 8, NS])
    ln_d = din("ln", [128, 4, D])
    k1T_d = din("k1T", [128, 8, 128])
    k2T_d = din("k2T", [128, 8, 128])
    pu_d = din("peer_u", [nexp, D])
    pv_d = din("peer_v", [nexp, D])

    y_o = dout("y", [TTOK, D])
    k_o = dout("k_o", [TOWN, 1024])
    v_o = dout("v_o", [TOWN, 1024])
    conv_o = dout("conv_o", [128, 8, 3])
    h_o = dout("h_o", [128, 8])
    ks_o = dout("ks_o", [NS, 1024])
    vs_o = dout("vs_o", [NS, 1024])
    convs_o = dout("convs_o", [128, 8, 3, NS])
    hs_o = dout("hs_o", [128, 8, NS])
    if dbg:
        mix_o = dout("mix_o", [128, 16, TTOK])
        x1_o = dout("x1_o", [TTOK, D])

    P = Prog(nc)

    def tt(eng, out, in0, in1, op, r, w):
        P.op(eng, lambda e: e.tensor_tensor(out=out, in0=in0, in1=in1, op=op), r, w)

    def ts(eng, out, in0, s1, s2, op0, op1, r, w):
        if op1 is None:
            P.op(eng, lambda e: e.tensor_scalar(out=out, in0=in0, scalar1=s1, scalar2=None, op0=op0), r, w)
        else:
            P.op(eng, lambda e: e.tensor_scalar(out=out, in0=in0, scalar1=s1, scalar2=s2, op0=op0, op1=op1), r, w)

    def stt(out, in0, scalar, in1, op0, op1, r, w, accum=None):
        if accum is None:
            P.op("vector", lambda e: e.scalar_tensor_tensor(out=out, in0=in0, scalar=scalar, in1=in1, op0=op0, op1=op1), r, w)
        else:
            P.op("vector", lambda e: e.scalar_tensor_tensor(out=out, in0=in0, scalar=scalar, in1=in1, op0=op0, op1=op1,
                                                            accum_out=accum), r, w)

    def act(out, in_, func, r, w, bias=None, scale=None, accum=None):
        kw = {}
        if bias is not None:
            kw["bias"] = bias
        if scale is not None:
            kw["scale"] = scale
        if accum is not None:
            kw["accum_out"] = accum
        P.op("scalar", lambda e: e.activation(out=out, in_=in_, func=func, **kw), r, w)

    def cp(eng, out, in_, r, w):
        if eng == "scalar":
            P.op("scalar", lambda e: e.copy(out=out, in_=in_), r, w)
        else:
            P.op(eng, lambda e: e.tensor_copy(out=out, in_=in_), r, w)

    def mm(out, lhsT, rhs, start, stop, r, w):
        P.op("tensor", lambda e: e.matmul(out, lhsT=lhsT, rhs=rhs, start=start, stop=stop), r, w)

    def tr(out, in_, ident, r, w):
        P.op("tensor", lambda e: e.transpose(out=out, in_=in_, identity=ident), r, w)

    def dma(q, out, in_, r, w, final=False):
        P.dma(q, lambda e: e.dma_start(out=out, in_=in_), r, w, final=final)

    def idma(out, in_, idx, r, w, eoff=0):
        P.dma("gpsimd", lambda e: e.indirect_dma_start(out=out, out_offset=None, in_=in_,
                                                      in_offset=bass.IndirectOffsetOnAxis(ap=idx, axis=0), element_offset=eoff), r, w)

    def memset(eng, ap, val, w):
        P.op(eng, lambda e: e.memset(ap, val), (), w)

    def red(eng, out, in_, op, r, w):
        P.op(eng, lambda e: e.tensor_reduce(out=out, in_=in_, axis=AX.X, op=op), r, w)

    def recip(out, in_, r, w):
        P.op("vector", lambda e: e.reciprocal(out=out, in_=in_), r, w)

    es_all = contextlib.ExitStack()
    with es_all:
        def T(es, name, shape, dt):
            return es.enter_context(nc.sbuf_tensor("s_" + name, list(shape), dt))

        def PSt(es, name, shape, dt):
            return es.enter_context(nc.psum_tensor("p_" + name, list(shape), dt))

        psS = PSt(es_all, "psS", [128, 2048], F32); bS = Buf("psS")
        psT = PSt(es_all, "psT", [128, 1024], BF16); bT = Buf("psT")
        psA = PSt(es_all, "psA", [128, 512], F32); bA = Buf("psA")
        psB = PSt(es_all, "psB", [128, 512], F32); bB = Buf("psB")
        psC = PSt(es_all, "psC", [128, 512], F32); bC = Buf("psC")
        psrot = [(psA, bA), (psB, bB), (psC, bC)]

        mix_dt = nc.dram_tensor("mix_d", [16, 128, TTOK], BF16)
        mix_d = mix_dt.ap()
        x1_dt = nc.dram_tensor("x1_d", [TTOK, D], F32)
        x1_d = x1_dt.ap()
        bmix = [Buf(f"mix{i}") for i in range(16)]
        pub_d = nc.dram_tensor("pub_d", [nexp, D], BF16).ap()
        pvb_d = nc.dram_tensor("pvb_d", [nexp, D], BF16).ap()
        mstg = T(es_all, "mstg", [128, TTOK], BF16); bmstg = Buf("mstg")
        identf = T(es_all, "identf", [128, 128], F32); bidf = Buf("identf")
        identb = T(es_all, "identb", [128, 128], BF16); bidb = Buf("identb")
        pbt = T(es_all, "pbt", [128, 2], F32); bpb = Buf("pb")

        dma("sync", identf[:], ident_d, (), [bidf])
        dma("gpsimd", identb[:], ident_d, (), [bidb])
        dma("sync", pbt[:], pb_d, (), [bpb])

        es1 = contextlib.ExitStack()
        with es1:
            xT = T(es1, "xTb", [128, 16, NSLOT], BF16); bxT = [Buf(f"xT{k}") for k in range(16)]
            xsT = T(es1, "xsTb", [128, 16, NS], BF16); bxs = Buf("xsT")
            chan = T(es1, "chan", [128, 8, 8], F32); bchan = Buf("chan")
            nsp = T(es1, "nsp", [128, 8, 2], F32); bnsp = Buf("nsp")
            for k in range(16):
                dma("gpsimd", xT[:, k, :], xT_d[k * 128:(k + 1) * 128, :], (), [bxT[k]])
            dma("gpsimd", xsT[:], xsT_d.rearrange("(k p) s -> p k s", p=128), (), [bxs])
            CR = 512
            conv_list = [(tab, r0) for tab in range(2) for r0 in range(0, nexp, CR)] if (do_p2 and do_peer) else []
            conv_pos = [0]

            def conv_steps(n):
                for _ in range(n):
                    if conv_pos[0] >= len(conv_list):
                        return
                    tab, r0 = conv_list[conv_pos[0]]
                    conv_pos[0] += 1
                    src = (pu_d if tab == 0 else pv_d)[r0:r0 + CR, :]
                    dst = (pub_d if tab == 0 else pvb_d)[r0:r0 + CR, :]
                    dma("gpsimd", dst, src, (), ())

            dma("sync", chan[:], chan_d, (), [bchan])
            tmpl = T(es1, "tmpl", [128, 8], F32); btl = Buf("tmpl")
            act(tmpl[:], chan[:, :, 7], AF.Exp, [bchan], [btl], scale=-1.0)
            act(tmpl[:], tmpl[:], AF.Ln, [btl], [btl], bias=1.0)
            ts("vector", nsp[:, :, 0], tmpl[:], -8.0, None, ALU.mult, None, [btl], [bnsp])
            ts("vector", nsp[:, :, 1], tmpl[:], -16.0, None, ALU.mult, None, [btl], [bnsp])

            if do_rec:
                esr = contextlib.ExitStack()
                with esr:
                    wrx = T(esr, "wrx", [128, 16, 128], BF16); bwrx = Buf("wrx")
                    wrg = T(esr, "wrg", [128, 16, 128], BF16); bwrg = Buf("wrg")
                    wab = T(esr, "wab", [128, 128], BF16); bwab = Buf("wab")
                    wxb = T(esr, "wxb", [128, 128], BF16); bwxb = Buf("wxb")
                    rx = T(esr, "rx", [128, 3 + NSLOT], F32); brx = Buf("rx")
                    gg = T(esr, "gg", [128, TOWN], F32); bgg = Buf("gg")
                    cvt = T(esr, "cvt", [128, NSLOT], F32); bcv = Buf("cv")
                    cvb = T(esr, "cvb", [128, NSLOT], BF16); bcvb = Buf("cvb")
                    rgt = T(esr, "rgt", [128, NSLOT], F32); brgt = Buf("rgt")
                    igt = T(esr, "igt", [128, NSLOT], F32); bigt = Buf("igt")
                    at = T(esr, "at", [128, NSLOT], F32); bat = Buf("at")
                    a2t = T(esr, "a2t", [128, NSLOT], F32); ba2 = Buf("a2t")
                    hpre = T(esr, "hpre", [128, TOWN], F32); bhp = Buf("hpre")
                    hown = T(esr, "hown", [128, TOWN], F32); bho = Buf("hown")
                    h0 = T(esr, "h0", [128, 1], F32); bh0 = Buf("h0")
                    cst = T(esr, "cst", [128, 8, 3], F32); bcst = Buf("cst")
                    hst = T(esr, "hst", [128, 8], F32); bhst = Buf("hst")
                    sconv = T(esr, "sconv", [128, 8, 3, NS], F32); bsc = Buf("sconv")
                    sht = T(esr, "sht", [128, 8, NS], F32); bsh = Buf("sht")
                    cvsst = T(esr, "cvsst", [128, 8, 3, NS], F32); bcvs = Buf("cvsst")
                    hsst = T(esr, "hsst", [128, 8, NS], F32); bhss = Buf("hsst")
                    sm = T(esr, "sm", [128, 12, NS], F32); bsm = Buf("sm")
                    smb = T(esr, "smb", [128, NS], BF16); bsmb = Buf("smb")
                    dma("sync", sconv[:], sconv_d, (), [bsc])
                    dma("sync", sht[:], sh_d, (), [bsh])
                    memset("vector", rx[:, 0:3], 0.0, [brx])
                    for r in range(8):
                        dma("gpsimd", wrx[:], w_in_d[:, r * 128:(r + 1) * 128].rearrange("(k p) c -> p k c", p=128), (), [bwrx])
                        dma("gpsimd", wrg[:], w_in_d[:, 1024 + r * 128:1024 + (r + 1) * 128].rearrange("(k p) c -> p k c", p=128), (), [bwrg])
                        dma("gpsimd", wab[:], lwa_d[:, r, :], (), [bwab])
                        dma("gpsimd", wxb[:], lwx_d[:, r, :], (), [bwxb])
                        conv_steps(8)
                        for blk in range(4):
                            ps, bp = psrot[blk % 3]
                            for k in range(16):
                                mm(ps[:, 0:512], wrx[:, k, :], xT[:, k, blk * 512:(blk + 1) * 512], k == 0, k == 15, [bwrx, bxT[k]], [bp])
                            cp("scalar", rx[:, 3 + blk * 512:3 + (blk + 1) * 512], ps[:, 0:512], [bp], [brx])
                        for blk in range(2, 4):
                            ps, bp = psrot[(blk + 1) % 3]
                            for k in range(16):
                                mm(ps[:, 0:512], wrg[:, k, :], xT[:, k, blk * 512:(blk + 1) * 512], k == 0, k == 15, [bwrg, bxT[k]], [bp])
                            act(gg[:, (blk - 2) * 512:(blk - 1) * 512], ps[:, 0:512], AF.Gelu, [bp], [bgg])
                        ts("vector", cvt[:], rx[:, 0:NSLOT], chan[:, r, 0:1], chan[:, r, 4:5], ALU.mult, ALU.add, [brx, bchan], [bcv])
                        for j in range(1, 4):
                            stt(cvt[:], rx[:, j:j + NSLOT], chan[:, r, j:j + 1], cvt[:], ALU.mult, ALU.add, [brx, bchan, bcv], [bcv])
                        cp("gpsimd", cvb[:], cvt[:], [bcv], [bcvb])
                        for blk in range(4):
                            ps, bp = psrot[blk % 3]
                            mm(ps[:, 0:512], wab[:], cvb[:, blk * 512:(blk + 1) * 512], True, True, [bwab, bcvb], [bp])
                            act(rgt[:, blk * 512:(blk + 1) * 512], ps[:, 0:512], AF.Sigmoid, [bp, bchan], [brgt], bias=chan[:, r, 5:6])
                            ps2, bp2 = psrot[(blk + 1) % 3]
                            mm(ps2[:, 0:512], wxb[:], cvb[:, blk * 512:(blk + 1) * 512], True, True, [bwxb, bcvb], [bp2])
                            act(igt[:, blk * 512:(blk + 1) * 512], ps2[:, 0:512], AF.Sigmoid, [bp2, bchan], [bigt], bias=chan[:, r, 6:7])
                        act(at[:], rgt[:], AF.Exp, [brgt, bnsp], [bat], scale=nsp[:, r, 0:1])
                        act(a2t[:], rgt[:], AF.Exp, [brgt, bnsp], [ba2], scale=nsp[:, r, 1:2])
                        ts("vector", a2t[:], a2t[:], -1.0, 1.0, ALU.mult, ALU.add, [ba2], [ba2])
                        ts("gpsimd", a2t[:], a2t[:], 1e-30, None, ALU.max, None, [ba2], [ba2])
                        act(a2t[:], a2t[:], AF.Sqrt, [ba2], [ba2])
                        tt("gpsimd", igt[:], igt[:], cvt[:], ALU.mult, [bigt, bcv], [bigt])
                        tt("vector", a2t[:], a2t[:], igt[:], ALU.mult, [ba2, bigt], [ba2])
                        P.op("vector", lambda e: e.tensor_tensor_scan(out=hpre[:], data0=at[:, 0:TOWN], data1=a2t[:, 0:TOWN], initial=0.0,
                                                                    op0=ALU.mult, op1=ALU.add), [bat, ba2], [bhp])
                        ts("vector", h0[:], hpre[:, TOWN - 1:TOWN], pbt[:, 1:2], None, ALU.mult, None, [bhp, bpb], [bh0])
                        P.op("vector", lambda e: e.tensor_tensor_scan(out=hown[:], data0=at[:, TOWN:NSLOT], data1=a2t[:, TOWN:NSLOT], initial=h0[:, 0:1],
                                                                    op0=ALU.mult, op1=ALU.add), [bat, ba2, bh0], [bho])
                        tt("gpsimd", mstg[:, 0:TOWN], hown[:], gg[:], ALU.mult, [bho, bgg], [bmstg])
                        cp("gpsimd", cst[:, r, :], rx[:, NSLOT:NSLOT + 3], [brx], [bcst])
                        cp("gpsimd", hst[:, r:r + 1], hown[:, TOWN - 1:TOWN], [bho], [bhst])
                        ps, bp = psrot[0]
                        for k in range(16):
                            mm(ps[:, 0:NS], wrx[:, k, :], xsT[:, k, :], k == 0, k == 15, [bwrx, bxs], [bp])
                        cp("vector", sm[:, 0, :], ps[:, 0:NS], [bp], [bsm])
                        ps, bp = psrot[1]
                        for k in range(16):
                            mm(ps[:, 0:NS], wrg[:, k, :], xsT[:, k, :], k == 0, k == 15, [bwrg, bxs], [bp])
                        act(sm[:, 1, :], ps[:, 0:NS], AF.Gelu, [bp], [bsm])
                        ts("vector", sm[:, 2, :], sconv[:, r, 0, :], chan[:, r, 0:1], chan[:, r, 4:5], ALU.mult, ALU.add, [bsc, bchan, bsm], [bsm])
                        stt(sm[:, 2, :], sconv[:, r, 1, :], chan[:, r, 1:2], sm[:, 2, :], ALU.mult, ALU.add, [bsc, bchan, bsm], [bsm])
                        stt(sm[:, 2, :], sconv[:, r, 2, :], chan[:, r, 2:3], sm[:, 2, :], ALU.mult, ALU.add, [bsc, bchan, bsm], [bsm])
                        stt(sm[:, 2, :], sm[:, 0, :], chan[:, r, 3:4], sm[:, 2, :], ALU.mult, ALU.add, [bchan, bsm], [bsm])
                        cp("vector", smb[:], sm[:, 2, :], [bsm], [bsmb])
                        ps, bp = psrot[2]
                        mm(ps[:, 0:NS], wab[:], smb[:], True, True, [bwab, bsmb], [bp])
                        act(sm[:, 3, :], ps[:, 0:NS], AF.Sigmoid, [bp, bchan], [bsm], bias=chan[:, r, 5:6])
                        ps, bp = psrot[0]
                        mm(ps[:, 0:NS], wxb[:], smb[:], True, True, [bwxb, bsmb], [bp])
                        act(sm[:, 4, :], ps[:, 0:NS], AF.Sigmoid, [bp, bchan], [bsm], bias=chan[:, r, 6:7])
                        act(sm[:, 5, :], sm[:, 3, :], AF.Exp, [bsm, bnsp], [bsm], scale=nsp[:, r, 0:1])
                        act(sm[:, 6, :], sm[:, 3, :], AF.Exp, [bsm, bnsp], [bsm], scale=nsp[:, r, 1:2])
                        ts("vector", sm[:, 6, :], sm[:, 6, :], -1.0, 1.0, ALU.mult, ALU.add, [bsm], [bsm])
                        ts("vector", sm[:, 6, :], sm[:, 6, :], 1e-30, None, ALU.max, None, [bsm], [bsm])
                        act(sm[:, 6, :], sm[:, 6, :], AF.Sqrt, [bsm], [bsm])
                        tt("vector", sm[:, 6, :], sm[:, 6, :], sm[:, 4, :], ALU.mult, [bsm], [bsm])
                        tt("vector", sm[:, 6, :], sm[:, 6, :], sm[:, 2, :], ALU.mult, [bsm], [bsm])
                        tt("vector", sm[:, 7, :], sm[:, 5, :], sht[:, r, :], ALU.mult, [bsm, bsh], [bsm])
                        tt("vector", sm[:, 7, :], sm[:, 7, :], sm[:, 6, :], ALU.add, [bsm], [bsm])
                        tt("vector", mstg[:, TOWN:TTOK], sm[:, 7, :], sm[:, 1, :], ALU.mult, [bsm], [bmstg])
                        dma("sync", mix_d[r], mstg[:], [bmstg], [bmix[r]])
                        cp("vector", hsst[:, r, :], sm[:, 7, :], [bsm], [bhss])
                        cp("vector", cvsst[:, r, 0, :], sconv[:, r, 1, :], [bsc], [bcvs])
                        cp("vector", cvsst[:, r, 1, :], sconv[:, r, 2, :], [bsc], [bcvs])
                        cp("vector", cvsst[:, r, 2, :], sm[:, 0, :], [bsm], [bcvs])
                    dma("sync", conv_o, cst[:], [bcst], (), final=True)
                    dma("sync", h_o, hst[:], [bhst], (), final=True)
                    dma("sync", convs_o, cvsst[:], [bcvs], (), final=True)
                    dma("sync", hs_o, hsst[:], [bhss], (), final=True)
                    P.barrier()
                    P.flush()

            if do_att:
                esa = contextlib.ExitStack()
                with esa:
                    wq3 = T(esa, "wq3", [128, 16, 384], BF16); bw3 = Buf("wq3")
                    rcos = T(esa, "rcos", [128, 16, 32], F32); brc = Buf("rcos")
                    rsin = T(esa, "rsin", [128, 16, 2, 32], F32); brs = Buf("rsin")
                    rsam = T(esa, "rsam", [NS, 96], F32); brsm = Buf("rsam")
                    cmask = T(esa, "cmask", [128, 128], F32); bcm = Buf("cmask")
                    lamv = T(esa, "lamv", [128, 4, 64], F32); blv = Buf("lamv")
                    lam = T(esa, "lam", [128, 4], F32); blam = Buf("lam")
                    gsc = T(esa, "gsc", [128, 128], F32); bgsc = Buf("gsc")
                    t1 = T(esa, "t1", [128, 256], F32); bt1 = Buf("t1")
                    t2 = T(esa, "t2", [128, 256], F32); bt2 = Buf("t2")
                    kb16 = T(esa, "kb16", [128, 16, 128], BF16); bkb = Buf("kb16")
                    qb16 = T(esa, "qb16", [128, 8, 128], BF16); bqb = Buf("qb16")
                    vb = T(esa, "vb", [128, 16, 128], BF16); bvb = Buf("vb")
                    kst = T(esa, "kst", [128, 8, 128], F32); bkst = Buf("kst")
                    vst = T(esa, "vst", [128, 8, 128], F32); bvst = Buf("vst")
                    kT = [T(esa, f"kT{m}", [65, NSLOT], BF16) for m in range(2)]; bkT = [Buf("kT0"), Buf("kT1")]
                    qT = [T(esa, f"qT{m}", [65, TOWN], BF16) for m in range(2)]; bqT = [Buf("qT0"), Buf("qT1")]
                    Pp = [T(esa, f"Pp{i}", [128, 512], BF16) for i in range(3)]; bPp = [Buf(f"Pp{i}") for i in range(3)]
                    PTp = [T(esa, f"PTp{i}", [128, 4, 128], BF16) for i in range(3)]; bPTp = [Buf(f"PTp{i}") for i in range(3)]
                    cmaskb = T(esa, "cmaskb", [128, 128], BF16); bcmb = Buf("cmaskb")
                    nrm = T(esa, "nrm", [128, 16, 4], F32); bnrm = Buf("nrm")
                    nbq = T(esa, "nbq", [128, 8, 2], F32); bnbq = Buf("nbq")
                    kmx = T(esa, "kmx", [128, 4], F32); bkmx = Buf("kmx")
                    kmx2 = T(esa, "kmx2", [2, 132], F32); bkmx2 = Buf("kmx2")
                    zc = T(esa, "zc", [128, 2, 4], F32); bzc = Buf("zc")
                    bSp = [Buf(f"psS{i}") for i in range(4)]
                    bTh1 = Buf("psT_h1")
                    acnt = [0, 0, 0]
                    st = T(esa, "st", [128, 16], F32); bst = Buf("st")
                    att = T(esa, "att", [128, 128], F32); batt = Buf("att")
                    att2 = T(esa, "att2", [128, 128], F32); batt2 = Buf("att2")
                    attb = T(esa, "attb", [128, 128], BF16); battb = Buf("attb")
                    junk = T(esa, "junk", [128, 128], F32); bjunk = Buf("junk")
                    ksst = T(esa, "ksst", [NS, 2, 128], F32); bkss = Buf("ksst")
                    vsst = T(esa, "vsst", [NS, 2, 128], F32); bvss = Buf("vsst")
                    qs = T(esa, "qs", [NS, 8, 128], F32); bqs = Buf("qs")
                    msts = T(esa, "msts", [128, 8, NS], BF16); bmsts = Buf("msts")
                    ts1 = T(esa, "ts1", [NS, 256], F32); bts1 = Buf("ts1")
                    ts2 = T(esa, "ts2", [NS, 256], F32); bts2 = Buf("ts2")
                    ptT = T(esa, "ptT", [128, NS], I32); bpt = Buf("ptT")
                    ptf = T(esa, "ptf", [128, NS], F32); bptf = Buf("ptf")
                    ptc2 = T(esa, "ptc2", [128, NS, 4], I32); bptc = Buf("ptc")
                    sel128 = T(esa, "sel128", [NS, NS * 128], F32); bsel = Buf("sel128")
                    ones128 = T(esa, "ones128", [128, 128], F32); bon = Buf("ones128")
                    CH = 16
                    NCH = 128 // CH
                    NKT = 4
                    Kt = [T(esa, f"Kt{i}", [128, CH * 128], F32) for i in range(NKT)]; bKt = [Buf(f"Kt{i}") for i in range(NKT)]
                    qbc = T(esa, "qbc", [128, NS, 128], F32); bqbc = Buf("qbc")
                    knew = T(esa, "knew", [1, NS, 128], F32); bknew = Buf("knew")
                    vnew = T(esa, "vnew", [1, NS, 128], F32); bvnew = Buf("vnew")
                    vnewb = T(esa, "vnewb", [1, NS, 128], BF16); bvnewb = Buf("vnewb")
                    sc = T(esa, "sc", [128, NS, 65, 2], F32); bsc2 = Buf("sc")
                    ee = sc; bee = bsc2
                    wv = T(esa, "wv", [128, 65], F32); bwv = Buf("wv")
                    Bz = T(esa, "Bz", [128, NS, 65, 7], BF16); bBz = Buf("Bz")
                    Vb = [T(esa, f"Vb{i}", [128, CH * 128], BF16) for i in range(2)]; bVb = [Buf("Vb0"), Buf("Vb1")]
                    dsm = T(esa, "dsm", [128, 40], F32); bdsm = Buf("dsm")
                    dsm2 = T(esa, "dsm2", [8, 136], F32); bdsm2 = Buf("dsm2")
                    kcnt = [0]

                    dma("sync", rcos[:], rcos_d, (), [brc])
                    dma("sync", rsin[:], rsin_d, (), [brs])
                    dma("sync", rsam[:], rsam_d, (), [brsm])
                    dma("sync", cmask[:], cmask_d, (), [bcm])
                    dma("gpsimd", cmaskb[:], cmask_d, (), [bcmb])
                    for m in range(2):
                        dma("gpsimd", kT[m][64:65, :], pbrow_d, (), [bkT[m]])
                        memset("gpsimd", qT[m][64:65, :], 1.0, [bqT[m]])
                    memset("vector", nrm[:], 0.0, [bnrm])
                    dma("sync", lamv[:], lamv_d, (), [blv])
                    dma("sync", gsc[:], subg_d, (), [bgsc])
                    dma("sync", ptT[:], pt_d, (), [bpt])
                    dma("sync", sel128[:], sel4_d, (), [bsel])
                    dma("sync", ones128[:], ones_d, (), [bon])
                    tt("vector", t1[:, 0:64], lamv[:, 0, :], lamv[:, 1, :], ALU.mult, [blv], [bt1])
                    tt("vector", t1[:, 64:128], lamv[:, 2, :], lamv[:, 3, :], ALU.mult, [blv], [bt1])
                    red("vector", lam[:, 2:4], t1[:, 0:128].rearrange("p (a b) -> p a b", a=2), ALU.add, [bt1], [blam])
                    act(lam[:, 2:4], lam[:, 2:4], AF.Exp, [blam], [blam])
                    tt("vector", lam[:, 0:1], lam[:, 2:3], lam[:, 3:4], ALU.subtract, [blam], [blam])
                    ts("vector", lam[:, 0:1], lam[:, 0:1], LAM_INIT, None, ALU.add, None, [blam], [blam])
                    ts("vector", lam[:, 1:2], lam[:, 0:1], -1.0, None, ALU.mult, None, [blam], [blam])
                    ts("vector", gsc[:], gsc[:], 1.0 - LAM_INIT, None, ALU.mult, None, [bgsc], [bgsc])
                    cp("vector", ptf[:], ptT[:], [bpt], [bptf])
                    for cq in range(4):
                        ts("vector", ptc2[0:64, :, cq], ptf[0:64, :], 8.0, float(2 * cq), ALU.mult, ALU.add, [bptf], [bptc])
                        ts("vector", ptc2[64:128, :, cq], ptf[64:128, :], 8.0, float(2 * cq + 1), ALU.mult, ALU.add, [bptf], [bptc])
                    for i in range(NKT):
                        memset("vector", Kt[i][:], 0.0, [bKt[i]])
                    memset("vector", Bz[:], 0.0, [bBz])
                    memset("vector", knew[:], 0.0, [bknew])

                    def rope(ps, G, cos_ap, sin0_ap, sin1_ap, np_, o1, o2, bo1, bo2, rdeps):
                        pv = ps[0:np_, 0:G * 64].rearrange("p (g h f) -> p g h f", g=G, h=2)
                        o1v = o1[0:np_, 0:G * 64].rearrange("p (g h f) -> p g h f", g=G, h=2)
                        o2v = o2[0:np_, 0:G * 64].rearrange("p (g h f) -> p g h f", g=G, h=2)
                        cb = cos_ap.unsqueeze(1).unsqueeze(1).to_broadcast([np_, G, 2, 32])
                        tt("vector", o1v, pv, cb, ALU.mult, rdeps, [bo1])
                        tt("vector", o2v[:, :, 0, :], pv[:, :, 1, :], sin0_ap.unsqueeze(1).to_broadcast([np_, G, 32]), ALU.mult, rdeps, [bo2])
                        tt("vector", o2v[:, :, 1, :], pv[:, :, 0, :], sin1_ap.unsqueeze(1).to_broadcast([np_, G, 32]), ALU.mult, rdeps, [bo2])
                        tt("vector", o1[0:np_, 0:G * 64], o1[0:np_, 0:G * 64], o2[0:np_, 0:G * 64], ALU.add, [bo1, bo2], [bo1])

                    def rmsnorm_to_mix(src_ps_or_sb, np_, h, col0, ncol, rdeps, dest=None, bdest=None):
                        act(junk[0:np_, :], att[0:np_, :], AF.Square, [batt], [bjunk, bst], accum=st[0:np_, 8:9])
                        act(st[0:np_, 9:10], st[0:np_, 8:9], AF.Sqrt, [bst], [bst], bias=EPS, scale=1.0 / 128.0)
                        recip(st[0:np_, 10:11], st[0:np_, 9:10], [bst], [bst])
                        stt(attb[0:np_, :], att[0:np_, :], st[0:np_, 10:11], gsc[0:np_, :], ALU.mult, ALU.mult, [batt, bst, bgsc], [battb])
                        tr(psT[:, 0:np_], attb[0:np_, :], identb[0:np_, 0:np_], [battb, bidb], [bT])
                        if dest is None:
                            cp("scalar", mstg[:, col0:col0 + ncol], psT[:, 0:ncol], [bT], [bmstg])
                        else:
                            cp("scalar", dest, psT[:, 0:ncol], [bT], [bdest])

                    def dec_gen(h):
                        NCP = NCH // 2
                        for s in range(NS):
                            mm(psC[:, 0:128], sel128[:, s * 128:(s + 1) * 128], qs[:, h, :], True, True, [bsel, bqs], [bC])
                            cp("scalar", qbc[:, s, :], psC[:, 0:128], [bC], [bqbc])
                            dma("sync", knew[0:1, s, :], ksst[s:s + 1, h % 2, :], [bkss], [bknew])
                            dma("sync", vnew[0:1, s, :], vsst[s:s + 1, h % 2, :], [bvss], [bvnew])
                        cp("scalar", vnewb[:], vnew[:], [bvnew], [bvnewb])
                        memset("gpsimd", sc[:, :, 64, :], -1e30, [bsc2])
                        for s in range(NS):
                            for cq in range(NCP):
                                bi = kcnt[0] % NKT
                                kcnt[0] += 1
                                idma(Kt[bi][:, :], bass.AP(tensor=ck_d.tensor, offset=0, ap=[[CH * 128, npool * 8], [1, CH * 128]]), ptc2[:, s, cq:cq + 1],
                                     [bptc], [bKt[bi]], eoff=h * npool * 16384)
                                kv = Kt[bi][:, :].rearrange("p (t f) -> p t f", t=CH)
                                tt("vector", kv, kv, qbc[:, s, :].unsqueeze(1).to_broadcast([128, CH, 128]), ALU.mult, [bKt[bi], bqbc], [bKt[bi]])
                                red("vector", sc[:, s, cq * CH:(cq + 1) * CH, :].rearrange("p t m -> p (t m)"),
                                    Kt[bi][:, :].rearrange("p (a d) -> p a d", d=64), ALU.add, [bKt[bi]], [bsc2])
                                yield 1
                        tt("vector", knew[0:1, :, :], knew[0:1, :, :], qbc[0:1, :, :], ALU.mult, [bknew, bqbc], [bknew])
                        red("vector", sc[0:1, :, 64, :], knew[0:1, :, :].rearrange("p s (m d) -> p s m d", d=64), ALU.add, [bknew], [bsc2])
                        red("vector", dsm[:, 0:8].rearrange("p (s m) -> p s m", m=2), sc[:].rearrange("p s t m -> p s m t"), ALU.max, [bsc2], [bdsm])
                        P.op("tensor", lambda e: e.transpose(out=psC[0:8, 128:256], in_=dsm[:, 0:8], identity=identf[:]), [bdsm, bidf], [bC])
                        red("vector", dsm2[:, 0:1], psC[0:8, 128:256], ALU.max, [bC], [bdsm2])
                        cp("vector", dsm2[:, 8:136], dsm2[:, 0:1].to_broadcast([8, 128]), [bdsm2], [bdsm2])
                        P.op("tensor", lambda e: e.transpose(out=psC[:, 256:264], in_=dsm2[:, 8:136], identity=identf[0:8, 0:8]), [bdsm2, bidf], [bC])
                        cp("vector", dsm[:, 8:16], psC[:, 256:264], [bC], [bdsm])
                        tt("vector", sc[:], sc[:], dsm[:, 8:16].rearrange("p (s m) -> p s m", m=2).unsqueeze(2).to_broadcast([128, NS, 65, 2]), ALU.subtract,
                           [bsc2, bdsm], [bsc2])
                        act(sc[:], sc[:], AF.Exp, [bsc2], [bsc2], scale=0.125)
                        red("vector", dsm[:, 16:24].rearrange("p (s m) -> p s m", m=2), sc[:].rearrange("p s t m -> p s m t"), ALU.add, [bsc2], [bdsm])
                        mm(psC[:, 320:328], ones128[:], dsm[:, 16:24], True, True, [bon, bdsm], [bC])
                        recip(dsm[:, 24:32], psC[:, 320:328], [bC], [bdsm])
                        rzv = dsm[:, 24:32].rearrange("p (s m) -> p s m", m=2)
                        ts("vector", dsm[:, 32:36], rzv[:, :, 1], lam[:, 1:2], None, ALU.mult, None, [bdsm, blam], [bdsm])
                        for s in range(NS):
                            ts("vector", wv[:], sc[:, s, :, 0], rzv[:, s, 0:1], None, ALU.mult, None, [bsc2, bdsm], [bwv])
                            stt(Bz[:, s, :, 3], sc[:, s, :, 1], dsm[:, 32 + s:33 + s], wv[:], ALU.mult, ALU.add, [bsc2, bdsm, bwv], [bBz])
                        yield 1
                        first_pv = True
                        for s in range(NS):
                            for cq in range(NCP):
                                bi = kcnt[0] % NKT
                                kcnt[0] += 1
                                idma(Kt[bi][:, :], bass.AP(tensor=cv_d.tensor, offset=0, ap=[[CH * 128, npool * 8], [1, CH * 128]]), ptc2[:, s, cq:cq + 1],
                                     [bptc], [bKt[bi]], eoff=h * npool * 16384)
                                vi = bi % 2
                                cp("scalar", Vb[vi][:], Kt[bi][:], [bKt[bi]], [bVb[vi]])
                                for t in range(CH):
                                    mm(psB[0:NS, 0:128], Bz[:, s, cq * CH + t, 3 - s:7 - s], Vb[vi][:, t * 128:(t + 1) * 128], first_pv, False, [bBz, bVb[vi]], [bB])
                                    first_pv = False
                                yield 1
                            mm(psB[0:NS, 0:128], Bz[0:1, s, 64, 3 - s:7 - s], vnewb[0:1, s, :], False, s == NS - 1, [bBz, bvnewb], [bB])
                        cp("vector", att[0:NS, :], psB[0:NS, 0:128], [bB], [batt])
                        rmsnorm_to_mix(None, NS, h, TOWN, NS, None, dest=msts[:, h, :], bdest=bmsts)
                        yield 1

                    pending = None
                    for h in range(8):
                        qc, kc, vc = 2048 + h * 128, 3072 + h * 128, 4096 + h * 128
                        for ci, c0 in enumerate((qc, kc, vc)):
                            dma("gpsimd", wq3[:, :, ci * 128:(ci + 1) * 128], w_in_d[:, c0:c0 + 128].rearrange("(k p) c -> p k c", p=128), (), [bw3])
                        for tti in range(16):
                            own = tti >= 8
                            ps, bp = psrot[tti % 3]
                            if own:
                                for k in range(16):
                                    mm(ps[:, 0:384], xT[:, k, tti * 128:(tti + 1) * 128], wq3[:, k, 0:384], k == 0, k == 15, [bxT[k], bw3], [bp])
                                rope(ps, 4, rcos[:, tti, :], rsin[:, tti, 0, :], rsin[:, tti, 1, :], 128, t1, t2, bt1, bt2, [bp, brc, brs])
                                tt("gpsimd", t2[:, 0:256], t1[:, 0:256], t1[:, 0:256], ALU.mult, [bt1, bt2], [bt2])
                                red("vector", nrm[:, tti, 0:4], t2[:, 0:256].rearrange("p (g d) -> p g d", d=64), ALU.add, [bt2], [bnrm])
                                cp("scalar", qb16[:, tti - 8, :], t1[:, 0:128], [bt1], [bqb])
                                cp("scalar", kb16[:, tti, :], t1[:, 128:256], [bt1], [bkb])
                                cp("gpsimd", kst[:, tti - 8, :], t1[:, 128:256], [bt1], [bkst])
                                cp("scalar", vb[:, tti, :], ps[:, 256:384], [bp], [bvb])
                                cp("scalar", vst[:, tti - 8, :], ps[:, 256:384], [bp], [bvst])
                            else:
                                for k in range(16):
                                    mm(ps[:, 0:256], xT[:, k, tti * 128:(tti + 1) * 128], wq3[:, k, 128:384], k == 0, k == 15, [bxT[k], bw3], [bp])
                                rope(ps, 2, rcos[:, tti, :], rsin[:, tti, 0, :], rsin[:, tti, 1, :], 128, t1, t2, bt1, bt2, [bp, brc, brs])
                                tt("gpsimd", t2[:, 0:128], t1[:, 0:128], t1[:, 0:128], ALU.mult, [bt1, bt2], [bt2])
                                red("vector", nrm[:, tti, 2:4], t2[:, 0:128].rearrange("p (g d) -> p g d", d=64), ALU.add, [bt2], [bnrm])
                                cp("scalar", kb16[:, tti, :], t1[:, 0:128], [bt1], [bkb])
                                cp("scalar", vb[:, tti, :], ps[:, 128:256], [bp], [bvb])
                        dma("sync", k_o[:, h * 128:(h + 1) * 128].rearrange("(n p) f -> p n f", p=128), kst[:], [bkst], (), final=True)
                        dma("sync", v_o[:, h * 128:(h + 1) * 128].rearrange("(n p) f -> p n f", p=128), vst[:], [bvst], (), final=True)
                        for m in range(2):
                            for g in range(2):
                                for j in range(8):
                                    tr(psT[0:64, j * 128:(j + 1) * 128], kb16[:, g * 8 + j, m * 64:(m + 1) * 64], identb[:], [bkb, bidb], [bT, bTh1])
                                cp("vector" if g == 0 else "scalar", kT[m][0:64, g * 1024:(g + 1) * 1024], psT[0:64, :], [bT, bTh1], [bkT[m]])
                            for j in range(8):
                                tr(psT[0:64, j * 128:(j + 1) * 128], qb16[:, j, m * 64:(m + 1) * 64], identb[:], [bqb, bidb], [bT, bTh1])
                            cp("vector", qT[m][0:64, :], psT[0:64, :], [bT, bTh1], [bqT[m]])
                        red("vector", kmx[:, 0:2], nrm[:, :, 2:4].rearrange("p t m -> p m t"), ALU.max, [bnrm], [bkmx])
                        P.op("tensor", lambda e: e.transpose(out=psC[0:2, 0:128], in_=kmx[:, 0:2], identity=identf[:]), [bkmx, bidf], [bC])
                        red("vector", kmx2[:, 0:1], psC[0:2, 0:128], ALU.max, [bC], [bkmx2])
                        cp("vector", kmx2[:, 4:132], kmx2[:, 0:1].to_broadcast([2, 128]), [bkmx2], [bkmx2])
                        P.op("tensor", lambda e: e.transpose(out=psC[:, 128:130], in_=kmx2[:, 4:132], identity=identf[0:2, 0:2]), [bkmx2, bidf], [bC])
                        cp("vector", kmx[:, 2:4], psC[:, 128:130], [bC], [bkmx])
                        tt("vector", nbq[:], nrm[:, 8:16, 0:2], kmx[:, 2:4].unsqueeze(1).to_broadcast([128, 8, 2]), ALU.mult, [bnrm, bkmx], [bnbq])
                        act(nbq[:], nbq[:], AF.Sqrt, [bnbq], [bnbq])
                        ts("vector", nbq[:], nbq[:], -0.125, None, ALU.mult, None, [bnbq], [bnbq])
                        for i in range(8):
                            nk = 1024 + (i + 1) * 128
                            pieces = [(c0, min(512, nk - c0)) for c0 in range(0, nk, 512)]
                            npc = len(pieces)
                            for m in range(2):
                                for pi, (c0, w_) in enumerate(pieces):
                                    sb = acnt[0] % 4
                                    acnt[0] += 1
                                    lastp = (pi == npc - 1)
                                    S = psS[:, sb * 512:sb * 512 + w_]
                                    mm(S, qT[m][:, i * 128:(i + 1) * 128], kT[m][:, c0:c0 + w_], True, not lastp, [bqT[m], bkT[m]], [bSp[sb]])
                                    if lastp:
                                        mm(psS[:, sb * 512 + w_ - 128:sb * 512 + w_], identb[:], cmaskb[:], False, True, [bidb, bcmb], [bSp[sb]])
                                    pb_ = acnt[1] % 3
                                    acnt[1] += 1
                                    act(Pp[pb_][:, 0:w_], S, AF.Exp, [bSp[sb], bnbq], [bPp[pb_], bzc], bias=nbq[:, i, m:m + 1], scale=0.125, accum=zc[:, m, pi:pi + 1])
                                    nblk = w_ // 128
                                    tb = acnt[2] % 2
                                    acnt[2] += 1
                                    btb = bT if tb == 0 else bTh1
                                    for j in range(nblk):
                                        tr(psT[:, tb * 512 + j * 128:tb * 512 + (j + 1) * 128], Pp[pb_][:, j * 128:(j + 1) * 128], identb[:], [bPp[pb_], bidb], [btb])
                                    cp("scalar" if (acnt[2] % 3 == 0) else "vector", PTp[pb_][:, 0:nblk, :],
                                       psT[:, tb * 512:tb * 512 + nblk * 128].rearrange("p (a b) -> p a b", a=nblk), [btb], [bPTp[pb_]])
                                    for j in range(nblk):
                                        kb = c0 // 128 + j
                                        mm(psA[:, m * 128:(m + 1) * 128], PTp[pb_][:, j, :], vb[:, kb, :], (pi == 0 and j == 0), (lastp and j == nblk - 1),
                                           [bPTp[pb_], bvb], [bA])
                            red("vector", st[:, 5:7], zc[:, :, 0:npc], ALU.add, [bzc], [bst])
                            recip(st[:, 11:13], st[:, 5:7], [bst], [bst])
                            ts("vector", att2[:], psA[:, 0:128], st[:, 11:12], None, ALU.mult, None, [bA, bst], [batt2])
                            tt("vector", st[:, 13:14], st[:, 12:13], lam[:, 1:2], ALU.mult, [bst, blam], [bst])
                            stt(att[:], psA[:, 128:256], st[:, 13:14], att2[:], ALU.mult, ALU.add, [bA, bst, batt2], [batt])
                            rmsnorm_to_mix(None, 128, h, i * 128, 128, None)
                            if pending is not None:
                                for _ in range(10):
                                    if next(pending, "done") == "done":
                                        pending = None
                                        break

                        ps, bp = psrot[2]
                        for k in range(16):
                            mm(ps[0:NS, 0:384], xsT[:, k, :], wq3[:, k, 0:384], k == 0, k == 15, [bxs, bw3], [bp])
                        rope(ps, 4, rsam[:, 0:32], rsam[:, 32:64], rsam[:, 64:96], NS, ts1, ts2, bts1, bts2, [bp, brsm])
                        cp("vector", qs[:, h, :], ts1[:, 0:128], [bts1], [bqs])
                        cp("vector", ksst[:, h % 2, :], ts1[:, 128:256], [bts1], [bkss])
                        cp("vector", vsst[:, h % 2, :], ps[0:NS, 256:384], [bp], [bvss])
                        dma("sync", ks_o[:, h * 128:(h + 1) * 128], ksst[:, h % 2, :], [bkss], (), final=True)
                        dma("sync", vs_o[:, h * 128:(h + 1) * 128], vsst[:, h % 2, :], [bvss], (), final=True)
                        if do_dec:
                            for _ in dec_gen(h):
                                pass
                        dma("sync", mix_d[8 + h][:, 0:TOWN], mstg[:, 0:TOWN], [bmstg], [bmix[8 + h]])
                    if pending is not None:
                        for _ in pending:
                            pass
                    if not do_dec:
                        memset("vector", msts[:], 0.0, [bmsts])
                    with nc.allow_non_contiguous_dma(reason="tiny sample columns"):
                        for hh in range(8):
                            dma("sync", mix_d[8 + hh][:, TOWN:TTOK], msts[:, hh, :], [bmsts], [bmix[8 + hh]])
                    conv_steps(100000)
                    P.barrier()
                    P.flush()
        if dbg:
            esd = contextlib.ExitStack()
            with esd:
                mixb = T(esd, "mixb", [128, TTOK], BF16); bmb = Buf("mixb")
                mixf = T(esd, "mixf", [128, TTOK], F32); bmf = Buf("mixf")
                for kk in range(16):
                    dma("sync", mixb[:], mix_d[kk], [bmix[kk]], [bmb])
                    cp("vector", mixf[:], mixb[:], [bmb], [bmf])
                    dma("sync", mix_o[:, kk, :], mixf[:], [bmf], (), final=True)
                P.barrier()
                P.flush()

        NT = 9
        bx1d = [Buf(f"x1d{t}") for t in range(NT)]
        if do_p2:
            es2 = contextlib.ExitStack()
            with es2:
                lnp = T(es2, "lnp", [128, 2, D], F32); blnp = Buf("lnp")
                xt = T(es2, "xt", [128, D], F32); bxt = Buf("xt")
                st2 = T(es2, "st2", [128, 16], F32); bst2 = Buf("st2")
                E = T(es2, "E", [128, NT, 128], I32); bE = [Buf(f"E{t}") for t in range(NT)]
                G = T(es2, "G", [128, NT, 128], F32); bG = [Buf(f"G{t}") for t in range(NT)]

                def layer_norm(buf_ap, bbuf, np_):
                    red("vector", st2[0:np_, 0:1], buf_ap, ALU.add, [bbuf], [bst2])
                    ts("vector", st2[0:np_, 1:2], st2[0:np_, 0:1], -1.0 / D, None, ALU.mult, None, [bst2], [bst2])
                    ts("vector", buf_ap, buf_ap, st2[0:np_, 1:2], None, ALU.add, None, [bbuf, bst2], [bbuf])
                    act(xt[0:np_, :], buf_ap, AF.Square, [bbuf], [bxt, bst2], accum=st2[0:np_, 2:3])
                    act(st2[0:np_, 3:4], st2[0:np_, 2:3], AF.Sqrt, [bst2], [bst2], bias=EPS, scale=1.0 / D)
                    recip(st2[0:np_, 4:5], st2[0:np_, 3:4], [bst2], [bst2])
                    stt(buf_ap, buf_ap, st2[0:np_, 4:5], lnp[0:np_, 0, :], ALU.mult, ALU.mult, [bbuf, bst2, blnp], [bbuf])
                    tt("gpsimd", buf_ap, buf_ap, lnp[0:np_, 1, :], ALU.add, [bbuf, blnp], [bbuf])

                esA = contextlib.ExitStack()
                with esA:
                    wbuf = T(esA, "wbufA", [128, 16, D], BF16); bwb = Buf("wbufA")
                    mixt = [T(esA, f"mixt{i}", [128, 16, 128], BF16) for i in range(2)]; bmt = [Buf("mixt0"), Buf("mixt1")]
                    x1t = [T(esA, f"x1tA{i}", [128, D], F32) for i in range(2)]; bx1t = [Buf("x1tA0"), Buf("x1tA1")]
                    xin = [T(esA, f"xin{i}", [128, D], F32) for i in range(2)]; bxin = [Buf("xin0"), Buf("xin1")]
                    for k in range(16):
                        dma("gpsimd", wbuf[:, k, :], w_out_d[k * 128:(k + 1) * 128, :], (), [bwb])
                    dma("sync", lnp[:], ln_d[:, 0:2, :], (), [blnp])
                    for t in range(NT):
                        np_ = 128 if t < 8 else NS
                        bi = t % 2
                        dma("sync", xin[bi][0:np_, :], xtok_d[t * 128:t * 128 + np_, :], (), [bxin[bi]])
                        dma("sync", mixt[bi][:, :, 0:np_], mix_d[:, :, t * 128:t * 128 + np_].rearrange("k p t -> p k t"), bmix, [bmt[bi]])
                        for nb in range(4):
                            ps, bp = psrot[nb % 3]
                            for kk in range(16):
                                mm(ps[0:np_, 0:512], mixt[bi][:, kk, 0:np_], wbuf[:, kk, nb * 512:(nb + 1) * 512], kk == 0, kk == 15,
                                   [bmt[bi], bwb], [bp])
                            stt(x1t[bi][0:np_, nb * 512:(nb + 1) * 512], xin[bi][0:np_, nb * 512:(nb + 1) * 512], ALPHA, ps[0:np_, 0:512], ALU.mult, ALU.add,
                                [bxin[bi], bp], [bx1t[bi]])
                        layer_norm(x1t[bi][0:np_, :], bx1t[bi], np_)
                        dma("sync", x1_d[t * 128:t * 128 + np_, :], x1t[bi][0:np_, :], [bx1t[bi]], [bx1d[t]])
                        if dbg:
                            dma("sync", x1_o[t * 128:t * 128 + np_, :], x1t[bi][0:np_, :], [bx1t[bi]], (), final=True)
                    P.barrier()
                    P.flush()

                if do_peer:
                    esB = contextlib.ExitStack()
                    with esB:
                        wbuf = T(esB, "wbufB", [128, 16, D], BF16); bwb = Buf("wbufB")
                        k1T = T(esB, "k1T", [128, 8, 128], F32); bk1 = Buf("k1T")
                        k2T = T(esB, "k2T", [128, 8, 128], F32); bk2 = Buf("k2T")
                        iota16 = T(esB, "iota16", [128, 16, 16], F32); bio = Buf("iota")
                        x1f = T(esB, "x1f", [128, D], F32); bx1f = Buf("x1f")
                        x1T = T(esB, "x1T", [128, 16, 128], BF16); bx1T = Buf("x1T")
                        x1b = T(esB, "x1b", [128, D], BF16); bx1b = Buf("x1b")
                        qTf = T(esB, "qTf", [128, 16, 128], F32); bqTf = Buf("qTf")
                        S12 = T(esB, "S12", [128, 16, 128], F32); bS12 = Buf("S12")
                        wk = T(esB, "wk", [128, 256], F32); bwk = Buf("wk")
                        V12 = T(esB, "V12", [128, 16, 16], F32); bV12 = Buf("V12")
                        I12 = T(esB, "I12", [128, 16, 16], U32); bI12 = Buf("I12")
                        I12f = T(esB, "I12f", [128, 16, 16], F32); bI12f = Buf("I12f")
                        comb = T(esB, "comb", [128, 8, 256], F32); bcomb = Buf("comb")
                        sv = T(esB, "sv", [128, 8, 16], F32); bsv = Buf("sv")
                        svx = T(esB, "svx", [128, 8, 16], F32); bsvx = Buf("svx")
                        si = T(esB, "si", [128, 8, 16], U32); bsi = Buf("si")
                        sif = T(esB, "sif", [128, 8, 16], F32); bsif = Buf("sif")
                        sab = T(esB, "sab", [128, 2, 8, 16], F32); bsab = Buf("sab")
                        oh = T(esB, "oh", [128, 8, 16, 16], F32); boh = Buf("oh")
                        isel = T(esB, "isel", [128, 2, 8, 16], F32); bisel = Buf("isel")
                        ef = T(esB, "ef", [128, 128], F32); bef = Buf("ef")

                        for k in range(16):
                            dma("gpsimd", wbuf[:, k, :], wq_d[k * 128:(k + 1) * 128, :], (), [bwb])
                        dma("sync", k1T[:], k1T_d, (), [bk1])
                        dma("sync", k2T[:], k2T_d, (), [bk2])
                        dma("sync", iota16[:], iota_d.rearrange("p (a b) -> p a b", a=16), (), [bio])
                        memset("vector", x1f[:], 0.0, [bx1f])

                        def bc4(ap3, h0):
                            return ap3.unsqueeze(1).to_broadcast([128, 4, 16, 16])

                        for t in range(NT):
                            np_ = 128 if t < 8 else NS
                            dma("sync", x1f[0:np_, :], x1_d[t * 128:t * 128 + np_, :], [bx1d[t]], [bx1f])
                            cp("scalar", x1b[:], x1f[:], [bx1f], [bx1b])
                            for g in range(2):
                                for j in range(8):
                                    tr(psT[:, j * 128:(j + 1) * 128], x1b[:, (g * 8 + j) * 128:(g * 8 + j + 1) * 128], identb[:], [bx1b, bidb], [bT])
                                cp("vector", x1T[:, g * 8:(g + 1) * 8, :], psT[:, :].rearrange("p (a b) -> p a b", a=8), [bT], [bx1T])
                            for c in range(16):
                                ps, bp = psrot[c % 3]
                                for kk in range(16):
                                    mm(ps[:, 0:128], wbuf[:, kk, c * 128:(c + 1) * 128], x1T[:, kk, :], kk == 0, kk == 15, [bwb, bx1T], [bp])
                                cp("scalar" if c % 2 == 0 else "vector", qTf[:, c, :], ps[:, 0:128], [bp], [bqTf])
                            for c in range(16):
                                hh, half = c // 2, c % 2
                                ps, bp = psrot[c % 3]
                                kk_ap = k1T[:, hh, :] if half == 0 else k2T[:, hh, :]
                                mm(ps[:, 0:128], qTf[:, c, :], kk_ap, True, True, [bqTf, bk1, bk2], [bp])
                                cp("scalar" if c % 2 == 0 else "vector", S12[:, c, :], ps[:, 0:128], [bp], [bS12])
                            for c in range(16):
                                P.op("vector", lambda e, c=c: e.max(out=V12[:, c, 0:8], in_=S12[:, c, :]), [bS12], [bV12])
                                P.op("vector", lambda e, c=c: e.max_index(out=I12[:, c, 0:8], in_max=V12[:, c, 0:8], in_values=S12[:, c, :]), [bS12, bV12], [bI12])
                                P.op("vector", lambda e, c=c: e.match_replace(out=wk[:, 0:128], in_to_replace=V12[:, c, 0:8], in_values=S12[:, c, :], imm_value=-1e30),
                                     [bS12, bV12], [bwk])
                                P.op("vector", lambda e, c=c: e.max(out=V12[:, c, 8:16], in_=wk[:, 0:128]), [bwk], [bV12])
                                P.op("vector", lambda e, c=c: e.max_index(out=I12[:, c, 8:16], in_max=V12[:, c, 8:16], in_values=wk[:, 0:128]), [bwk, bV12], [bI12])
                            cp("vector", I12f[:], I12[:], [bI12], [bI12f])
                            V4 = V12[:].rearrange("p (h two) k -> p h two k", two=2)
                            I4 = I12f[:].rearrange("p (h two) k -> p h two k", two=2)
                            cv4 = comb[:].rearrange("p h (a b) -> p h a b", a=16)
                            for hq in (0, 4):
                                tt("vector", cv4[:, hq:hq + 4], V4[:, hq:hq + 4, 0, :].unsqueeze(3).to_broadcast([128, 4, 16, 16]),
                                   V4[:, hq:hq + 4, 1, :].unsqueeze(2).to_broadcast([128, 4, 16, 16]), ALU.add, [bV12], [bcomb])
                            for hh in range(8):
                                P.op("vector", lambda e, hh=hh: e.max(out=sv[:, hh, 0:8], in_=comb[:, hh, :]), [bcomb], [bsv])
                                P.op("vector", lambda e, hh=hh: e.max_index(out=si[:, hh, 0:8], in_max=sv[:, hh, 0:8], in_values=comb[:, hh, :]), [bcomb, bsv], [bsi])
                                P.op("vector", lambda e, hh=hh: e.match_replace(out=wk[:], in_to_replace=sv[:, hh, 0:8], in_values=comb[:, hh, :], imm_value=-1e30),
                                     [bcomb, bsv], [bwk])
                                P.op("vector", lambda e, hh=hh: e.max(out=sv[:, hh, 8:16], in_=wk[:]), [bwk], [bsv])
                                P.op("vector", lambda e, hh=hh: e.max_index(out=si[:, hh, 8:16], in_max=sv[:, hh, 8:16], in_values=wk[:]), [bwk, bsv], [bsi])
                            cp("vector", sif[:], si[:], [bsi], [bsif])
                            ts("vector", svx[:], sif[:], 0.0625, -1.0, ALU.mult, ALU.add, [bsif], [bsvx])
                            for hq in (0, 4):
                                tt("vector", oh[:, hq:hq + 4], svx[:, hq:hq + 4, :].unsqueeze(3).to_broadcast([128, 4, 16, 16]), bc4(iota16[:], hq), ALU.is_ge,
                                   [bsvx, bio], [boh])
                            red("vector", sab[:, 0].rearrange("p h j -> p (h j)"), oh[:].rearrange("p h j a -> p (h j) a"), ALU.add, [boh], [bsab])
                            stt(sab[:, 1].rearrange("p h j -> p (h j)"), sab[:, 0].rearrange("p h j -> p (h j)"), -16.0, sif[:].rearrange("p h j -> p (h j)"),
                                ALU.mult, ALU.add, [bsab, bsif], [bsab])
                            for side in range(2):
                                for hq in (0, 4):
                                    tt("vector", oh[:, hq:hq + 4], bc4(iota16[:], hq), sab[:, side, hq:hq + 4, :].unsqueeze(3).to_broadcast([128, 4, 16, 16]),
                                       ALU.is_equal, [bio, bsab], [boh])
                                    tt("vector", oh[:, hq:hq + 4], oh[:, hq:hq + 4], I4[:, hq:hq + 4, side, :].unsqueeze(2).to_broadcast([128, 4, 16, 16]),
                                       ALU.mult, [boh, bI12f], [boh])
                                red("vector", isel[:, side].rearrange("p h j -> p (h j)"), oh[:].rearrange("p h j a -> p (h j) a"), ALU.add, [boh], [bisel])
                            stt(ef[:], isel[:, 0].rearrange("p h j -> p (h j)"), 128.0, isel[:, 1].rearrange("p h j -> p (h j)"), ALU.mult, ALU.add, [bisel], [bef])
                            cp("vector", E[:, t, :], ef[:], [bef], [bE[t]])
                            tt("vector", svx[:], sv[:], sv[:, :, 0:1].to_broadcast([128, 8, 16]), ALU.subtract, [bsv, bsvx], [bsvx])
                            act(svx[:], svx[:], AF.Exp, [bsvx], [bsvx])
                            red("vector", st2[:, 8:16], svx[:], ALU.add, [bsvx], [bst2])
                            recip(st2[:, 8:16], st2[:, 8:16], [bst2], [bst2])
                            tt("vector", G[:, t, :].rearrange("p (h j) -> p h j", h=8), svx[:], st2[:, 8:16].unsqueeze(2).to_broadcast([128, 8, 16]), ALU.mult,
                               [bsvx, bst2], [bG[t]])
                        P.barrier()
                        P.flush()

                    esC = contextlib.ExitStack()
                    with esC:
                        NG = 20
                        gsl = [T(esC, f"gsl{i}", [128, D], BF16) for i in range(NG)]; bg = [Buf(f"g{i}") for i in range(NG)]
                        x1c = [T(esC, f"x1c{i}", [128, D], F32) for i in range(2)]; bx1c = [Buf("x1c0"), Buf("x1c1")]
                        accs = [T(esC, f"acc{i}", [128, D], F32) for i in range(2)]; bacc = [Buf("acc0"), Buf("acc1")]
                        junk = T(esC, "junkC", [128, D], BF16); bjunk = Buf("junkC")
                        dg = [T(esC, f"dg{i}", [128, 128], BF16) for i in range(4)]; bdg = [Buf(f"dg{i}") for i in range(4)]
                        actv = T(esC, "actv", [128, 128], F32); bactv = Buf("actv")
                        coef = T(esC, "coef", [128, 128], F32); bcoef = Buf("coef")
                        dma("sync", lnp[:], ln_d[:, 2:4, :], (), [blnp])
                        gi = 0
                        di = 0
                        for t in range(NT):
                            np_ = 128 if t < 8 else NS
                            bi = t % 2
                            dma("sync", x1c[bi][0:np_, :], x1_d[t * 128:t * 128 + np_, :], [bx1d[t]], [bx1c[bi]])
                            for s in range(128):
                                g_ap = gsl[gi][0:np_, :]
                                idma(g_ap, pub_d, E[0:np_, t, s:s + 1], [bE[t]], [bg[gi]])
                                stt(junk[0:np_, :], g_ap, 1.0, x1c[bi][0:np_, :], ALU.mult, ALU.mult, [bg[gi], bx1c[bi]], [bjunk, bactv], accum=actv[0:np_, s:s + 1])
                                gi = (gi + 1) % NG
                            act(coef[0:np_, :], actv[0:np_, :], AF.Gelu, [bactv], [bcoef])
                            tt("vector", coef[0:np_, :], coef[0:np_, :], G[0:np_, t, :], ALU.mult, [bcoef, bG[t]], [bcoef])
                            for s in range(128):
                                g_ap = gsl[gi][0:np_, :]
                                idma(g_ap, pvb_d, E[0:np_, t, s:s + 1], [bE[t]], [bg[gi]])
                                act(dg[di][0:np_, 0:np_], identb[0:np_, 0:np_], AF.Copy, [bidb, bcoef], [bdg[di]], scale=coef[0:np_, s:s + 1])
                                for nb in range(4):
                                    mm(psS[0:np_, nb * 512:(nb + 1) * 512], dg[di][0:np_, 0:np_], gsl[gi][0:np_, nb * 512:(nb + 1) * 512], s == 0, s == 127,
                                       [bdg[di], bg[gi]], [bS])
                                gi = (gi + 1) % NG
                                di = (di + 1) % 4
                            stt(accs[bi][0:np_, :], x1c[bi][0:np_, :], ALPHA, psS[0:np_, :], ALU.mult, ALU.add, [bx1c[bi], bS], [bacc[bi]])
                            layer_norm(accs[bi][0:np_, :], bacc[bi], np_)
                            dma("sync", y_o[t * 128:t * 128 + np_, :], accs[bi][0:np_, :], [bacc[bi]], (), final=True)
                        P.flush(last=True)
                else:
                    esC = contextlib.ExitStack()
                    with esC:
                        x1c = T(esC, "x1c", [128, D], F32); bx1c = Buf("x1c")
                        for t in range(NT):
                            np_ = 128 if t < 8 else NS
                            dma("sync", x1c[0:np_, :], x1_d[t * 128:t * 128 + np_, :], [bx1d[t]], [bx1c])
                            dma("sync", y_o[t * 128:t * 128 + np_, :], x1c[0:np_, :], [bx1c], (), final=True)
                        P.flush(last=True)
        else:
            P.flush(last=True)
    return nc


def _rope_tables(pos):
    half = 32
    inv = (10000.0 ** (-np.arange(half, dtype=np.float32) * 2.0 / 64.0)).astype(np.float32)
    ang = pos.astype(np.float32)[:, None] * inv[None, :]
    return np.cos(ang).astype(np.float32), np.sin(ang).astype(np.float32)


_CACHE = {}


def kernel(x_prompt, x_sample, cache_k, cache_v, state_conv, state_h, page_table,
           w_in, conv_w, conv_b, lru_wa, lru_ba, lru_wx, lru_bx, lru_lambda,
           lambda_q1, lambda_k1, lambda_q2, lambda_k2, subln_g, w_out, ln1_g, ln1_b,
           peer_wq, peer_k1, peer_k2, peer_u, peer_v, ln2_g, ln2_b, _flags=None, _trace=False):
    f = np.float32
    A = lambda a: np.ascontiguousarray(np.asarray(a))
    flags = _flags or {}
    key = tuple(sorted(flags.items()))
    if key not in _CACHE:
        _CACHE[key] = build_program(**flags)
    nc = _CACHE[key]

    x_prompt = A(x_prompt); x_sample = A(x_sample)
    npool = flags.get("npool", NPOOL); nexp = flags.get("nexp", 16384)
    ck = A(np.asarray(cache_k)[0][:npool].reshape(npool, 128, 8, 128).transpose(2, 0, 1, 3).reshape(8, npool, 16384))
    cv = A(np.asarray(cache_v)[0][:npool].transpose(2, 0, 1, 3).reshape(8, npool, 16384))
    w_in0 = A(np.asarray(w_in)[0]); w_out0 = A(np.asarray(w_out)[0]); wq0 = A(np.asarray(peer_wq)[0])
    chan = np.stack([np.asarray(conv_w)[0][0], np.asarray(conv_w)[0][1], np.asarray(conv_w)[0][2], np.asarray(conv_w)[0][3],
                     np.asarray(conv_b)[0], np.asarray(lru_ba)[0], np.asarray(lru_bx)[0], np.asarray(lru_lambda)[0]], axis=-1)
    chan = A(chan.reshape(8, 128, 8).transpose(1, 0, 2))
    lwa = A(np.asarray(lru_wa)[0].transpose(1, 0, 2))
    lwx = A(np.asarray(lru_wx)[0].transpose(1, 0, 2))
    lamv = A(np.broadcast_to(np.stack([np.asarray(lambda_q1)[0], np.asarray(lambda_k1)[0], np.asarray(lambda_q2)[0], np.asarray(lambda_k2)[0]])[None], (128, 4, 64)))
    subg = A(np.broadcast_to(np.asarray(subln_g)[0][None], (128, 128)))
    ln = A(np.broadcast_to(np.stack([np.asarray(ln1_g)[0], np.asarray(ln1_b)[0], np.asarray(ln2_g)[0], np.asarray(ln2_b)[0]])[None], (128, 4, D)))
    k1T = A(np.asarray(peer_k1)[0].transpose(2, 0, 1))
    k2T = A(np.asarray(peer_k2)[0].transpose(2, 0, 1))
    pu = A(np.asarray(peer_u)[0][:nexp]); pv = A(np.asarray(peer_v)[0][:nexp])
    ident = np.eye(128, dtype=f)
    cmask = np.where(np.arange(128)[None, :] <= np.arange(128)[:, None], 0.0, 8.0 * NEG).astype(f)
    sel4 = np.zeros((NS, NS, 128), f)
    for s in range(NS):
        sel4[s, s, :] = 1.0
    sel4 = sel4.reshape(NS, NS * 128)
    ones65 = np.ones((128, 128), f)
    iota16 = A(np.broadcast_to(np.arange(16, dtype=f)[None, None, :], (128, 16, 16)).reshape(128, 256))
    cs, sn = _rope_tables(np.array([8192]))
    rsam = A(np.broadcast_to(np.concatenate([cs[0], -sn[0], sn[0]])[None], (NS, 96)))
    sc0 = np.asarray(state_conv)[0]
    sh0 = np.asarray(state_h)[0]
    pt = np.asarray(page_table)

    in_maps = []
    for j in range(NCORES):
        b, hf = j // 2, j % 2
        xs = x_prompt[b]
        xT = np.zeros((D, NSLOT), f)
        if hf == 1:
            xT[:, :] = xs.T
        else:
            xT[:, TOWN:] = xs[0:TOWN].T
        own = xs[hf * TOWN:(hf + 1) * TOWN]
        smp = x_sample[NS * j:NS * (j + 1), 0, :]
        pos = np.concatenate([np.arange(TOWN), hf * TOWN + np.arange(TOWN)])
        c_, s_ = _rope_tables(pos)
        rcos = A(c_.reshape(16, 128, 32).transpose(1, 0, 2))
        rsin = A(np.stack([-s_, s_], axis=1).reshape(16, 128, 2, 32).transpose(1, 0, 2, 3))
        pbrow = np.zeros((1, NSLOT), f)
        if hf == 0:
            pbrow[0, 0:TOWN] = 8.0 * NEG
        pbv = np.zeros((128, 2), f)
        pbv[:, 0] = 0.0 if hf == 1 else NEG
        pbv[:, 1] = 1.0 if hf == 1 else 0.0
        scj = sc0[NS * j:NS * (j + 1)]
        sconv = A(scj.reshape(NS, 3, 8, 128).transpose(3, 2, 1, 0))
        shj = A(sh0[NS * j:NS * (j + 1)].reshape(NS, 8, 128).transpose(2, 1, 0))
        in_maps.append({
            "xT": xT, "xsT": A(smp.T), "xtok": A(np.concatenate([own, smp], 0)),
            "w_in": w_in0, "w_out": w_out0, "wq": wq0, "chan": chan, "lwa": lwa, "lwx": lwx, "lamv": lamv, "subg": subg,
            "rcos": rcos, "rsin": rsin, "rsam": rsam, "pb": pbv, "pbrow": pbrow, "ident": ident, "cmask": cmask, "sel128": sel4, "ones128": ones65,
            "iota16": iota16, "cache_k": ck, "cache_v": cv, "ptT": A(np.concatenate([pt[NS * j:NS * (j + 1)].T, pt[NS * j:NS * (j + 1)].T], 0).astype(np.int32)),
            "sconv": sconv, "sh": shj, "ln": ln, "k1T": k1T, "k2T": k2T, "peer_u": pu, "peer_v": pv,
        })
    res = run_bass_kernel_spmd(nc, in_maps, core_ids=list(range(NCORES)), **({"trace": True} if _trace else {}))
    R = res.results
    yp = np.zeros((4, 2048, D), f); ys = np.zeros((32, 1, D), f)
    kp = np.zeros((1, 4, 2048, 16, 64), f); vp = np.zeros((1, 4, 2048, 8, 128), f)
    cp_ = np.zeros((1, 4, 3, 1024), f); hp = np.zeros((1, 4, 1024), f)
    ksn = np.zeros((1, 32, 1, 16, 64), f); vsn = np.zeros((1, 32, 1, 8, 128), f)
    csn = np.zeros((1, 32, 3, 1024), f); hsn = np.zeros((1, 32, 1024), f)
    for j in range(NCORES):
        b, hf = j // 2, j % 2
        r = R[j]
        yp[b, hf * TOWN:(hf + 1) * TOWN] = r["y"][0:TOWN]
        ys[NS * j:NS * (j + 1), 0] = r["y"][TOWN:TTOK]
        kp[0, b, hf * TOWN:(hf + 1) * TOWN] = r["k_o"].reshape(TOWN, 16, 64)
        vp[0, b, hf * TOWN:(hf + 1) * TOWN] = r["v_o"].reshape(TOWN, 8, 128)
        if hf == 1:
            cp_[0, b] = r["conv_o"].transpose(2, 1, 0).reshape(3, 1024)
            hp[0, b] = r["h_o"].transpose(1, 0).reshape(1024)
        ksn[0, NS * j:NS * (j + 1), 0] = r["ks_o"].reshape(NS, 16, 64)
        vsn[0, NS * j:NS * (j + 1), 0] = r["vs_o"].reshape(NS, 8, 128)
        csn[0, NS * j:NS * (j + 1)] = r["convs_o"].transpose(3, 2, 1, 0).reshape(NS, 3, 1024)
        hsn[0, NS * j:NS * (j + 1)] = r["hs_o"].transpose(2, 1, 0).reshape(NS, 1024)
    if flags.get("dbg"):
        kernel._dbg = [(R[j]["mix_o"], R[j]["x1_o"]) for j in range(NCORES)]
    if _trace:
        kernel._exec_ns = res.exec_time_ns
    return (yp, ys, kp, vp, cp_, hp, ksn, vsn, csn, hsn)
```

```python
import contextlib
import numpy as np
import concourse.bass as bass
import concourse.mybir as mybir
from concourse.bass_utils import run_bass_kernel_spmd

F32 = mybir.dt.float32
BF16 = mybir.dt.bfloat16
I32 = mybir.dt.int32
U32 = mybir.dt.uint32
AF = mybir.ActivationFunctionType
ALU = mybir.AluOpType
AX = mybir.AxisListType

NCORES = 8
D = 2048
TOWN = 1024
NSLOT = 2048
NS = 4
TTOK = TOWN + NS
NPOOL = 2560
NPAGES = 64
EPS = 1e-5
NEG = -30000.0
ALPHA = 2.0 ** 0.25
LAM_INIT = 0.2

ENGS = ("sync", "scalar", "vector", "gpsimd", "tensor")


class Buf:
    __slots__ = ("name", "w", "r")

    def __init__(self, name):
        self.name = name
        self.w = None
        self.r = []


class Prog:
    def __init__(self, nc, n_dma_sems=12):
        self.nc = nc
        self.lists = {e: [] for e in ENGS}
        self.esem = {e: nc.alloc_semaphore(name="es_" + e) for e in ENGS}
        self.ecnt = {e: 0 for e in ENGS}
        self.dsem, self.dcnt, self.dpos = {}, {}, {}
        for q in ("sync", "scalar", "gpsimd"):
            self.dsem[q] = [nc.alloc_semaphore(name=f"ds_{q}{i}") for i in range(n_dma_sems)]
            self.dcnt[q] = [0] * n_dma_sems
            self.dpos[q] = 0
        self.waited = {e: {} for e in ENGS}
        self.final_events = []

    def _need(self, eng, ev):
        if ev is None:
            return
        sem, val, src = ev
        if src == eng and eng == "tensor":
            return
        key = id(sem)
        if self.waited[eng].get(key, 0) >= val:
            return
        self.waited[eng][key] = val
        self.lists[eng].append(("wait", sem, val))

    def _deps(self, eng, reads, writes):
        for b in reads:
            self._need(eng, b.w)
        for b in writes:
            self._need(eng, b.w)
            for ev in b.r:
                self._need(eng, ev)

    def _mark(self, ev, reads, writes):
        for b in reads:
            b.r.append(ev)
            if len(b.r) > 48:
                last = {}
                for e2 in b.r:
                    k = id(e2[0])
                    if k not in last or last[k][1] < e2[1]:
                        last[k] = e2
                b.r = list(last.values())
        for b in writes:
            b.w = ev
            b.r = []

    def op(self, eng, fn, reads=(), writes=()):
        self._deps(eng, reads, writes)
        self.ecnt[eng] += 1
        ev = (self.esem[eng], self.ecnt[eng], eng)
        self.lists[eng].append(("op", fn, self.esem[eng], 1))
        self._mark(ev, reads, writes)
        return ev

    def dma(self, q, fn, reads=(), writes=(), final=False):
        i = self.dpos[q]
        self.dpos[q] = (i + 1) % len(self.dsem[q])
        sem = self.dsem[q][i]
        if self.dcnt[q][i] > 0:
            self._need(q, (sem, self.dcnt[q][i], None))
        self._deps(q, reads, writes)
        self.dcnt[q][i] += 16
        ev = (sem, self.dcnt[q][i], None)
        self.lists[q].append(("op", fn, sem, 16))
        self._mark(ev, reads, writes)
        if final:
            self.final_events.append(ev)
        return ev

    def all_events(self):
        evs = [(self.esem[e], self.ecnt[e], e) for e in ENGS if self.ecnt[e] > 0]
        for q in self.dsem:
            for i, s in enumerate(self.dsem[q]):
                if self.dcnt[q][i] > 0:
                    evs.append((s, self.dcnt[q][i], None))
        return evs

    def barrier(self):
        evs = self.all_events()
        for e in ENGS:
            for ev in evs:
                if ev[2] == e:
                    continue
                self._need(e, ev)

    def flush(self, last=False):
        if last:
            for ev in self.all_events():
                self._need("sync", ev)
        lists = self.lists
        self.lists = {e: [] for e in ENGS}
        nc = self.nc

        def run(engobj, items):
            for it in items:
                if it[0] == "wait":
                    engobj.wait_ge(it[1], it[2])
                else:
                    it[1](engobj).then_inc(it[2], it[3])

        with nc.Block() as block:
            @block.sync
            def _(e):
                run(e, lists["sync"])

            @block.scalar
            def _(e):
                run(e, lists["scalar"])

            @block.vector
            def _(e):
                run(e, lists["vector"])

            @block.gpsimd
            def _(e):
                run(e, lists["gpsimd"])

            @block.tensor
            def _(e):
                run(e, lists["tensor"])


def build_program(do_rec=True, do_att=True, do_dec=True, do_p2=True, do_peer=True, dbg=False, npool=NPOOL, nexp=16384):
    nc = bass.Bass("TRN2", target_bir_lowering=False)

    def din(name, shape, dt=F32):
        return nc.dram_tensor(name, list(shape), dt, kind="ExternalInput").ap()

    def dout(name, shape, dt=F32):
        return nc.dram_tensor(name, list(shape), dt, kind="ExternalOutput").ap()

    xT_d = din("xT", [D, NSLOT])
    xsT_d = din("xsT", [D, NS])
    xtok_d = din("xtok", [TTOK, D])
    w_in_d = din("w_in", [D, 5120])
    w_out_d = din("w_out", [D, D])
    wq_d = din("wq", [D, D])
    chan_d = din("chan", [128, 8, 8])
    lwa_d = din("lwa", [128, 8, 128])
    lwx_d = din("lwx", [128, 8, 128])
    lamv_d = din("lamv", [128, 4, 64])
    subg_d = din("subg", [128, 128])
    rcos_d = din("rcos", [128, 16, 32])
    rsin_d = din("rsin", [128, 16, 2, 32])
    rsam_d = din("rsam", [NS, 96])
    pb_d = din("pb", [128, 2])
    pbrow_d = din("pbrow", [1, NSLOT])
    ident_d = din("ident", [128, 128])
    cmask_d = din("cmask", [128, 128])
    sel4_d = din("sel128", [NS, NS * 128])
    ones_d = din("ones128", [128, 128])
    iota_d = din("iota16", [128, 256])
    ck_d = din("cache_k", [8, npool, 16384])
    cv_d = din("cache_v", [8, npool, 16384])
    pt_d = din("ptT", [128, NS], I32)
    sconv_d = din("sconv", [128, 8, 3, NS])
    sh_d = din("sh", [128, 8, NS])
    ln_d = din("ln", [128, 4, D])
    k1T_d = din("k1T", [128, 8, 128])
    k2T_d = din("k2T", [128, 8, 128])
    pu_d = din("peer_u", [nexp, D])
    pv_d = din("peer_v", [nexp, D])

    y_o = dout("y", [TTOK, D])
    k_o = dout("k_o", [TOWN, 1024])
    v_o = dout("v_o", [TOWN, 1024])
    conv_o = dout("conv_o", [128, 8, 3])
    h_o = dout("h_o", [128, 8])
    ks_o = dout("ks_o", [NS, 1024])
    vs_o = dout("vs_o", [NS, 1024])
    convs_o = dout("convs_o", [128, 8, 3, NS])
    hs_o = dout("hs_o", [128, 8, NS])
    if dbg:
        mix_o = dout("mix_o", [128, 16, TTOK])
        x1_o = dout("x1_o", [TTOK, D])

    P = Prog(nc)

    def tt(eng, out, in0, in1, op, r, w):
        P.op(eng, lambda e: e.tensor_tensor(out=out, in0=in0, in1=in1, op=op), r, w)

    def ts(eng, out, in0, s1, s2, op0, op1, r, w):
        if op1 is None:
            P.op(eng, lambda e: e.tensor_scalar(out=out, in0=in0, scalar1=s1, scalar2=None, op0=op0), r, w)
        else:
            P.op(eng, lambda e: e.tensor_scalar(out=out, in0=in0, scalar1=s1, scalar2=s2, op0=op0, op1=op1), r, w)

    def stt(out, in0, scalar, in1, op0, op1, r, w, accum=None):
        if accum is None:
            P.op("vector", lambda e: e.scalar_tensor_tensor(out=out, in0=in0, scalar=scalar, in1=in1, op0=op0, op1=op1), r, w)
        else:
            P.op("vector", lambda e: e.scalar_tensor_tensor(out=out, in0=in0, scalar=scalar, in1=in1, op0=op0, op1=op1,
                                                            accum_out=accum), r, w)

    def act(out, in_, func, r, w, bias=None, scale=None, accum=None):
        kw = {}
        if bias is not None:
            kw["bias"] = bias
        if scale is not None:
            kw["scale"] = scale
        if accum is not None:
            kw["accum_out"] = accum
        P.op("scalar", lambda e: e.activation(out=out, in_=in_, func=func, **kw), r, w)

    def cp(eng, out, in_, r, w):
        if eng == "scalar":
            P.op("scalar", lambda e: e.copy(out=out, in_=in_), r, w)
        else:
            P.op(eng, lambda e: e.tensor_copy(out=out, in_=in_), r, w)

    def mm(out, lhsT, rhs, start, stop, r, w):
        P.op("tensor", lambda e: e.matmul(out, lhsT=lhsT, rhs=rhs, start=start, stop=stop), r, w)

    def tr(out, in_, ident, r, w):
        P.op("tensor", lambda e: e.transpose(out=out, in_=in_, identity=ident), r, w)

    def dma(q, out, in_, r, w, final=False):
        P.dma(q, lambda e: e.dma_start(out=out, in_=in_), r, w, final=final)

    def idma(out, in_, idx, r, w, eoff=0):
        P.dma("gpsimd", lambda e: e.indirect_dma_start(out=out, out_offset=None, in_=in_,
                                                      in_offset=bass.IndirectOffsetOnAxis(ap=idx, axis=0), element_offset=eoff), r, w)

    def memset(eng, ap, val, w):
        P.op(eng, lambda e: e.memset(ap, val), (), w)

    def red(eng, out, in_, op, r, w):
        P.op(eng, lambda e: e.tensor_reduce(out=out, in_=in_, axis=AX.X, op=op), r, w)

    def recip(out, in_, r, w):
        P.op("vector", lambda e: e.reciprocal(out=out, in_=in_), r, w)

    es_all = contextlib.ExitStack()
    with es_all:
        def T(es, name, shape, dt):
            return es.enter_context(nc.sbuf_tensor("s_" + name, list(shape), dt))

        def PSt(es, name, shape, dt):
            return es.enter_context(nc.psum_tensor("p_" + name, list(shape), dt))

        psS = PSt(es_all, "psS", [128, 2048], F32); bS = Buf("psS")
        psT = PSt(es_all, "psT", [128, 1024], BF16); bT = Buf("psT")
        psA = PSt(es_all, "psA", [128, 512], F32); bA = Buf("psA")
        psB = PSt(es_all, "psB", [128, 512], F32); bB = Buf("psB")
        psC = PSt(es_all, "psC", [128, 512], F32); bC = Buf("psC")
        psrot = [(psA, bA), (psB, bB), (psC, bC)]

        mix_dt = nc.dram_tensor("mix_d", [16, 128, TTOK], BF16)
        mix_d = mix_dt.ap()
        x1_dt = nc.dram_tensor("x1_d", [TTOK, D], F32)
        x1_d = x1_dt.ap()
        bmix = [Buf(f"mix{i}") for i in range(16)]
        pub_d = nc.dram_tensor("pub_d", [nexp, D], BF16).ap()
        pvb_d = nc.dram_tensor("pvb_d", [nexp, D], BF16).ap()
        mstg = T(es_all, "mstg", [128, TTOK], BF16); bmstg = Buf("mstg")
        identf = T(es_all, "identf", [128, 128], F32); bidf = Buf("identf")
        identb = T(es_all, "identb", [128, 128], BF16); bidb = Buf("identb")
        pbt = T(es_all, "pbt", [128, 2], F32); bpb = Buf("pb")

        dma("sync", identf[:], ident_d, (), [bidf])
        dma("gpsimd", identb[:], ident_d, (), [bidb])
        dma("sync", pbt[:], pb_d, (), [bpb])

        es1 = contextlib.ExitStack()
        with es1:
            xT = T(es1, "xTb", [128, 16, NSLOT], BF16); bxT = [Buf(f"xT{k}") for k in range(16)]
            xsT = T(es1, "xsTb", [128, 16, NS], BF16); bxs = Buf("xsT")
            chan = T(es1, "chan", [128, 8, 8], F32); bchan = Buf("chan")
            nsp = T(es1, "nsp", [128, 8, 2], F32); bnsp = Buf("nsp")
            for k in range(16):
                dma("gpsimd", xT[:, k, :], xT_d[k * 128:(k + 1) * 128, :], (), [bxT[k]])
            dma("gpsimd", xsT[:], xsT_d.rearrange("(k p) s -> p k s", p=128), (), [bxs])
            CR = 512
            conv_list = [(tab, r0) for tab in range(2) for r0 in range(0, nexp, CR)] if (do_p2 and do_peer) else []
            conv_pos = [0]

            def conv_steps(n):
                for _ in range(n):
                    if conv_pos[0] >= len(conv_list):
                        return
                    tab, r0 = conv_list[conv_pos[0]]
                    conv_pos[0] += 1
                    src = (pu_d if tab == 0 else pv_d)[r0:r0 + CR, :]
                    dst = (pub_d if tab == 0 else pvb_d)[r0:r0 + CR, :]
                    dma("gpsimd", dst, src, (), ())

            dma("sync", chan[:], chan_d, (), [bchan])
            tmpl = T(es1, "tmpl", [128, 8], F32); btl = Buf("tmpl")
            act(tmpl[:], chan[:, :, 7], AF.Exp, [bchan], [btl], scale=-1.0)
            act(tmpl[:], tmpl[:], AF.Ln, [btl], [btl], bias=1.0)
            ts("vector", nsp[:, :, 0], tmpl[:], -8.0, None, ALU.mult, None, [btl], [bnsp])
            ts("vector", nsp[:, :, 1], tmpl[:], -16.0, None, ALU.mult, None, [btl], [bnsp])

            if do_rec:
                esr = contextlib.ExitStack()
                with esr:
                    wrx = T(esr, "wrx", [128, 16, 128], BF16); bwrx = Buf("wrx")
                    wrg = T(esr, "wrg", [128, 16, 128], BF16); bwrg = Buf("wrg")
                    wab = T(esr, "wab", [128, 128], BF16); bwab = Buf("wab")
                    wxb = T(esr, "wxb", [128, 128], BF16); bwxb = Buf("wxb")
                    rx = T(esr, "rx", [128, 3 + NSLOT], F32); brx = Buf("rx")
                    gg = T(esr, "gg", [128, TOWN], F32); bgg = Buf("gg")
                    cvt = T(esr, "cvt", [128, NSLOT], F32); bcv = Buf("cv")
                    cvb = T(esr, "cvb", [128, NSLOT], BF16); bcvb = Buf("cvb")
                    rgt = T(esr, "rgt", [128, NSLOT], F32); brgt = Buf("rgt")
                    igt = T(esr, "igt", [128, NSLOT], F32); bigt = Buf("igt")
                    at = T(esr, "at", [128, NSLOT], F32); bat = Buf("at")
                    a2t = T(esr, "a2t", [128, NSLOT], F32); ba2 = Buf("a2t")
                    hpre = T(esr, "hpre", [128, TOWN], F32); bhp = Buf("hpre")
                    hown = T(esr, "hown", [128, TOWN], F32); bho = Buf("hown")
                    h0 = T(esr, "h0", [128, 1], F32); bh0 = Buf("h0")
                    cst = T(esr, "cst", [128, 8, 3], F32); bcst = Buf("cst")
                    hst = T(esr, "hst", [128, 8], F32); bhst = Buf("hst")
                    sconv = T(esr, "sconv", [128, 8, 3, NS], F32); bsc = Buf("sconv")
                    sht = T(esr, "sht", [128, 8, NS], F32); bsh = Buf("sht")
                    cvsst = T(esr, "cvsst", [128, 8, 3, NS], F32); bcvs = Buf("cvsst")
                    hsst = T(esr, "hsst", [128, 8, NS], F32); bhss = Buf("hsst")
                    sm = T(esr, "sm", [128, 12, NS], F32); bsm = Buf("sm")
                    smb = T(esr, "smb", [128, NS], BF16); bsmb = Buf("smb")
                    dma("sync", sconv[:], sconv_d, (), [bsc])
                    dma("sync", sht[:], sh_d, (), [bsh])
                    memset("vector", rx[:, 0:3], 0.0, [brx])
                    for r in range(8):
                        dma("gpsimd", wrx[:], w_in_d[:, r * 128:(r + 1) * 128].rearrange("(k p) c -> p k c", p=128), (), [bwrx])
                        dma("gpsimd", wrg[:], w_in_d[:, 1024 + r * 128:1024 + (r + 1) * 128].rearrange("(k p) c -> p k c", p=128), (), [bwrg])
                        dma("gpsimd", wab[:], lwa_d[:, r, :], (), [bwab])
                        dma("gpsimd", wxb[:], lwx_d[:, r, :], (), [bwxb])
                        conv_steps(8)
                        for blk in range(4):
                            ps, bp = psrot[blk % 3]
                            for k in range(16):
                                mm(ps[:, 0:512], wrx[:, k, :], xT[:, k, blk * 512:(blk + 1) * 512], k == 0, k == 15, [bwrx, bxT[k]], [bp])
                            cp("scalar", rx[:, 3 + blk * 512:3 + (blk + 1) * 512], ps[:, 0:512], [bp], [brx])
                        for blk in range(2, 4):
                            ps, bp = psrot[(blk + 1) % 3]
                            for k in range(16):
                                mm(ps[:, 0:512], wrg[:, k, :], xT[:, k, blk * 512:(blk + 1) * 512], k == 0, k == 15, [bwrg, bxT[k]], [bp])
                            act(gg[:, (blk - 2) * 512:(blk - 1) * 512], ps[:, 0:512], AF.Gelu, [bp], [bgg])
                        ts("vector", cvt[:], rx[:, 0:NSLOT], chan[:, r, 0:1], chan[:, r, 4:5], ALU.mult, ALU.add, [brx, bchan], [bcv])
                        for j in range(1, 4):
                            stt(cvt[:], rx[:, j:j + NSLOT], chan[:, r, j:j + 1], cvt[:], ALU.mult, ALU.add, [brx, bchan, bcv], [bcv])
                        cp("gpsimd", cvb[:], cvt[:], [bcv], [bcvb])
                        for blk in range(4):
                            ps, bp = psrot[blk % 3]
                            mm(ps[:, 0:512], wab[:], cvb[:, blk * 512:(blk + 1) * 512], True, True, [bwab, bcvb], [bp])
                            act(rgt[:, blk * 512:(blk + 1) * 512], ps[:, 0:512], AF.Sigmoid, [bp, bchan], [brgt], bias=chan[:, r, 5:6])
                            ps2, bp2 = psrot[(blk + 1) % 3]
                            mm(ps2[:, 0:512], wxb[:], cvb[:, blk * 512:(blk + 1) * 512], True, True, [bwxb, bcvb], [bp2])
                            act(igt[:, blk * 512:(blk + 1) * 512], ps2[:, 0:512], AF.Sigmoid, [bp2, bchan], [bigt], bias=chan[:, r, 6:7])
                        act(at[:], rgt[:], AF.Exp, [brgt, bnsp], [bat], scale=nsp[:, r, 0:1])
                        act(a2t[:], rgt[:], AF.Exp, [brgt, bnsp], [ba2], scale=nsp[:, r, 1:2])
                        ts("vector", a2t[:], a2t[:], -1.0, 1.0, ALU.mult, ALU.add, [ba2], [ba2])
                        ts("gpsimd", a2t[:], a2t[:], 1e-30, None, ALU.max, None, [ba2], [ba2])
                        act(a2t[:], a2t[:], AF.Sqrt, [ba2], [ba2])
                        tt("gpsimd", igt[:], igt[:], cvt[:], ALU.mult, [bigt, bcv], [bigt])
                        tt("vector", a2t[:], a2t[:], igt[:], ALU.mult, [ba2, bigt], [ba2])
                        P.op("vector", lambda e: e.tensor_tensor_scan(out=hpre[:], data0=at[:, 0:TOWN], data1=a2t[:, 0:TOWN], initial=0.0,
                                                                    op0=ALU.mult, op1=ALU.add), [bat, ba2], [bhp])
                        ts("vector", h0[:], hpre[:, TOWN - 1:TOWN], pbt[:, 1:2], None, ALU.mult, None, [bhp, bpb], [bh0])
                        P.op("vector", lambda e: e.tensor_tensor_scan(out=hown[:], data0=at[:, TOWN:NSLOT], data1=a2t[:, TOWN:NSLOT], initial=h0[:, 0:1],
                                                                    op0=ALU.mult, op1=ALU.add), [bat, ba2, bh0], [bho])
                        tt("gpsimd", mstg[:, 0:TOWN], hown[:], gg[:], ALU.mult, [bho, bgg], [bmstg])
                        cp("gpsimd", cst[:, r, :], rx[:, NSLOT:NSLOT + 3], [brx], [bcst])
                        cp("gpsimd", hst[:, r:r + 1], hown[:, TOWN - 1:TOWN], [bho], [bhst])
                        ps, bp = psrot[0]
                        for k in range(16):
                            mm(ps[:, 0:NS], wrx[:, k, :], xsT[:, k, :], k == 0, k == 15, [bwrx, bxs], [bp])
                        cp("vector", sm[:, 0, :], ps[:, 0:NS], [bp], [bsm])
                        ps, bp = psrot[1]
                        for k in range(16):
                            mm(ps[:, 0:NS], wrg[:, k, :], xsT[:, k, :], k == 0, k == 15, [bwrg, bxs], [bp])
                        act(sm[:, 1, :], ps[:, 0:NS], AF.Gelu, [bp], [bsm])
                        ts("vector", sm[:, 2, :], sconv[:, r, 0, :], chan[:, r, 0:1], chan[:, r, 4:5], ALU.mult, ALU.add, [bsc, bchan, bsm], [bsm])
                        stt(sm[:, 2, :], sconv[:, r, 1, :], chan[:, r, 1:2], sm[:, 2, :], ALU.mult, ALU.add, [bsc, bchan, bsm], [bsm])
                        stt(sm[:, 2, :], sconv[:, r, 2, :], chan[:, r, 2:3], sm[:, 2, :], ALU.mult, ALU.add, [bsc, bchan, bsm], [bsm])
                        stt(sm[:, 2, :], sm[:, 0, :], chan[:, r, 3:4], sm[:, 2, :], ALU.mult, ALU.add, [bchan, bsm], [bsm])
                        cp("vector", smb[:], sm[:, 2, :], [bsm], [bsmb])
                        ps, bp = psrot[2]
                        mm(ps[:, 0:NS], wab[:], smb[:], True, True, [bwab, bsmb], [bp])
                        act(sm[:, 3, :], ps[:, 0:NS], AF.Sigmoid, [bp, bchan], [bsm], bias=chan[:, r, 5:6])
                        ps, bp = psrot[0]
                        mm(ps[:, 0:NS], wxb[:], smb[:], True, True, [bwxb, bsmb], [bp])
                        act(sm[:, 4, :], ps[:, 0:NS], AF.Sigmoid, [bp, bchan], [bsm], bias=chan[:, r, 6:7])
                        act(sm[:, 5, :], sm[:, 3, :], AF.Exp, [bsm, bnsp], [bsm], scale=nsp[:, r, 0:1])
                        act(sm[:, 6, :], sm[:, 3, :], AF.Exp, [bsm, bnsp], [bsm], scale=nsp[:, r, 1:2])
                        ts("vector", sm[:, 6, :], sm[:, 6, :], -1.0, 1.0, ALU.mult, ALU.add, [bsm], [bsm])
                        ts("vector", sm[:, 6, :], sm[:, 6, :], 1e-30, None, ALU.max, None, [bsm], [bsm])
                        act(sm[:, 6, :], sm[:, 6, :], AF.Sqrt, [bsm], [bsm])
                        tt("vector", sm[:, 6, :], sm[:, 6, :], sm[:, 4, :], ALU.mult, [bsm], [bsm])
                        tt("vector", sm[:, 6, :], sm[:, 6, :], sm[:, 2, :], ALU.mult, [bsm], [bsm])
                        tt("vector", sm[:, 7, :], sm[:, 5, :], sht[:, r, :], ALU.mult, [bsm, bsh], [bsm])
                        tt("vector", sm[:, 7, :], sm[:, 7, :], sm[:, 6, :], ALU.add, [bsm], [bsm])
                        tt("vector", mstg[:, TOWN:TTOK], sm[:, 7, :], sm[:, 1, :], ALU.mult, [bsm], [bmstg])
                        dma("sync", mix_d[r], mstg[:], [bmstg], [bmix[r]])
                        cp("vector", hsst[:, r, :], sm[:, 7, :], [bsm], [bhss])
                        cp("vector", cvsst[:, r, 0, :], sconv[:, r, 1, :], [bsc], [bcvs])
                        cp("vector", cvsst[:, r, 1, :], sconv[:, r, 2, :], [bsc], [bcvs])
                        cp("vector", cvsst[:, r, 2, :], sm[:, 0, :], [bsm], [bcvs])
                    dma("sync", conv_o, cst[:], [bcst], (), final=True)
                    dma("sync", h_o, hst[:], [bhst], (), final=True)
                    dma("sync", convs_o, cvsst[:], [bcvs], (), final=True)
                    dma("sync", hs_o, hsst[:], [bhss], (), final=True)
                    P.barrier()
                    P.flush()

            if do_att:
                esa = contextlib.ExitStack()
                with esa:
                    wq3 = T(esa, "wq3", [128, 16, 384], BF16); bw3 = Buf("wq3")
                    rcos = T(esa, "rcos", [128, 16, 32], F32); brc = Buf("rcos")
                    rsin = T(esa, "rsin", [128, 16, 2, 32], F32); brs = Buf("rsin")
                    rsam = T(esa, "rsam", [NS, 96], F32); brsm = Buf("rsam")
                    cmask = T(esa, "cmask", [128, 128], F32); bcm = Buf("cmask")
                    lamv = T(esa, "lamv", [128, 4, 64], F32); blv = Buf("lamv")
                    lam = T(esa, "lam", [128, 4], F32); blam = Buf("lam")
                    gsc = T(esa, "gsc", [128, 128], F32); bgsc = Buf("gsc")
                    t1 = T(esa, "t1", [128, 256], F32); bt1 = Buf("t1")
                    t2 = T(esa, "t2", [128, 256], F32); bt2 = Buf("t2")
                    kb16 = T(esa, "kb16", [128, 16, 128], BF16); bkb = Buf("kb16")
                    qb16 = T(esa, "qb16", [128, 8, 128], BF16); bqb = Buf("qb16")
                    vb = T(esa, "vb", [128, 16, 128], BF16); bvb = Buf("vb")
                    kst = T(esa, "kst", [128, 8, 128], F32); bkst = Buf("kst")
                    vst = T(esa, "vst", [128, 8, 128], F32); bvst = Buf("vst")
                    kT = [T(esa, f"kT{m}", [65, NSLOT], BF16) for m in range(2)]; bkT = [Buf("kT0"), Buf("kT1")]
                    qT = [T(esa, f"qT{m}", [65, TOWN], BF16) for m in range(2)]; bqT = [Buf("qT0"), Buf("qT1")]
                    Pp = [T(esa, f"Pp{i}", [128, 512], BF16) for i in range(3)]; bPp = [Buf(f"Pp{i}") for i in range(3)]
                    PTp = [T(esa, f"PTp{i}", [128, 4, 128], BF16) for i in range(3)]; bPTp = [Buf(f"PTp{i}") for i in range(3)]
                    cmaskb = T(esa, "cmaskb", [128, 128], BF16); bcmb = Buf("cmaskb")
                    nrm = T(esa, "nrm", [128, 16, 4], F32); bnrm = Buf("nrm")
                    nbq = T(esa, "nbq", [128, 8, 2], F32); bnbq = Buf("nbq")
                    kmx = T(esa, "kmx", [128, 4], F32); bkmx = Buf("kmx")
                    kmx2 = T(esa, "kmx2", [2, 132], F32); bkmx2 = Buf("kmx2")
                    zc = T(esa, "zc", [128, 2, 4], F32); bzc = Buf("zc")
                    bSp = [Buf(f"psS{i}") for i in range(4)]
                    bTh1 = Buf("psT_h1")
                    acnt = [0, 0, 0]
                    st = T(esa, "st", [128, 16], F32); bst = Buf("st")
                    att = T(esa, "att", [128, 128], F32); batt = Buf("att")
                    att2 = T(esa, "att2", [128, 128], F32); batt2 = Buf("att2")
                    attb = T(esa, "attb", [128, 128], BF16); battb = Buf("attb")
                    junk = T(esa, "junk", [128, 128], F32); bjunk = Buf("junk")
                    ksst = T(esa, "ksst", [NS, 2, 128], F32); bkss = Buf("ksst")
                    vsst = T(esa, "vsst", [NS, 2, 128], F32); bvss = Buf("vsst")
                    qs = T(esa, "qs", [NS, 8, 128], F32); bqs = Buf("qs")
                    msts = T(esa, "msts", [128, 8, NS], BF16); bmsts = Buf("msts")
                    ts1 = T(esa, "ts1", [NS, 256], F32); bts1 = Buf("ts1")
                    ts2 = T(esa, "ts2", [NS, 256], F32); bts2 = Buf("ts2")
                    ptT = T(esa, "ptT", [128, NS], I32); bpt = Buf("ptT")
                    ptf = T(esa, "ptf", [128, NS], F32); bptf = Buf("ptf")
                    ptc2 = T(esa, "ptc2", [128, NS, 4], I32); bptc = Buf("ptc")
                    sel128 = T(esa, "sel128", [NS, NS * 128], F32); bsel = Buf("sel128")
                    ones128 = T(esa, "ones128", [128, 128], F32); bon = Buf("ones128")
                    CH = 16
                    NCH = 128 // CH
                    NKT = 4
                    Kt = [T(esa, f"Kt{i}", [128, CH * 128], F32) for i in range(NKT)]; bKt = [Buf(f"Kt{i}") for i in range(NKT)]
                    qbc = T(esa, "qbc", [128, NS, 128], F32); bqbc = Buf("qbc")
                    knew = T(esa, "knew", [1, NS, 128], F32); bknew = Buf("knew")
                    vnew = T(esa, "vnew", [1, NS, 128], F32); bvnew = Buf("vnew")
                    vnewb = T(esa, "vnewb", [1, NS, 128], BF16); bvnewb = Buf("vnewb")
                    sc = T(esa, "sc", [128, NS, 65, 2], F32); bsc2 = Buf("sc")
                    ee = sc; bee = bsc2
                    wv = T(esa, "wv", [128, 65], F32); bwv = Buf("wv")
                    Bz = T(esa, "Bz", [128, NS, 65, 7], BF16); bBz = Buf("Bz")
                    Vb = [T(esa, f"Vb{i}", [128, CH * 128], BF16) for i in range(2)]; bVb = [Buf("Vb0"), Buf("Vb1")]
                    dsm = T(esa, "dsm", [128, 40], F32); bdsm = Buf("dsm")
                    dsm2 = T(esa, "dsm2", [8, 136], F32); bdsm2 = Buf("dsm2")
                    kcnt = [0]

                    dma("sync", rcos[:], rcos_d, (), [brc])
                    dma("sync", rsin[:], rsin_d, (), [brs])
                    dma("sync", rsam[:], rsam_d, (), [brsm])
                    dma("sync", cmask[:], cmask_d, (), [bcm])
                    dma("gpsimd", cmaskb[:], cmask_d, (), [bcmb])
                    for m in range(2):
                        dma("gpsimd", kT[m][64:65, :], pbrow_d, (), [bkT[m]])
                        memset("gpsimd", qT[m][64:65, :], 1.0, [bqT[m]])
                    memset("vector", nrm[:], 0.0, [bnrm])
                    dma("sync", lamv[:], lamv_d, (), [blv])
                    dma("sync", gsc[:], subg_d, (), [bgsc])
                    dma("sync", ptT[:], pt_d, (), [bpt])
                    dma("sync", sel128[:], sel4_d, (), [bsel])
                    dma("sync", ones128[:], ones_d, (), [bon])
                    tt("vector", t1[:, 0:64], lamv[:, 0, :], lamv[:, 1, :], ALU.mult, [blv], [bt1])
                    tt("vector", t1[:, 64:128], lamv[:, 2, :], lamv[:, 3, :], ALU.mult, [blv], [bt1])
                    red("vector", lam[:, 2:4], t1[:, 0:128].rearrange("p (a b) -> p a b", a=2), ALU.add, [bt1], [blam])
                    act(lam[:, 2:4], lam[:, 2:4], AF.Exp, [blam], [blam])
                    tt("vector", lam[:, 0:1], lam[:, 2:3], lam[:, 3:4], ALU.subtract, [blam], [blam])
                    ts("vector", lam[:, 0:1], lam[:, 0:1], LAM_INIT, None, ALU.add, None, [blam], [blam])
                    ts("vector", lam[:, 1:2], lam[:, 0:1], -1.0, None, ALU.mult, None, [blam], [blam])
                    ts("vector", gsc[:], gsc[:], 1.0 - LAM_INIT, None, ALU.mult, None, [bgsc], [bgsc])
                    cp("vector", ptf[:], ptT[:], [bpt], [bptf])
                    for cq in range(4):
                        ts("vector", ptc2[0:64, :, cq], ptf[0:64, :], 8.0, float(2 * cq), ALU.mult, ALU.add, [bptf], [bptc])
                        ts("vector", ptc2[64:128, :, cq], ptf[64:128, :], 8.0, float(2 * cq + 1), ALU.mult, ALU.add, [bptf], [bptc])
                    for i in range(NKT):
                        memset("vector", Kt[i][:], 0.0, [bKt[i]])
                    memset("vector", Bz[:], 0.0, [bBz])
                    memset("vector", knew[:], 0.0, [bknew])

                    def rope(ps, G, cos_ap, sin0_ap, sin1_ap, np_, o1, o2, bo1, bo2, rdeps):
                        pv = ps[0:np_, 0:G * 64].rearrange("p (g h f) -> p g h f", g=G, h=2)
                        o1v = o1[0:np_, 0:G * 64].rearrange("p (g h f) -> p g h f", g=G, h=2)
                        o2v = o2[0:np_, 0:G * 64].rearrange("p (g h f) -> p g h f", g=G, h=2)
                        cb = cos_ap.unsqueeze(1).unsqueeze(1).to_broadcast([np_, G, 2, 32])
                        tt("vector", o1v, pv, cb, ALU.mult, rdeps, [bo1])
                        tt("vector", o2v[:, :, 0, :], pv[:, :, 1, :], sin0_ap.unsqueeze(1).to_broadcast([np_, G, 32]), ALU.mult, rdeps, [bo2])
                        tt("vector", o2v[:, :, 1, :], pv[:, :, 0, :], sin1_ap.unsqueeze(1).to_broadcast([np_, G, 32]), ALU.mult, rdeps, [bo2])
                        tt("vector", o1[0:np_, 0:G * 64], o1[0:np_, 0:G * 64], o2[0:np_, 0:G * 64], ALU.add, [bo1, bo2], [bo1])

                    def rmsnorm_to_mix(src_ps_or_sb, np_, h, col0, ncol, rdeps, dest=None, bdest=None):
                        act(junk[0:np_, :], att[0:np_, :], AF.Square, [batt], [bjunk, bst], accum=st[0:np_, 8:9])
                        act(st[0:np_, 9:10], st[0:np_, 8:9], AF.Sqrt, [bst], [bst], bias=EPS, scale=1.0 / 128.0)
                        recip(st[0:np_, 10:11], st[0:np_, 9:10], [bst], [bst])
                        stt(attb[0:np_, :], att[0:np_, :], st[0:np_, 10:11], gsc[0:np_, :], ALU.mult, ALU.mult, [batt, bst, bgsc], [battb])
                        tr(psT[:, 0:np_], attb[0:np_, :], identb[0:np_, 0:np_], [battb, bidb], [bT])
                        if dest is None:
                            cp("scalar", mstg[:, col0:col0 + ncol], psT[:, 0:ncol], [bT], [bmstg])
                        else:
                            cp("scalar", dest, psT[:, 0:ncol], [bT], [bdest])

                    def dec_gen(h):
                        NCP = NCH // 2
                        for s in range(NS):
                            mm(psC[:, 0:128], sel128[:, s * 128:(s + 1) * 128], qs[:, h, :], True, True, [bsel, bqs], [bC])
                            cp("scalar", qbc[:, s, :], psC[:, 0:128], [bC], [bqbc])
                            dma("sync", knew[0:1, s, :], ksst[s:s + 1, h % 2, :], [bkss], [bknew])
                            dma("sync", vnew[0:1, s, :], vsst[s:s + 1, h % 2, :], [bvss], [bvnew])
                        cp("scalar", vnewb[:], vnew[:], [bvnew], [bvnewb])
                        memset("gpsimd", sc[:, :, 64, :], -1e30, [bsc2])
                        for s in range(NS):
                            for cq in range(NCP):
                                bi = kcnt[0] % NKT
                                kcnt[0] += 1
                                idma(Kt[bi][:, :], bass.AP(tensor=ck_d.tensor, offset=0, ap=[[CH * 128, npool * 8], [1, CH * 128]]), ptc2[:, s, cq:cq + 1],
                                     [bptc], [bKt[bi]], eoff=h * npool * 16384)
                                kv = Kt[bi][:, :].rearrange("p (t f) -> p t f", t=CH)
                                tt("vector", kv, kv, qbc[:, s, :].unsqueeze(1).to_broadcast([128, CH, 128]), ALU.mult, [bKt[bi], bqbc], [bKt[bi]])
                                red("vector", sc[:, s, cq * CH:(cq + 1) * CH, :].rearrange("p t m -> p (t m)"),
                                    Kt[bi][:, :].rearrange("p (a d) -> p a d", d=64), ALU.add, [bKt[bi]], [bsc2])
                                yield 1
                        tt("vector", knew[0:1, :, :], knew[0:1, :, :], qbc[0:1, :, :], ALU.mult, [bknew, bqbc], [bknew])
                        red("vector", sc[0:1, :, 64, :], knew[0:1, :, :].rearrange("p s (m d) -> p s m d", d=64), ALU.add, [bknew], [bsc2])
                        red("vector", dsm[:, 0:8].rearrange("p (s m) -> p s m", m=2), sc[:].rearrange("p s t m -> p s m t"), ALU.max, [bsc2], [bdsm])
                        P.op("tensor", lambda e: e.transpose(out=psC[0:8, 128:256], in_=dsm[:, 0:8], identity=identf[:]), [bdsm, bidf], [bC])
                        red("vector", dsm2[:, 0:1], psC[0:8, 128:256], ALU.max, [bC], [bdsm2])
                        cp("vector", dsm2[:, 8:136], dsm2[:, 0:1].to_broadcast([8, 128]), [bdsm2], [bdsm2])
                        P.op("tensor", lambda e: e.transpose(out=psC[:, 256:264], in_=dsm2[:, 8:136], identity=identf[0:8, 0:8]), [bdsm2, bidf], [bC])
                        cp("vector", dsm[:, 8:16], psC[:, 256:264], [bC], [bdsm])
                        tt("vector", sc[:], sc[:], dsm[:, 8:16].rearrange("p (s m) -> p s m", m=2).unsqueeze(2).to_broadcast([128, NS, 65, 2]), ALU.subtract,
                           [bsc2, bdsm], [bsc2])
                        act(sc[:], sc[:], AF.Exp, [bsc2], [bsc2], scale=0.125)
                        red("vector", dsm[:, 16:24].rearrange("p (s m) -> p s m", m=2), sc[:].rearrange("p s t m -> p s m t"), ALU.add, [bsc2], [bdsm])
                        mm(psC[:, 320:328], ones128[:], dsm[:, 16:24], True, True, [bon, bdsm], [bC])
                        recip(dsm[:, 24:32], psC[:, 320:328], [bC], [bdsm])
                        rzv = dsm[:, 24:32].rearrange("p (s m) -> p s m", m=2)
                        ts("vector", dsm[:, 32:36], rzv[:, :, 1], lam[:, 1:2], None, ALU.mult, None, [bdsm, blam], [bdsm])
                        for s in range(NS):
                            ts("vector", wv[:], sc[:, s, :, 0], rzv[:, s, 0:1], None, ALU.mult, None, [bsc2, bdsm], [bwv])
                            stt(Bz[:, s, :, 3], sc[:, s, :, 1], dsm[:, 32 + s:33 + s], wv[:], ALU.mult, ALU.add, [bsc2, bdsm, bwv], [bBz])
                        yield 1
                        first_pv = True
                        for s in range(NS):
                            for cq in range(NCP):
                                bi = kcnt[0] % NKT
                                kcnt[0] += 1
                                idma(Kt[bi][:, :], bass.AP(tensor=cv_d.tensor, offset=0, ap=[[CH * 128, npool * 8], [1, CH * 128]]), ptc2[:, s, cq:cq + 1],
                                     [bptc], [bKt[bi]], eoff=h * npool * 16384)
                                vi = bi % 2
                                cp("scalar", Vb[vi][:], Kt[bi][:], [bKt[bi]], [bVb[vi]])
                                for t in range(CH):
                                    mm(psB[0:NS, 0:128], Bz[:, s, cq * CH + t, 3 - s:7 - s], Vb[vi][:, t * 128:(t + 1) * 128], first_pv, False, [bBz, bVb[vi]], [bB])
                                    first_pv = False
                                yield 1
                            mm(psB[0:NS, 0:128], Bz[0:1, s, 64, 3 - s:7 - s], vnewb[0:1, s, :], False, s == NS - 1, [bBz, bvnewb], [bB])
                        cp("vector", att[0:NS, :], psB[0:NS, 0:128], [bB], [batt])
                        rmsnorm_to_mix(None, NS, h, TOWN, NS, None, dest=msts[:, h, :], bdest=bmsts)
                        yield 1

                    pending = None
                    for h in range(8):
                        qc, kc, vc = 2048 + h * 128, 3072 + h * 128, 4096 + h * 128
                        for ci, c0 in enumerate((qc, kc, vc)):
                            dma("gpsimd", wq3[:, :, ci * 128:(ci + 1) * 128], w_in_d[:, c0:c0 + 128].rearrange("(k p) c -> p k c", p=128), (), [bw3])
                        for tti in range(16):
                            own = tti >= 8
                            ps, bp = psrot[tti % 3]
                            if own:
                                for k in range(16):
                                    mm(ps[:, 0:384], xT[:, k, tti * 128:(tti + 1) * 128], wq3[:, k, 0:384], k == 0, k == 15, [bxT[k], bw3], [bp])
                                rope(ps, 4, rcos[:, tti, :], rsin[:, tti, 0, :], rsin[:, tti, 1, :], 128, t1, t2, bt1, bt2, [bp, brc, brs])
                                tt("gpsimd", t2[:, 0:256], t1[:, 0:256], t1[:, 0:256], ALU.mult, [bt1, bt2], [bt2])
                                red("vector", nrm[:, tti, 0:4], t2[:, 0:256].rearrange("p (g d) -> p g d", d=64), ALU.add, [bt2], [bnrm])
                                cp("scalar", qb16[:, tti - 8, :], t1[:, 0:128], [bt1], [bqb])
                                cp("scalar", kb16[:, tti, :], t1[:, 128:256], [bt1], [bkb])
                                cp("gpsimd", kst[:, tti - 8, :], t1[:, 128:256], [bt1], [bkst])
                                cp("scalar", vb[:, tti, :], ps[:, 256:384], [bp], [bvb])
                                cp("scalar", vst[:, tti - 8, :], ps[:, 256:384], [bp], [bvst])
                            else:
                                for k in range(16):
                                    mm(ps[:, 0:256], xT[:, k, tti * 128:(tti + 1) * 128], wq3[:, k, 128:384], k == 0, k == 15, [bxT[k], bw3], [bp])
                                rope(ps, 2, rcos[:, tti, :], rsin[:, tti, 0, :], rsin[:, tti, 1, :], 128, t1, t2, bt1, bt2, [bp, brc, brs])
                                tt("gpsimd", t2[:, 0:128], t1[:, 0:128], t1[:, 0:128], ALU.mult, [bt1, bt2], [bt2])
                                red("vector", nrm[:, tti, 2:4], t2[:, 0:128].rearrange("p (g d) -> p g d", d=64), ALU.add, [bt2], [bnrm])
                                cp("scalar", kb16[:, tti, :], t1[:, 0:128], [bt1], [bkb])
                                cp("scalar", vb[:, tti, :], ps[:, 128:256], [bp], [bvb])
                        dma("sync", k_o[:, h * 128:(h + 1) * 128].rearrange("(n p) f -> p n f", p=128), kst[:], [bkst], (), final=True)
                        dma("sync", v_o[:, h * 128:(h + 1) * 128].rearrange("(n p) f -> p n f", p=128), vst[:], [bvst], (), final=True)
                        for m in range(2):
                            for g in range(2):
                                for j in range(8):
                                    tr(psT[0:64, j * 128:(j + 1) * 128], kb16[:, g * 8 + j, m * 64:(m + 1) * 64], identb[:], [bkb, bidb], [bT, bTh1])
                                cp("vector" if g == 0 else "scalar", kT[m][0:64, g * 1024:(g + 1) * 1024], psT[0:64, :], [bT, bTh1], [bkT[m]])
                            for j in range(8):
                                tr(psT[0:64, j * 128:(j + 1) * 128], qb16[:, j, m * 64:(m + 1) * 64], identb[:], [bqb, bidb], [bT, bTh1])
                            cp("vector", qT[m][0:64, :], psT[0:64, :], [bT, bTh1], [bqT[m]])
                        red("vector", kmx[:, 0:2], nrm[:, :, 2:4].rearrange("p t m -> p m t"), ALU.max, [bnrm], [bkmx])
                        P.op("tensor", lambda e: e.transpose(out=psC[0:2, 0:128], in_=kmx[:, 0:2], identity=identf[:]), [bkmx, bidf], [bC])
                        red("vector", kmx2[:, 0:1], psC[0:2, 0:128], ALU.max, [bC], [bkmx2])
                        cp("vector", kmx2[:, 4:132], kmx2[:, 0:1].to_broadcast([2, 128]), [bkmx2], [bkmx2])
                        P.op("tensor", lambda e: e.transpose(out=psC[:, 128:130], in_=kmx2[:, 4:132], identity=identf[0:2, 0:2]), [bkmx2, bidf], [bC])
                        cp("vector", kmx[:, 2:4], psC[:, 128:130], [bC], [bkmx])
                        tt("vector", nbq[:], nrm[:, 8:16, 0:2], kmx[:, 2:4].unsqueeze(1).to_broadcast([128, 8, 2]), ALU.mult, [bnrm, bkmx], [bnbq])
                        act(nbq[:], nbq[:], AF.Sqrt, [bnbq], [bnbq])
                        ts("vector", nbq[:], nbq[:], -0.125, None, ALU.mult, None, [bnbq], [bnbq])
                        for i in range(8):
                            nk = 1024 + (i + 1) * 128
                            pieces = [(c0, min(512, nk - c0)) for c0 in range(0, nk, 512)]
                            npc = len(pieces)
                            for m in range(2):
                                for pi, (c0, w_) in enumerate(pieces):
                                    sb = acnt[0] % 4
                                    acnt[0] += 1
                                    lastp = (pi == npc - 1)
                                    S = psS[:, sb * 512:sb * 512 + w_]
                                    mm(S, qT[m][:, i * 128:(i + 1) * 128], kT[m][:, c0:c0 + w_], True, not lastp, [bqT[m], bkT[m]], [bSp[sb]])
                                    if lastp:
                                        mm(psS[:, sb * 512 + w_ - 128:sb * 512 + w_], identb[:], cmaskb[:], False, True, [bidb, bcmb], [bSp[sb]])
                                    pb_ = acnt[1] % 3
                                    acnt[1] += 1
                                    act(Pp[pb_][:, 0:w_], S, AF.Exp, [bSp[sb], bnbq], [bPp[pb_], bzc], bias=nbq[:, i, m:m + 1], scale=0.125, accum=zc[:, m, pi:pi + 1])
                                    nblk = w_ // 128
                                    tb = acnt[2] % 2
                                    acnt[2] += 1
                                    btb = bT
                                    for j in range(nblk):
                                        tr(psT[:, tb * 512 + j * 128:tb * 512 + (j + 1) * 128], Pp[pb_][:, j * 128:(j + 1) * 128], identb[:], [bPp[pb_], bidb], [btb])
                                    cp("scalar" if (acnt[2] % 3 == 0) else "vector", PTp[pb_][:, 0:nblk, :],
                                       psT[:, tb * 512:tb * 512 + nblk * 128].rearrange("p (a b) -> p a b", a=nblk), [btb], [bPTp[pb_]])
                                    for j in range(nblk):
                                        kb = c0 // 128 + j
                                        mm(psA[:, m * 128:(m + 1) * 128], PTp[pb_][:, j, :], vb[:, kb, :], (pi == 0 and j == 0), (lastp and j == nblk - 1),
                                           [bPTp[pb_], bvb], [bA])
                            red("vector", st[:, 5:7], zc[:, :, 0:npc], ALU.add, [bzc], [bst])
                            recip(st[:, 11:13], st[:, 5:7], [bst], [bst])
                            ts("vector", att2[:], psA[:, 0:128], st[:, 11:12], None, ALU.mult, None, [bA, bst], [batt2])
                            tt("vector", st[:, 13:14], st[:, 12:13], lam[:, 1:2], ALU.mult, [bst, blam], [bst])
                            stt(att[:], psA[:, 128:256], st[:, 13:14], att2[:], ALU.mult, ALU.add, [bA, bst, batt2], [batt])
                            rmsnorm_to_mix(None, 128, h, i * 128, 128, None)
                            if pending is not None:
                                for _ in range(10):
                                    if next(pending, "done") == "done":
                                        pending = None
                                        break

                        ps, bp = psrot[2]
                        for k in range(16):
                            mm(ps[0:NS, 0:384], xsT[:, k, :], wq3[:, k, 0:384], k == 0, k == 15, [bxs, bw3], [bp])
                        rope(ps, 4, rsam[:, 0:32], rsam[:, 32:64], rsam[:, 64:96], NS, ts1, ts2, bts1, bts2, [bp, brsm])
                        cp("vector", qs[:, h, :], ts1[:, 0:128], [bts1], [bqs])
                        cp("vector", ksst[:, h % 2, :], ts1[:, 128:256], [bts1], [bkss])
                        cp("vector", vsst[:, h % 2, :], ps[0:NS, 256:384], [bp], [bvss])
                        dma("sync", ks_o[:, h * 128:(h + 1) * 128], ksst[:, h % 2, :], [bkss], (), final=True)
                        dma("sync", vs_o[:, h * 128:(h + 1) * 128], vsst[:, h % 2, :], [bvss], (), final=True)
                        if do_dec:
                            for _ in dec_gen(h):
                                pass
                        dma("sync", mix_d[8 + h][:, 0:TOWN], mstg[:, 0:TOWN], [bmstg], [bmix[8 + h]])
                    if pending is not None:
                        for _ in pending:
                            pass
                    if not do_dec:
                        memset("vector", msts[:], 0.0, [bmsts])
                    with nc.allow_non_contiguous_dma(reason="tiny sample columns"):
                        for hh in range(8):
                            dma("sync", mix_d[8 + hh][:, TOWN:TTOK], msts[:, hh, :], [bmsts], [bmix[8 + hh]])
                    conv_steps(100000)
                    P.barrier()
                    P.flush()
        if dbg:
            esd = contextlib.ExitStack()
            with esd:
                mixb = T(esd, "mixb", [128, TTOK], BF16); bmb = Buf("mixb")
                mixf = T(esd, "mixf", [128, TTOK], F32); bmf = Buf("mixf")
                for kk in range(16):
                    dma("sync", mixb[:], mix_d[kk], [bmix[kk]], [bmb])
                    cp("vector", mixf[:], mixb[:], [bmb], [bmf])
                    dma("sync", mix_o[:, kk, :], mixf[:], [bmf], (), final=True)
                P.barrier()
                P.flush()

        NT = 9
        bx1d = [Buf(f"x1d{t}") for t in range(NT)]
        if do_p2:
            es2 = contextlib.ExitStack()
            with es2:
                lnp = T(es2, "lnp", [128, 2, D], F32); blnp = Buf("lnp")
                xt = T(es2, "xt", [128, D], F32); bxt = Buf("xt")
                st2 = T(es2, "st2", [128, 16], F32); bst2 = Buf("st2")
                E = T(es2, "E", [128, NT, 128], I32); bE = [Buf(f"E{t}") for t in range(NT)]
                G = T(es2, "G", [128, NT, 128], F32); bG = [Buf(f"G{t}") for t in range(NT)]

                def layer_norm(buf_ap, bbuf, np_):
                    red("vector", st2[0:np_, 0:1], buf_ap, ALU.add, [bbuf], [bst2])
                    ts("vector", st2[0:np_, 1:2], st2[0:np_, 0:1], -1.0 / D, None, ALU.mult, None, [bst2], [bst2])
                    ts("vector", buf_ap, buf_ap, st2[0:np_, 1:2], None, ALU.add, None, [bbuf, bst2], [bbuf])
                    act(xt[0:np_, :], buf_ap, AF.Square, [bbuf], [bxt, bst2], accum=st2[0:np_, 2:3])
                    act(st2[0:np_, 3:4], st2[0:np_, 2:3], AF.Sqrt, [bst2], [bst2], bias=EPS, scale=1.0 / D)
                    recip(st2[0:np_, 4:5], st2[0:np_, 3:4], [bst2], [bst2])
                    stt(buf_ap, buf_ap, st2[0:np_, 4:5], lnp[0:np_, 0, :], ALU.mult, ALU.mult, [bbuf, bst2, blnp], [bbuf])
                    tt("gpsimd", buf_ap, buf_ap, lnp[0:np_, 1, :], ALU.add, [bbuf, blnp], [bbuf])

                esA = contextlib.ExitStack()
                with esA:
                    wbuf = T(esA, "wbufA", [128, 16, D], BF16); bwb = Buf("wbufA")
                    mixt = [T(esA, f"mixt{i}", [128, 16, 128], BF16) for i in range(2)]; bmt = [Buf("mixt0"), Buf("mixt1")]
                    x1t = [T(esA, f"x1tA{i}", [128, D], F32) for i in range(2)]; bx1t = [Buf("x1tA0"), Buf("x1tA1")]
                    xin = [T(esA, f"xin{i}", [128, D], F32) for i in range(2)]; bxin = [Buf("xin0"), Buf("xin1")]
                    for k in range(16):
                        dma("gpsimd", wbuf[:, k, :], w_out_d[k * 128:(k + 1) * 128, :], (), [bwb])
                    dma("sync", lnp[:], ln_d[:, 0:2, :], (), [blnp])
                    for t in range(NT):
                        np_ = 128 if t < 8 else NS
                        bi = t % 2
                        dma("sync", xin[bi][0:np_, :], xtok_d[t * 128:t * 128 + np_, :], (), [bxin[bi]])
                        dma("sync", mixt[bi][:, :, 0:np_], mix_d[:, :, t * 128:t * 128 + np_].rearrange("k p t -> p k t"), bmix, [bmt[bi]])
                        for nb in range(4):
                            ps, bp = psrot[nb % 3]
                            for kk in range(16):
                                mm(ps[0:np_, 0:512], mixt[bi][:, kk, 0:np_], wbuf[:, kk, nb * 512:(nb + 1) * 512], kk == 0, kk == 15,
                                   [bmt[bi], bwb], [bp])
                            stt(x1t[bi][0:np_, nb * 512:(nb + 1) * 512], xin[bi][0:np_, nb * 512:(nb + 1) * 512], ALPHA, ps[0:np_, 0:512], ALU.mult, ALU.add,
                                [bxin[bi], bp], [bx1t[bi]])
                        layer_norm(x1t[bi][0:np_, :], bx1t[bi], np_)
                        dma("sync", x1_d[t * 128:t * 128 + np_, :], x1t[bi][0:np_, :], [bx1t[bi]], [bx1d[t]])
                        if dbg:
                            dma("sync", x1_o[t * 128:t * 128 + np_, :], x1t[bi][0:np_, :], [bx1t[bi]], (), final=True)
                    P.barrier()
                    P.flush()

                if do_peer:
                    esB = contextlib.ExitStack()
                    with esB:
                        wbuf = T(esB, "wbufB", [128, 16, D], BF16); bwb = Buf("wbufB")
                        k1T = T(esB, "k1T", [128, 8, 128], F32); bk1 = Buf("k1T")
                        k2T = T(esB, "k2T", [128, 8, 128], F32); bk2 = Buf("k2T")
                        iota16 = T(esB, "iota16", [128, 16, 16], F32); bio = Buf("iota")
                        x1f = T(esB, "x1f", [128, D], F32); bx1f = Buf("x1f")
                        x1T = T(esB, "x1T", [128, 16, 128], BF16); bx1T = Buf("x1T")
                        x1b = T(esB, "x1b", [128, D], BF16); bx1b = Buf("x1b")
                        qTf = T(esB, "qTf", [128, 16, 128], F32); bqTf = Buf("qTf")
                        S12 = T(esB, "S12", [128, 16, 128], F32); bS12 = Buf("S12")
                        wk = T(esB, "wk", [128, 256], F32); bwk = Buf("wk")
                        V12 = T(esB, "V12", [128, 16, 16], F32); bV12 = Buf("V12")
                        I12 = T(esB, "I12", [128, 16, 16], U32); bI12 = Buf("I12")
                        I12f = T(esB, "I12f", [128, 16, 16], F32); bI12f = Buf("I12f")
                        comb = T(esB, "comb", [128, 8, 256], F32); bcomb = Buf("comb")
                        sv = T(esB, "sv", [128, 8, 16], F32); bsv = Buf("sv")
                        svx = T(esB, "svx", [128, 8, 16], F32); bsvx = Buf("svx")
                        si = T(esB, "si", [128, 8, 16], U32); bsi = Buf("si")
                        sif = T(esB, "sif", [128, 8, 16], F32); bsif = Buf("sif")
                        sab = T(esB, "sab", [128, 2, 8, 16], F32); bsab = Buf("sab")
                        oh = T(esB, "oh", [128, 8, 16, 16], F32); boh = Buf("oh")
                        isel = T(esB, "isel", [128, 2, 8, 16], F32); bisel = Buf("isel")
                        ef = T(esB, "ef", [128, 128], F32); bef = Buf("ef")

                        for k in range(16):
                            dma("gpsimd", wbuf[:, k, :], wq_d[k * 128:(k + 1) * 128, :], (), [bwb])
                        dma("sync", k1T[:], k1T_d, (), [bk1])
                        dma("sync", k2T[:], k2T_d, (), [bk2])
                        dma("sync", iota16[:], iota_d.rearrange("p (a b) -> p a b", a=16), (), [bio])
                        memset("vector", x1f[:], 0.0, [bx1f])

                        def bc4(ap3, h0):
                            return ap3.unsqueeze(1).to_broadcast([128, 4, 16, 16])

                        for t in range(NT):
                            np_ = 128 if t < 8 else NS
                            dma("sync", x1f[0:np_, :], x1_d[t * 128:t * 128 + np_, :], [bx1d[t]], [bx1f])
                            cp("scalar", x1b[:], x1f[:], [bx1f], [bx1b])
                            for g in range(2):
                                for j in range(8):
                                    tr(psT[:, j * 128:(j + 1) * 128], x1b[:, (g * 8 + j) * 128:(g * 8 + j + 1) * 128], identb[:], [bx1b, bidb], [bT])
                                cp("vector", x1T[:, g * 8:(g + 1) * 8, :], psT[:, :].rearrange("p (a b) -> p a b", a=8), [bT], [bx1T])
                            for c in range(16):
                                ps, bp = psrot[c % 3]
                                for kk in range(16):
                                    mm(ps[:, 0:128], wbuf[:, kk, c * 128:(c + 1) * 128], x1T[:, kk, :], kk == 0, kk == 15, [bwb, bx1T], [bp])
                                cp("scalar" if c % 2 == 0 else "vector", qTf[:, c, :], ps[:, 0:128], [bp], [bqTf])
                            for c in range(16):
                                hh, half = c // 2, c % 2
                                ps, bp = psrot[c % 3]
                                kk_ap = k1T[:, hh, :] if half == 0 else k2T[:, hh, :]
                                mm(ps[:, 0:128], qTf[:, c, :], kk_ap, True, True, [bqTf, bk1, bk2], [bp])
                                cp("scalar" if c % 2 == 0 else "vector", S12[:, c, :], ps[:, 0:128], [bp], [bS12])
                            for c in range(16):
                                P.op("vector", lambda e, c=c: e.max(out=V12[:, c, 0:8], in_=S12[:, c, :]), [bS12], [bV12])
                                P.op("vector", lambda e, c=c: e.max_index(out=I12[:, c, 0:8], in_max=V12[:, c, 0:8], in_values=S12[:, c, :]), [bS12, bV12], [bI12])
                                P.op("vector", lambda e, c=c: e.match_replace(out=wk[:, 0:128], in_to_replace=V12[:, c, 0:8], in_values=S12[:, c, :], imm_value=-1e30),
                                     [bS12, bV12], [bwk])
                                P.op("vector", lambda e, c=c: e.max(out=V12[:, c, 8:16], in_=wk[:, 0:128]), [bwk], [bV12])
                                P.op("vector", lambda e, c=c: e.max_index(out=I12[:, c, 8:16], in_max=V12[:, c, 8:16], in_values=wk[:, 0:128]), [bwk, bV12], [bI12])
                            cp("vector", I12f[:], I12[:], [bI12], [bI12f])
                            V4 = V12[:].rearrange("p (h two) k -> p h two k", two=2)
                            I4 = I12f[:].rearrange("p (h two) k -> p h two k", two=2)
                            cv4 = comb[:].rearrange("p h (a b) -> p h a b", a=16)
                            for hq in (0, 4):
                                tt("vector", cv4[:, hq:hq + 4], V4[:, hq:hq + 4, 0, :].unsqueeze(3).to_broadcast([128, 4, 16, 16]),
                                   V4[:, hq:hq + 4, 1, :].unsqueeze(2).to_broadcast([128, 4, 16, 16]), ALU.add, [bV12], [bcomb])
                            for hh in range(8):
                                P.op("vector", lambda e, hh=hh: e.max(out=sv[:, hh, 0:8], in_=comb[:, hh, :]), [bcomb], [bsv])
                                P.op("vector", lambda e, hh=hh: e.max_index(out=si[:, hh, 0:8], in_max=sv[:, hh, 0:8], in_values=comb[:, hh, :]), [bcomb, bsv], [bsi])
                                P.op("vector", lambda e, hh=hh: e.match_replace(out=wk[:], in_to_replace=sv[:, hh, 0:8], in_values=comb[:, hh, :], imm_value=-1e30),
                                     [bcomb, bsv], [bwk])
                                P.op("vector", lambda e, hh=hh: e.max(out=sv[:, hh, 8:16], in_=wk[:]), [bwk], [bsv])
                                P.op("vector", lambda e, hh=hh: e.max_index(out=si[:, hh, 8:16], in_max=sv[:, hh, 8:16], in_values=wk[:]), [bwk, bsv], [bsi])
                            cp("vector", sif[:], si[:], [bsi], [bsif])
                            ts("vector", svx[:], sif[:], 0.0625, -1.0, ALU.mult, ALU.add, [bsif], [bsvx])
                            for hq in (0, 4):
                                tt("vector", oh[:, hq:hq + 4], svx[:, hq:hq + 4, :].unsqueeze(3).to_broadcast([128, 4, 16, 16]), bc4(iota16[:], hq), ALU.is_ge,
                                   [bsvx, bio], [boh])
                            red("vector", sab[:, 0].rearrange("p h j -> p (h j)"), oh[:].rearrange("p h j a -> p (h j) a"), ALU.add, [boh], [bsab])
                            stt(sab[:, 1].rearrange("p h j -> p (h j)"), sab[:, 0].rearrange("p h j -> p (h j)"), -16.0, sif[:].rearrange("p h j -> p (h j)"),
                                ALU.mult, ALU.add, [bsab, bsif], [bsab])
                            for side in range(2):
                                for hq in (0, 4):
                                    tt("vector", oh[:, hq:hq + 4], bc4(iota16[:], hq), sab[:, side, hq:hq + 4, :].unsqueeze(3).to_broadcast([128, 4, 16, 16]),
                                       ALU.is_equal, [bio, bsab], [boh])
                                    tt("vector", oh[:, hq:hq + 4], oh[:, hq:hq + 4], I4[:, hq:hq + 4, side, :].unsqueeze(2).to_broadcast([128, 4, 16, 16]),
                                       ALU.mult, [boh, bI12f], [boh])
                                red("vector", isel[:, side].rearrange("p h j -> p (h j)"), oh[:].rearrange("p h j a -> p (h j) a"), ALU.add, [boh], [bisel])
                            stt(ef[:], isel[:, 0].rearrange("p h j -> p (h j)"), 128.0, isel[:, 1].rearrange("p h j -> p (h j)"), ALU.mult, ALU.add, [bisel], [bef])
                            cp("vector", E[:, t, :], ef[:], [bef], [bE[t]])
                            tt("vector", svx[:], sv[:], sv[:, :, 0:1].to_broadcast([128, 8, 16]), ALU.subtract, [bsv, bsvx], [bsvx])
                            act(svx[:], svx[:], AF.Exp, [bsvx], [bsvx])
                            red("vector", st2[:, 8:16], svx[:], ALU.add, [bsvx], [bst2])
                            recip(st2[:, 8:16], st2[:, 8:16], [bst2], [bst2])
                            tt("vector", G[:, t, :].rearrange("p (h j) -> p h j", h=8), svx[:], st2[:, 8:16].unsqueeze(2).to_broadcast([128, 8, 16]), ALU.mult,
                               [bsvx, bst2], [bG[t]])
                        P.barrier()
                        P.flush()

                    esC = contextlib.ExitStack()
                    with esC:
                        NG = 20
                        gsl = [T(esC, f"gsl{i}", [128, D], BF16) for i in range(NG)]; bg = [Buf(f"g{i}") for i in range(NG)]
                        x1c = [T(esC, f"x1c{i}", [128, D], F32) for i in range(2)]; bx1c = [Buf("x1c0"), Buf("x1c1")]
                        accs = [T(esC, f"acc{i}", [128, D], F32) for i in range(2)]; bacc = [Buf("acc0"), Buf("acc1")]
                        junk = T(esC, "junkC", [128, D], BF16); bjunk = Buf("junkC")
                        dg = [T(esC, f"dg{i}", [128, 128], BF16) for i in range(4)]; bdg = [Buf(f"dg{i}") for i in range(4)]
                        actv = T(esC, "actv", [128, 128], F32); bactv = Buf("actv")
                        coef = T(esC, "coef", [128, 128], F32); bcoef = Buf("coef")
                        dma("sync", lnp[:], ln_d[:, 2:4, :], (), [blnp])
                        gi = 0
                        di = 0
                        for t in range(NT):
                            np_ = 128 if t < 8 else NS
                            bi = t % 2
                            dma("sync", x1c[bi][0:np_, :], x1_d[t * 128:t * 128 + np_, :], [bx1d[t]], [bx1c[bi]])
                            for s in range(128):
                                g_ap = gsl[gi][0:np_, :]
                                idma(g_ap, pub_d, E[0:np_, t, s:s + 1], [bE[t]], [bg[gi]])
                                stt(junk[0:np_, :], g_ap, 1.0, x1c[bi][0:np_, :], ALU.mult, ALU.mult, [bg[gi], bx1c[bi]], [bjunk, bactv], accum=actv[0:np_, s:s + 1])
                                gi = (gi + 1) % NG
                            act(coef[0:np_, :], actv[0:np_, :], AF.Gelu, [bactv], [bcoef])
                            tt("vector", coef[0:np_, :], coef[0:np_, :], G[0:np_, t, :], ALU.mult, [bcoef, bG[t]], [bcoef])
                            for s in range(128):
                                g_ap = gsl[gi][0:np_, :]
                                idma(g_ap, pvb_d, E[0:np_, t, s:s + 1], [bE[t]], [bg[gi]])
                                act(dg[di][0:np_, 0:np_], identb[0:np_, 0:np_], AF.Copy, [bidb, bcoef], [bdg[di]], scale=coef[0:np_, s:s + 1])
                                for nb in range(4):
                                    mm(psS[0:np_, nb * 512:(nb + 1) * 512], dg[di][0:np_, 0:np_], gsl[gi][0:np_, nb * 512:(nb + 1) * 512], s == 0, s == 127,
                                       [bdg[di], bg[gi]], [bS])
                                gi = (gi + 1) % NG
                                di = (di + 1) % 4
                            stt(accs[bi][0:np_, :], x1c[bi][0:np_, :], ALPHA, psS[0:np_, :], ALU.mult, ALU.add, [bx1c[bi], bS], [bacc[bi]])
                            layer_norm(accs[bi][0:np_, :], bacc[bi], np_)
                            dma("sync", y_o[t * 128:t * 128 + np_, :], accs[bi][0:np_, :], [bacc[bi]], (), final=True)
                        P.flush(last=True)
                else:
                    esC = contextlib.ExitStack()
                    with esC:
                        x1c = T(esC, "x1c", [128, D], F32); bx1c = Buf("x1c")
                        for t in range(NT):
                            np_ = 128 if t < 8 else NS
                            dma("sync", x1c[0:np_, :], x1_d[t * 128:t * 128 + np_, :], [bx1d[t]], [bx1c])
                            dma("sync", y_o[t * 128:t * 128 + np_, :], x1c[0:np_, :], [bx1c], (), final=True)
                        P.flush(last=True)
        else:
            P.flush(last=True)
    return nc


def _rope_tables(pos):
    half = 32
    inv = (10000.0 ** (-np.arange(half, dtype=np.float32) * 2.0 / 64.0)).astype(np.float32)
    ang = pos.astype(np.float32)[:, None] * inv[None, :]
    return np.cos(ang).astype(np.float32), np.sin(ang).astype(np.float32)


_CACHE = {}


def kernel(x_prompt, x_sample, cache_k, cache_v, state_conv, state_h, page_table,
           w_in, conv_w, conv_b, lru_wa, lru_ba, lru_wx, lru_bx, lru_lambda,
           lambda_q1, lambda_k1, lambda_q2, lambda_k2, subln_g, w_out, ln1_g, ln1_b,
           peer_wq, peer_k1, peer_k2, peer_u, peer_v, ln2_g, ln2_b, _flags=None, _trace=False):
    f = np.float32
    A = lambda a: np.ascontiguousarray(np.asarray(a))
    flags = _flags or {}
    key = tuple(sorted(flags.items()))
    if key not in _CACHE:
        _CACHE[key] = build_program(**flags)
    nc = _CACHE[key]

    x_prompt = A(x_prompt); x_sample = A(x_sample)
    npool = flags.get("npool", NPOOL); nexp = flags.get("nexp", 16384)
    ck = A(np.asarray(cache_k)[0][:npool].reshape(npool, 128, 8, 128).transpose(2, 0, 1, 3).reshape(8, npool, 16384))
    cv = A(np.asarray(cache_v)[0][:npool].transpose(2, 0, 1, 3).reshape(8, npool, 16384))
    w_in0 = A(np.asarray(w_in)[0]); w_out0 = A(np.asarray(w_out)[0]); wq0 = A(np.asarray(peer_wq)[0])
    chan = np.stack([np.asarray(conv_w)[0][0], np.asarray(conv_w)[0][1], np.asarray(conv_w)[0][2], np.asarray(conv_w)[0][3],
                     np.asarray(conv_b)[0], np.asarray(lru_ba)[0], np.asarray(lru_bx)[0], np.asarray(lru_lambda)[0]], axis=-1)
    chan = A(chan.reshape(8, 128, 8).transpose(1, 0, 2))
    lwa = A(np.asarray(lru_wa)[0].transpose(1, 0, 2))
    lwx = A(np.asarray(lru_wx)[0].transpose(1, 0, 2))
    lamv = A(np.broadcast_to(np.stack([np.asarray(lambda_q1)[0], np.asarray(lambda_k1)[0], np.asarray(lambda_q2)[0], np.asarray(lambda_k2)[0]])[None], (128, 4, 64)))
    subg = A(np.broadcast_to(np.asarray(subln_g)[0][None], (128, 128)))
    ln = A(np.broadcast_to(np.stack([np.asarray(ln1_g)[0], np.asarray(ln1_b)[0], np.asarray(ln2_g)[0], np.asarray(ln2_b)[0]])[None], (128, 4, D)))
    k1T = A(np.asarray(peer_k1)[0].transpose(2, 0, 1))
    k2T = A(np.asarray(peer_k2)[0].transpose(2, 0, 1))
    pu = A(np.asarray(peer_u)[0][:nexp]); pv = A(np.asarray(peer_v)[0][:nexp])
    ident = np.eye(128, dtype=f)
    cmask = np.where(np.arange(128)[None, :] <= np.arange(128)[:, None], 0.0, 8.0 * NEG).astype(f)
    sel4 = np.zeros((NS, NS, 128), f)
    for s in range(NS):
        sel4[s, s, :] = 1.0
    sel4 = sel4.reshape(NS, NS * 128)
    ones65 = np.ones((128, 128), f)
    iota16 = A(np.broadcast_to(np.arange(16, dtype=f)[None, None, :], (128, 16, 16)).reshape(128, 256))
    cs, sn = _rope_tables(np.array([8192]))
    rsam = A(np.broadcast_to(np.concatenate([cs[0], -sn[0], sn[0]])[None], (NS, 96)))
    sc0 = np.asarray(state_conv)[0]
    sh0 = np.asarray(state_h)[0]
    pt = np.asarray(page_table)

    in_maps = []
    for j in range(NCORES):
        b, hf = j // 2, j % 2
        xs = x_prompt[b]
        xT = np.zeros((D, NSLOT), f)
        if hf == 1:
            xT[:, :] = xs.T
        else:
            xT[:, TOWN:] = xs[0:TOWN].T
        own = xs[hf * TOWN:(hf + 1) * TOWN]
        smp = x_sample[NS * j:NS * (j + 1), 0, :]
        pos = np.concatenate([np.arange(TOWN), hf * TOWN + np.arange(TOWN)])
        c_, s_ = _rope_tables(pos)
        rcos = A(c_.reshape(16, 128, 32).transpose(1, 0, 2))
        rsin = A(np.stack([-s_, s_], axis=1).reshape(16, 128, 2, 32).transpose(1, 0, 2, 3))
        pbrow = np.zeros((1, NSLOT), f)
        if hf == 0:
            pbrow[0, 0:TOWN] = 8.0 * NEG
        pbv = np.zeros((128, 2), f)
        pbv[:, 0] = 0.0 if hf == 1 else NEG
        pbv[:, 1] = 1.0 if hf == 1 else 0.0
        scj = sc0[NS * j:NS * (j + 1)]
        sconv = A(scj.reshape(NS, 3, 8, 128).transpose(3, 2, 1, 0))
        shj = A(sh0[NS * j:NS * (j + 1)].reshape(NS, 8, 128).transpose(2, 1, 0))
        in_maps.append({
            "xT": xT, "xsT": A(smp.T), "xtok": A(np.concatenate([own, smp], 0)),
            "w_in": w_in0, "w_out": w_out0, "wq": wq0, "chan": chan, "lwa": lwa, "lwx": lwx, "lamv": lamv, "subg": subg,
            "rcos": rcos, "rsin": rsin, "rsam": rsam, "pb": pbv, "pbrow": pbrow, "ident": ident, "cmask": cmask, "sel128": sel4, "ones128": ones65,
            "iota16": iota16, "cache_k": ck, "cache_v": cv, "ptT": A(np.concatenate([pt[NS * j:NS * (j + 1)].T, pt[NS * j:NS * (j + 1)].T], 0).astype(np.int32)),
            "sconv": sconv, "sh": shj, "ln": ln, "k1T": k1T, "k2T": k2T, "peer_u": pu, "peer_v": pv,
        })
    res = run_bass_kernel_spmd(nc, in_maps, core_ids=list(range(NCORES)), **({"trace": True} if _trace else {}))
    R = res.results
    yp = np.zeros((4, 2048, D), f); ys = np.zeros((32, 1, D), f)
    kp = np.zeros((1, 4, 2048, 16, 64), f); vp = np.zeros((1, 4, 2048, 8, 128), f)
    cp_ = np.zeros((1, 4, 3, 1024), f); hp = np.zeros((1, 4, 1024), f)
    ksn = np.zeros((1, 32, 1, 16, 64), f); vsn = np.zeros((1, 32, 1, 8, 128), f)
    csn = np.zeros((1, 32, 3, 1024), f); hsn = np.zeros((1, 32, 1024), f)
    for j in range(NCORES):
        b, hf = j // 2, j % 2
        r = R[j]
        yp[b, hf * TOWN:(hf + 1) * TOWN] = r["y"][0:TOWN]
        ys[NS * j:NS * (j + 1), 0] = r["y"][TOWN:TTOK]
        kp[0, b, hf * TOWN:(hf + 1) * TOWN] = r["k_o"].reshape(TOWN, 16, 64)
        vp[0, b, hf * TOWN:(hf + 1) * TOWN] = r["v_o"].reshape(TOWN, 8, 128)
        if hf == 1:
            cp_[0, b] = r["conv_o"].transpose(2, 1, 0).reshape(3, 1024)
            hp[0, b] = r["h_o"].transpose(1, 0).reshape(1024)
        ksn[0, NS * j:NS * (j + 1), 0] = r["ks_o"].reshape(NS, 16, 64)
        vsn[0, NS * j:NS * (j + 1), 0] = r["vs_o"].reshape(NS, 8, 128)
        csn[0, NS * j:NS * (j + 1)] = r["convs_o"].transpose(3, 2, 1, 0).reshape(NS, 3, 1024)
        hsn[0, NS * j:NS * (j + 1)] = r["hs_o"].transpose(2, 1, 0).reshape(NS, 1024)
    if flags.get("dbg"):
        kernel._dbg = [(R[j]["mix_o"], R[j]["x1_o"]) for j in range(NCORES)]
    if _trace:
        kernel._exec_ns = res.exec_time_ns
    return (yp, ys, kp, vp, cp_, hp, ksn, vsn, csn, hsn)
```
